# Optimizing a Trainium2 kernel written in Bass

```python
import math
import jax, jax.numpy as jnp
from jax import lax
import numpy as np

D_MODEL = 1024
BATCH = 16
SEQ = 256
DEPTH = 4
DEC_BATCH = 4
DEC_SEQ = 2048
PAST_LEN = 256

GRID_W = 64
N_DIRS = 2
D_A = D_MODEL // 4
RW_HEAD = 64
H_A = D_A // RW_HEAD
LORA_W = 64
LORA_A = 64
LORA_G = 128
D_B = D_MODEL // 2
DIFF_DH = 64
H_B = D_B // (2 * DIFF_DH)
D_C = D_MODEL // 4
HY_ORDER = 2
HY_DIRS = 2
HY_BANDS = 16
HY_EMB = 1 + 2 * HY_BANDS
HY_FFN = 64
HY_TARGET = 1e-2
HY_SHORT_PCT = 0.3
HY_LONG_PCT = 1.5
D_FF = 4 * D_MODEL
ROPE_BASE = 10000.0
Q_BLOCK = 128
RMS_EPS = 1e-6
GN_EPS = 64e-5
COL_SIZES = (3 * D_A, 2 * LORA_W, 2 * LORA_A, LORA_G, D_B, D_B, D_B, 3 * D_C)
IN_COLS = sum(COL_SIZES)

kernel_name = 'hybrid_rwkv7_diffattn_hyena_dit_step'


def _rmsnorm(x, g):
    xf = x.astype(jnp.float32)
    y = xf * lax.rsqrt(jnp.mean(xf * xf, axis=-1, keepdims=True) + RMS_EPS)
    return (y * g.astype(jnp.float32)).astype(x.dtype)


def _split_cols(u):
    idx = []
    s = 0
    for n in COL_SIZES[:-1]:
        s += n
        idx.append(s)
    return jnp.split(u, idx, axis=-1)


def _short_conv(u, w):
    up = jnp.pad(u, ((0, 0), (1, 1), (0, 0)))
    return w[0] * up[:, :-2] + w[1] * up[:, 1:-1] + w[2] * up[:, 2:]


def _axial_rope_tables(L):
    rows = L // GRID_W
    row = jnp.repeat(jnp.arange(rows), GRID_W).astype(jnp.float32)
    col = jnp.tile(jnp.arange(GRID_W), rows).astype(jnp.float32)
    half = DIFF_DH // 2
    inv = ROPE_BASE ** (-jnp.arange(0, half, 2, dtype=jnp.float32) / half)
    ang_r = row[:, None] * inv[None]
    ang_c = col[:, None] * inv[None]
    ang = jnp.concatenate([ang_r, ang_r, ang_c, ang_c], axis=-1)
    return jnp.cos(ang), jnp.sin(ang)


def _apply_rope(x, cos, sin):
    x1, x2, x3, x4 = jnp.split(x, 4, axis=-1)
    rot = jnp.concatenate([-x2, x1, -x4, x3], axis=-1)
    cb = cos[None, :, None, None, :]
    sb = sin[None, :, None, None, :]
    return (x.astype(jnp.float32) * cb + rot.astype(jnp.float32) * sb).astype(x.dtype)


def _diff_attention(q, k, v, lam):
    bsz, lq, nh, nm, dh = q.shape
    nb = lq // Q_BLOCK
    qb = jnp.moveaxis(q.reshape(bsz, nb, Q_BLOCK, nh, nm, dh), 1, 0)
    scale = dh ** -0.5

    def block(qblk):
        s = jnp.einsum('bqhmd,bkhmd->bhmqk', qblk, k).astype(jnp.float32) * scale
        p = jax.nn.softmax(s, axis=-1)
        wts = p[:, :, 0] - lam * p[:, :, 1]
        return jnp.einsum('bhqk,bkhe->bqhe', wts.astype(v.dtype), v)

    o = lax.map(block, qb)
    return jnp.moveaxis(o, 0, 1).reshape(bsz, lq, nh, v.shape[-1])


def _rwkv_scan(s0, r, w, kk, a, k, v, reverse):
    xs = tuple(jnp.swapaxes(t, 0, 1) for t in (r, w, kk, a, k, v))

    def step(s, xt):
        r_t, w_t, kk_t, a_t, k_t, v_t = xt
        sa = jnp.einsum('bhvk,bhk->bhv', s, -kk_t)
        s = (s * w_t[:, :, None, :] + sa[..., None] * (kk_t * a_t)[:, :, None, :]
             + v_t[..., None] * k_t[:, :, None, :])
        return s, jnp.einsum('bhvk,bhk->bhv', s, r_t)

    s_fin, ys = lax.scan(step, s0, xs, reverse=reverse)
    return jnp.swapaxes(ys, 0, 1), s_fin


def _rwkv_mix(u_rkv, u_w, u_a, u_g, s0, p):
    bsz, L, _ = u_rkv.shape
    f32 = jnp.float32

    def heads(t):
        return t.astype(f32).reshape(bsz, L, H_A, RW_HEAD)

    rkv = _short_conv(u_rkv, p['rwkv_conv'])
    r, k, v = jnp.split(rkv, 3, axis=-1)
    g = jax.nn.sigmoid(u_g) @ p['rwkv_g2']
    kk = heads(k * p['rwkv_kk'])
    kk = kk * lax.rsqrt(jnp.sum(kk * kk, axis=-1, keepdims=True) + 1e-12)
    rh, vh = heads(r), heads(v)
    uw = jnp.split(u_w, N_DIRS, axis=-1)
    ua = jnp.split(u_a, N_DIRS, axis=-1)
    ys, bonuses, finals = [], [], []
    for d in range(N_DIRS):
        wl = -jax.nn.softplus(-(p['rwkv_w0'][d] + jnp.tanh(uw[d]) @ p['rwkv_w2'][d]).astype(f32)) - 0.5
        decay = jnp.exp(-jnp.exp(wl))
        a = jax.nn.sigmoid(p['rwkv_a0'][d] + ua[d] @ p['rwkv_a2'][d])
        kd = heads(k * (1.0 + (a - 1.0) * p['rwkv_ka']))
        if s0 is None:
            init = jnp.zeros((bsz, H_A, RW_HEAD, RW_HEAD), f32)
        else:
            init = s0[:, d].astype(f32)
        yd, s_fin = _rwkv_scan(init, rh, heads(decay), kk, heads(a), kd, vh, reverse=(d == 1))
        ys.append(yd)
        bonuses.append(jnp.sum(rh * kd * p['rwkv_rk'].astype(f32), axis=-1, keepdims=True) * vh)
        finals.append(s_fin)
    y = ys[0] + ys[1]
    mu = jnp.mean(y, axis=-1, keepdims=True)
    var = jnp.mean(jnp.square(y - mu), axis=-1, keepdims=True)
    yn = ((y - mu) * lax.rsqrt(var + GN_EPS)).reshape(bsz, L, D_A)
    yn = yn * p['rwkv_ln_w'].astype(f32) + p['rwkv_ln_b'].astype(f32)
    out = (yn + (bonuses[0] + bonuses[1]).reshape(bsz, L, D_A)) * g.astype(f32)
    return out.astype(u_rkv.dtype), jnp.stack(finals, axis=1)


def _hyena_filters_freq(L, p):
    f32 = jnp.float32
    t = jnp.linspace(0.0, 1.0, L, dtype=f32)[:, None]
    ang = (2.0 * math.pi / L) * jnp.arange(L, dtype=f32)[:, None]
    bands = jnp.linspace(1e-4, HY_BANDS - 1, HY_BANDS, dtype=f32)[None, :]
    emb = jnp.concatenate([t, jnp.cos(bands * ang), -jnp.sin(bands * ang)], axis=-1)
    freq = p['hy_freq'].astype(f32)
    h = jnp.sin(freq * (emb @ p['hy_w1'].astype(f32) + p['hy_b1'].astype(f32)))
    h = jnp.sin(freq * (h @ p['hy_w2'].astype(f32) + p['hy_b2'].astype(f32)))
    h = (h @ p['hy_w3'].astype(f32)).reshape(L, HY_ORDER, HY_DIRS, D_C)
    h = h * jnp.exp(-t[:, :, None, None] * jnp.abs(p['hy_decay'].astype(f32)))
    h_fwd, h_bwd = h[:, :, 0], h[:, :, 1]
    filt = jnp.concatenate([h_fwd, jnp.zeros((1, HY_ORDER, D_C), f32), h_bwd[:0:-1]], axis=0)
    filt = filt * lax.rsqrt(jnp.sum(filt * filt, axis=0, keepdims=True) + 1e-6)
    return jnp.fft.rfft(filt, axis=0)


def _long_conv(z, filt_f, bias):
    L = z.shape[1]
    zf = jnp.fft.rfft(z.astype(jnp.float32), n=2 * L, axis=1)
    y = jnp.fft.irfft(zf * filt_f[None], n=2 * L, axis=1)[:, :L]
    return (y + bias.astype(jnp.float32) * z.astype(jnp.float32)).astype(z.dtype)


def _hyena_mix(u, p):
    L = u.shape[1]
    u = _short_conv(u, p['hy_conv_w']) + p['hy_conv_b']
    x1, x2, v = jnp.split(u, 3, axis=-1)
    filt_f = _hyena_filters_freq(L, p)
    z = x1 * _long_conv(v, filt_f[:, 0], p['hy_bias'][0])
    z = x2 * _long_conv(z, filt_f[:, 1], p['hy_bias'][1])
    return z


def _trunk_layer(x, mod, p, lam_init, cache):
    bsz, L, _ = x.shape
    f32 = jnp.float32
    sh1, sc1, gt1, sh2, sc2, gt2 = jnp.split(mod, 6, axis=-1)
    h = _rmsnorm(x, p['g_mix_pre']) * (1.0 + sc1) + sh1
    u_rkv, u_w, u_a, u_g, u_q, u_k, u_v, u_hy = _split_cols(h @ p['w_in'])
    y_a, s_fin = _rwkv_mix(u_rkv, u_w, u_a, u_g, None if cache is None else cache[0], p)
    q = u_q.reshape(bsz, L, H_B, 2, DIFF_DH)
    k = u_k.reshape(bsz, L, H_B, 2, DIFF_DH)
    v = u_v.reshape(bsz, L, H_B, 2 * DIFF_DH)
    lam = (jnp.exp(jnp.dot(p['diff_lq1'].astype(f32), p['diff_lk1'].astype(f32)))
           - jnp.exp(jnp.dot(p['diff_lq2'].astype(f32), p['diff_lk2'].astype(f32))) + lam_init)
    if cache is None:
        o = _diff_attention(q, k, v, lam)
    else:
        cos, sin = _axial_rope_tables(L)
        keys = jnp.concatenate([cache[1], _apply_rope(k, cos, sin)], axis=1)
        vals = jnp.concatenate([cache[2], v], axis=1)
        o = _diff_attention(_apply_rope(q, cos, sin), keys, vals, lam)
    y_b = (_rmsnorm(o, p['diff_subln']) * (1.0 - lam_init)).reshape(bsz, L, D_B)
    y_c = _hyena_mix(u_hy, p)
    mix = jnp.concatenate([y_a, y_b.astype(x.dtype), y_c.astype(x.dtype)], axis=-1) @ p['w_out']
    x = x + gt1 * _rmsnorm(mix, p['g_mix_post'])
    h = _rmsnorm(x, p['g_ffn_pre']) * (1.0 + sc2) + sh2
    f = jnp.square(jax.nn.relu(h @ p['w_ff1'])) @ p['w_ff2']
    x = x + gt2 * _rmsnorm(f, p['g_ffn_post'])
    return x, (s_fin, k, v)


def setup_inputs(seed: int = 0) -> dict:
    key = jax.random.key(seed)
    ks = jax.random.split(key, 48)

    def nrm(i, shape, scale=1.0):
        return jax.random.normal(ks[i], shape, jnp.float32) * scale

    D = D_MODEL
    conv_base = jnp.array([0.25, 1.0, 0.25], jnp.float32)[None, :, None]
    hy_rates = jnp.linspace(-math.log(HY_TARGET) / HY_LONG_PCT, -math.log(HY_TARGET) / HY_SHORT_PCT, D_C,
                            dtype=jnp.float32)
    return {
        'x_prompt': nrm(0, (BATCH, SEQ, D)),
        'x_sample': nrm(1, (DEC_BATCH, DEC_SEQ, D)),
        'state_rwkv': nrm(2, (DEC_BATCH, DEPTH, N_DIRS, H_A, RW_HEAD, RW_HEAD), 0.1),
        'cache_k': nrm(3, (DEC_BATCH, DEPTH, PAST_LEN, H_B, 2, DIFF_DH)),
        'cache_v': nrm(4, (DEC_BATCH, DEPTH, PAST_LEN, H_B, 2 * DIFF_DH)),
        'c': nrm(5, (DEC_BATCH, D)),
        'c_ctx': nrm(6, (D,)),
        'w_mod': nrm(7, (DEPTH, D, 6 * D), 0.5 * D ** -0.5),
        'b_mod': nrm(8, (DEPTH, 6 * D), 0.1),
        'g_mix_pre': 1.0 + nrm(9, (DEPTH, D), 0.05),
        'g_mix_post': 1.0 + nrm(10, (DEPTH, D), 0.05),
        'g_ffn_pre': 1.0 + nrm(11, (DEPTH, D), 0.05),
        'g_ffn_post': 1.0 + nrm(12, (DEPTH, D), 0.05),
        'w_in': nrm(13, (DEPTH, D, IN_COLS), D ** -0.5),
        'rwkv_conv': conv_base + nrm(14, (DEPTH, 3, 3 * D_A), 0.05),
        'rwkv_w0': jnp.linspace(-6.0, -1.0, D_A, dtype=jnp.float32) + nrm(15, (DEPTH, N_DIRS, D_A), 0.1),
        'rwkv_w2': nrm(16, (DEPTH, N_DIRS, LORA_W, D_A), 0.5 * LORA_W ** -0.5),
        'rwkv_a0': nrm(17, (DEPTH, N_DIRS, D_A), 0.1),
        'rwkv_a2': nrm(18, (DEPTH, N_DIRS, LORA_A, D_A), 0.5 * LORA_A ** -0.5),
        'rwkv_g2': nrm(19, (DEPTH, LORA_G, D_A), LORA_G ** -0.5),
        'rwkv_kk': 0.85 + nrm(20, (DEPTH, D_A), 0.05),
        'rwkv_ka': 1.0 + nrm(21, (DEPTH, D_A), 0.05),
        'rwkv_rk': nrm(22, (DEPTH, H_A, RW_HEAD), 0.1),
        'rwkv_ln_w': 1.0 + nrm(23, (DEPTH, D_A), 0.05),
        'rwkv_ln_b': nrm(24, (DEPTH, D_A), 0.02),
        'diff_lq1': nrm(25, (DEPTH, DIFF_DH), 0.1),
        'diff_lk1': nrm(26, (DEPTH, DIFF_DH), 0.1),
        'diff_lq2': nrm(27, (DEPTH, DIFF_DH), 0.1),
        'diff_lk2': nrm(28, (DEPTH, DIFF_DH), 0.1),
        'diff_subln': 1.0 + nrm(29, (DEPTH, 2 * DIFF_DH), 0.05),
        'hy_conv_w': conv_base + nrm(30, (DEPTH, 3, 3 * D_C), 0.05),
        'hy_conv_b': nrm(31, (DEPTH, 3 * D_C), 0.02),
        'hy_w1': nrm(32, (DEPTH, HY_EMB, HY_FFN), HY_EMB ** -0.5),
        'hy_b1': nrm(33, (DEPTH, HY_FFN), 0.1),
        'hy_freq': 1.0 + nrm(34, (DEPTH, HY_FFN), 0.05),
        'hy_w2': nrm(35, (DEPTH, HY_FFN, HY_FFN), HY_FFN ** -0.5),
        'hy_b2': nrm(36, (DEPTH, HY_FFN), 0.1),
        'hy_w3': nrm(37, (DEPTH, HY_FFN, HY_ORDER * HY_DIRS * D_C), HY_FFN ** -0.5),
        'hy_decay': hy_rates * (1.0 + nrm(38, (DEPTH, D_C), 0.05)),
        'hy_bias': nrm(39, (DEPTH, HY_ORDER, D_C), 0.1),
        'w_out': nrm(40, (DEPTH, D, D), D ** -0.5),
        'w_ff1': nrm(41, (DEPTH, D, D_FF), D ** -0.5),
        'w_ff2': nrm(42, (DEPTH, D_FF, D), D_FF ** -0.5),
    }


def reference(x_prompt, x_sample, state_rwkv, cache_k, cache_v, c, c_ctx, w_mod, b_mod,
              g_mix_pre, g_mix_post, g_ffn_pre, g_ffn_post, w_in, rwkv_conv, rwkv_w0, rwkv_w2,
              rwkv_a0, rwkv_a2, rwkv_g2, rwkv_kk, rwkv_ka, rwkv_rk, rwkv_ln_w, rwkv_ln_b,
              diff_lq1, diff_lk1, diff_lq2, diff_lk2, diff_subln, hy_conv_w, hy_conv_b,
              hy_w1, hy_b1, hy_freq, hy_w2, hy_b2, hy_w3, hy_decay, hy_bias,
              w_out, w_ff1, w_ff2):
    xp = x_prompt
    xs = x_sample
    st_list, k_list, v_list = [], [], []
    for l in range(DEPTH):
        p = {
            'g_mix_pre': g_mix_pre[l], 'g_mix_post': g_mix_post[l],
            'g_ffn_pre': g_ffn_pre[l], 'g_ffn_post': g_ffn_post[l],
            'w_in': w_in[l], 'rwkv_conv': rwkv_conv[l], 'rwkv_w0': rwkv_w0[l], 'rwkv_w2': rwkv_w2[l],
            'rwkv_a0': rwkv_a0[l], 'rwkv_a2': rwkv_a2[l], 'rwkv_g2': rwkv_g2[l],
            'rwkv_kk': rwkv_kk[l], 'rwkv_ka': rwkv_ka[l], 'rwkv_rk': rwkv_rk[l],
            'rwkv_ln_w': rwkv_ln_w[l], 'rwkv_ln_b': rwkv_ln_b[l],
            'diff_lq1': diff_lq1[l], 'diff_lk1': diff_lk1[l], 'diff_lq2': diff_lq2[l],
            'diff_lk2': diff_lk2[l], 'diff_subln': diff_subln[l],
            'hy_conv_w': hy_conv_w[l], 'hy_conv_b': hy_conv_b[l], 'hy_w1': hy_w1[l], 'hy_b1': hy_b1[l],
            'hy_freq': hy_freq[l], 'hy_w2': hy_w2[l], 'hy_b2': hy_b2[l], 'hy_w3': hy_w3[l],
            'hy_decay': hy_decay[l], 'hy_bias': hy_bias[l],
            'w_out': w_out[l], 'w_ff1': w_ff1[l], 'w_ff2': w_ff2[l],
        }
        lam_init = 0.8 - 0.6 * math.exp(-0.3 * l)
        mod_ctx = (jax.nn.silu(c_ctx) @ w_mod[l] + b_mod[l])[None, None, :]
        xp, (s_ctx, k_ctx, v_ctx) = _trunk_layer(xp, mod_ctx, p, lam_init, None)
        st_list.append(s_ctx.astype(x_prompt.dtype))
        k_list.append(k_ctx)
        v_list.append(v_ctx)
        mod_lat = (jax.nn.silu(c) @ w_mod[l] + b_mod[l])[:, None, :]
        xs, _ = _trunk_layer(xs, mod_lat, p, lam_init, (state_rwkv[:, l], cache_k[:, l], cache_v[:, l]))
    new_state_rwkv = jnp.stack(st_list, axis=1)
    new_cache_k = jnp.stack(k_list, axis=1)
    new_cache_v = jnp.stack(v_list, axis=1)
    return (xp, xs, new_state_rwkv, new_cache_k, new_cache_v)
```

```python
import math
import os
from contextlib import ExitStack
import numpy as np
import ml_dtypes
import concourse.bass as bass
import concourse.mybir as mybir
from concourse.bass_types import AP
from concourse.bass_utils import run_bass_kernel_spmd

F32 = mybir.dt.float32
BF16 = mybir.dt.bfloat16
AF = mybir.ActivationFunctionType
ALU = mybir.AluOpType
AX = mybir.AxisListType

ENGS = ("pe", "act", "dve", "pool", "sp")
NDSEM = 16


class _Rec:
    def __getattr__(self, name):
        def f(*a, **k):
            self.call = (name, a, k)
            return self
        return f


class Prog:
    def __init__(self, nc):
        self.nc = nc
        self.ops = {e: [] for e in ENGS}
        self.cnt = {e: 0 for e in ENGS}
        self.waited = {e: {} for e in ENGS}
        self.lastw = {}
        self.readers = {}
        self.dq = {e: {"next": 0, "use": [0] * NDSEM} for e in ("sp", "act", "pool")}
        self.sems = {}
        self.nps = 0

    def _need(self, e, deps):
        waits = []
        w = self.waited[e]
        best = {}
        for (sk, v) in deps:
            if v > best.get(sk, 0):
                best[sk] = v
        for sk, v in best.items():
            if w.get(sk, 0) >= v:
                continue
            w[sk] = v
            waits.append((sk, v))
        return waits

    def _deps(self, reads, writes):
        deps = []
        for k in reads:
            t = self.lastw.get(k)
            if t is not None:
                deps.append(t)
        for k in writes:
            t = self.lastw.get(k)
            if t is not None:
                deps.append(t)
            for sk, v in self.readers.get(k, {}).items():
                deps.append((sk, v))
        return deps

    def _commit(self, tok, reads, writes):
        for k in reads:
            r = self.readers.setdefault(k, {})
            if tok[1] > r.get(tok[0], 0):
                r[tok[0]] = tok[1]
        for k in writes:
            self.lastw[k] = tok
            self.readers[k] = {}

    def op(self, e, fn, reads=(), writes=(), pe_acc=False):
        rec = _Rec()
        fn(rec)
        call = rec.call
        fn = lambda eh, call=call: getattr(eh, call[0])(*call[1], **call[2])
        deps = self._deps(reads, writes)
        if pe_acc:
            deps = [d for d in deps if d[0] != ("c", "pe")]
        waits = self._need(e, deps)
        self.cnt[e] += 1
        tok = (("c", e), self.cnt[e])
        self.ops[e].append((waits, fn, ("c", e)))
        self._commit(tok, reads, writes)
        return tok

    def dma(self, q, out, in_, reads=(), writes=(), **kw):
        d = self.dq[q]
        i = d["next"]
        d["next"] = (i + 1) % NDSEM
        deps = self._deps(reads, writes)
        if d["use"][i] > 0:
            deps.append((("d", q, i), 16 * d["use"][i]))
        waits = self._need(q, deps)
        d["use"][i] += 1
        tok = (("d", q, i), 16 * d["use"][i])

        def fn(eh, out=out, in_=in_, kw=kw):
            return eh.dma_start(out=out, in_=in_, **kw)
        self.ops[q].append((waits, fn, ("d", q, i)))
        self._commit(tok, reads, writes)
        return tok

    def barrier(self):
        toks = [(("c", e), self.cnt[e]) for e in ENGS if self.cnt[e] > 0]
        for q in ("sp", "act", "pool"):
            for i in range(NDSEM):
                u = self.dq[q]["use"][i]
                if u > 0:
                    toks.append((("d", q, i), 16 * u))
        for e in ENGS:
            waits = self._need(e, list(toks))
            if waits:
                self.ops[e].append((waits, None, None))

    def finish_wait(self, e, tokens):
        waits = self._need(e, list(tokens))
        self.ops[e].append((waits, None, None))

    def emit(self, st):
        nc = self.nc
        for e in ENGS:
            self.sems[("c", e)] = st.enter_context(nc.semaphore("c_" + e))
        for q in ("sp", "act", "pool"):
            for i in range(NDSEM):
                self.sems[("d", q, i)] = st.enter_context(nc.semaphore("d_%s_%d" % (q, i)))
        block = st.enter_context(nc.Block())

        def mk(e):
            def body(eh):
                for (waits, fn, inc) in self.ops[e]:
                    for (sk, v) in waits:
                        eh.wait_ge(self.sems[sk], v)
                    if fn is None:
                        continue
                    ins = fn(eh)
                    ins.then_inc(self.sems[inc], 1 if inc[0] == "c" else 16)
            return body
        block.tensor(mk("pe"))
        block.scalar(mk("act"))
        block.vector(mk("dve"))
        block.gpsimd(mk("pool"))
        block.sync(mk("sp"))


class Cfg:
    def __init__(self, depth=4, ls=2048, lp=256, past=256, nps=2):
        self.depth, self.ls, self.lp, self.past, self.nps = depth, ls, lp, past, nps
        self.nt = ls + nps * lp
        self.d = 1024
        self.kc = 8
        self.incols = 3456
        self.dff = 4096
        self.tiles = []
        for s in range(0, ls, 512):
            self.tiles.append((s, min(512, ls - s), 0))
        for s in range(ls, self.nt, 512):
            self.tiles.append((s, min(512, self.nt - s), 1))
        self.seqs = [(0, ls, 0)] + [(ls + i * lp, lp, 1) for i in range(nps)]


def lam_init_of(l):
    return 0.8 - 0.6 * math.exp(-0.3 * l)


def host_consts(cfg):
    c = {}
    c["ident_f"] = np.eye(128, dtype=np.float32)
    c["ident_b"] = np.eye(128, dtype=np.float32).astype(ml_dtypes.bfloat16)
    ob = np.zeros((128, 128), np.float32)
    ob[:64, :64] = 1.0
    ob[64:, 64:] = 1.0
    c["onesblk_f"] = ob
    c["onesblk_b"] = ob.astype(ml_dtypes.bfloat16)
    c["ones_b"] = np.ones((128, 128), np.float32).astype(ml_dtypes.bfloat16)
    c["ones_f"] = np.ones((128, 128), np.float32)
    c["i2_f"] = np.concatenate([np.eye(64, dtype=np.float32)] * 2, axis=0)
    sel = np.zeros((64, 2, 128), np.float32)
    for j in range(2):
        sel[np.arange(64), j, j * 64 + np.arange(64)] = 1.0
    c["sel_f"] = sel
    R = np.zeros((128, 128), np.float32)
    for m in range(2):
        b = m * 64
        for i in range(16):
            R[b + 16 + i, b + i] = -1.0
            R[b + i, b + 16 + i] = 1.0
            R[b + 48 + i, b + 32 + i] = -1.0
            R[b + 32 + i, b + 48 + i] = 1.0
    c["ropeR"] = R.astype(ml_dtypes.bfloat16)
    L = cfg.ls
    rows = L // 64
    row = np.repeat(np.arange(rows), 64).astype(np.float32)
    col = np.tile(np.arange(64), rows).astype(np.float32)
    inv = (10000.0 ** (-np.arange(0, 32, 2, dtype=np.float32) / 32)).astype(np.float32)
    ang_r = row[:, None] * inv[None]
    ang_c = col[:, None] * inv[None]
    ang = np.concatenate([ang_r, ang_r, ang_c, ang_c], axis=-1)
    cos = np.cos(ang).astype(np.float32).T
    sin = np.sin(ang).astype(np.float32).T
    c["cos"] = np.concatenate([cos, cos], axis=0).astype(ml_dtypes.bfloat16)
    c["sin"] = np.concatenate([sin, sin], axis=0).astype(ml_dtypes.bfloat16)
    for nm, Lx in (("s", cfg.ls), ("p", cfg.lp)):
        t = np.linspace(0.0, 1.0, Lx, dtype=np.float32)[:, None]
        ang2 = (np.float32(2.0 * math.pi / Lx) * np.arange(Lx, dtype=np.float32))[:, None]
        bands = np.linspace(1e-4, 15, 16, dtype=np.float32)[None, :]
        emb = np.concatenate([t, np.cos(bands * ang2), -np.sin(bands * ang2)], axis=-1).astype(np.float32)
        c["embT_" + nm] = emb.T.copy()
        SC = Lx // 128
        FC = (Lx + 1 + 127) // 128
        c["tlinT_" + nm] = t[:, 0].reshape(SC, 128).T.copy()
        n2 = 2 * Lx
        f = np.arange(FC * 128, dtype=np.int64)
        sidx = np.arange(Lx, dtype=np.int64)
        m = (sidx[:, None] * f[None, :]) % n2
        ang3 = 2.0 * np.pi * m.astype(np.float64) / n2
        valid = (f <= Lx).astype(np.float64)[None, :]
        Cf = np.cos(ang3) * valid
        Sf = -np.sin(ang3) * valid
        def lay_f(M):
            return np.ascontiguousarray(M.reshape(SC, 128, FC, 128).transpose(2, 1, 0, 3)).astype(ml_dtypes.bfloat16)
        c["fwdC_" + nm] = lay_f(Cf)
        c["fwdS_" + nm] = lay_f(Sf)
        wgt = np.where((f == 0) | (f == Lx), 1.0, 2.0) * (f <= Lx) / n2
        Ci = (np.cos(ang3) * wgt[None, :]).T
        Si = (-np.sin(ang3) * wgt[None, :]).T
        c["invC_" + nm] = np.ascontiguousarray(Ci.reshape(FC, 128, Lx)).astype(ml_dtypes.bfloat16)
        c["invS_" + nm] = np.ascontiguousarray(Si.reshape(FC, 128, Lx)).astype(ml_dtypes.bfloat16)
    return c


CONST_SHAPES = None


def build_program(cfg, consts, debug=(), mixers=()):
    nc = bass.Bass("TRN2", target_bir_lowering=False)
    D, KC, NT, LS, LP = cfg.d, cfg.kc, cfg.nt, cfg.ls, cfg.lp
    DEPTH = cfg.depth
    NPS = cfg.nps
    PAST = cfg.past
    din = {}

    def inp(name, shape, dt=F32):
        din[name] = nc.dram_tensor(name, list(shape), dt, kind="ExternalInput").ap()
        return din[name]

    def outp(name, shape):
        din[name] = nc.dram_tensor(name, list(shape), F32, kind="ExternalOutput").ap()
        return din[name]

    for k, v in consts.items():
        inp("c_" + k, v.shape, BF16 if v.dtype == ml_dtypes.bfloat16 else F32)
    xT_in = inp("xT", [128, KC, NT])
    cvec = inp("cvecT", [128, KC, 2])
    w_mod = inp("w_mod", [DEPTH, D, 6 * D])
    b_modT = inp("b_modT", [DEPTH, 128, 48])
    gains = inp("gains", [DEPTH, 128, 4, KC])
    w_in = inp("w_in", [DEPTH, D, cfg.incols])
    w_out = inp("w_out", [DEPTH, D, D])
    w_ff1 = inp("w_ff1", [DEPTH, D, cfg.dff])
    w_ff2 = inp("w_ff2", [DEPTH, cfg.dff, D])
    rw_conv = inp("rw_conv", [DEPTH, 128, 6, 3])
    rw_w0 = inp("rw_w0", [DEPTH, 128, 2, 2])
    rw_a0 = inp("rw_a0", [DEPTH, 128, 2, 2])
    rw_w2 = inp("rw_w2", [DEPTH, 128, 256])
    rw_a2 = inp("rw_a2", [DEPTH, 128, 256])
    rw_g2 = inp("rw_g2", [DEPTH, 128, 256])
    rw_vec = inp("rw_vec", [DEPTH, 128, 3, 2])
    rw_ln = inp("rw_ln", [DEPTH, 64, 2, 4])
    st_in = inp("st_in", [DEPTH, 128, 4, 64])
    ckT = inp("ckT", [DEPTH, 4, 128, PAST])
    cv = inp("cv", [DEPTH, 128, PAST // 128, 4, 128])
    dlam = inp("dlam", [DEPTH, 128, 4, 64])
    dsub = inp("dsub", [DEPTH, 128, 1])
    hy_conv = inp("hy_conv", [DEPTH, 128, 6, 4])
    hy_w1 = inp("hy_w1", [DEPTH, 33, 64])
    hy_vec = inp("hy_vec", [DEPTH, 64, 3])
    hy_w2 = inp("hy_w2", [DEPTH, 64, 64])
    hy_w3 = inp("hy_w3", [DEPTH, 64, 1024])
    hy_decb = inp("hy_decb", [DEPTH, 128, 256])
    hy_bias = inp("hy_bias", [DEPTH, 128, 2, 2])

    yT_out = outp("yT", [128, KC, NT])
    st_out = outp("st_out", [DEPTH, NPS, 2, 4, 64, 64])
    kc_out = outp("kc_out", [DEPTH, NPS * LP, 512])
    vc_out = outp("vc_out", [DEPTH, NPS * LP, 512])
    dbg = {}
    for nm, shape in debug:
        dbg[nm] = outp("dbg_" + nm, shape)
    x_spill = nc.dram_tensor("x_spill", [128, KC, NT], F32, kind="Internal").ap()

    st = ExitStack()
    P = Prog(nc)
    out_toks = []

    def sb(name, shape, dt=F32):
        return st.enter_context(nc.sbuf_tensor("s_" + name, list(shape), dt))

    C = {}
    for k, v in consts.items():
        if k[:4] in ("embT", "tlin", "fwdC", "fwdS", "invC", "invS"):
            continue
        if k in ("cos", "sin") and "attn" not in mixers:
            continue
        t = sb("k_" + k, v.shape, BF16 if v.dtype == ml_dtypes.bfloat16 else F32)
        P.dma("sp", t[:], din["c_" + k][:], writes=["k_" + k])
        C[k] = t
    CK = {k: ["k_" + k] for k in C}

    BIGB = getattr(cfg, 'bigb', 170 * 1024)
    XB = KC * NT * 4
    MB0 = BIGB - KC * NT * 2
    HB0 = MB0 - KC * NT * 2
    big = sb("big", [128, BIGB // 4])
    xT = big[:, 0:XB // 4].rearrange("p (c t) -> p c t", c=KC)
    P.dma("sp", xT, xT_in[:], writes=["xT"])
    cv_t = sb("cvec", [128, KC, 2])
    P.dma("sp", cv_t[:], cvec[:], writes=["cvec"])
    scv = sb("scvec", [128, KC, 2], BF16)
    P.op("act", lambda e: e.activation(out=scv[:], in_=cv_t[:], func=AF.Silu), reads=["cvec"], writes=["scvec"])

    PS = [st.enter_context(nc.psum_tensor("ps%d" % i, [128, 512], F32)) for i in range(8)]
    psn = [0]

    def psum():
        i = psn[0] % 4
        psn[0] += 1
        return PS[i], "ps%d" % i

    def psfix(i):
        return PS[4 + i], "ps%d" % (4 + i)

    hT = big[:, HB0 // 4:MB0 // 4].bitcast(BF16).rearrange("p (c t) -> p c t", c=KC)
    mixT = big[:, MB0 // 4:BIGB // 4].bitcast(BF16).rearrange("p (c t) -> p c t", c=KC)
    h_spill = nc.dram_tensor("h_spill", [128, KC, NT], BF16, kind="Internal").ap()
    modT = sb("modT", [128, 48, 2])
    bmod = sb("bmod", [128, 48])
    gn = sb("gains", [128, 4, KC])
    sc_all = sb("sc_all", [128, 4, KC, 2])
    wst = [sb("wst%d" % i, [128, KC, 128], BF16) for i in range(3)]
    wstn = [0]
    tmpA = [sb("tmpA%d" % i, [128, 512]) for i in range(3)]
    tmpn = [0]
    rstd = sb("rstd", [128, 512])
    sqb = [sb("sqb%d" % i, [128, 512], BF16) for i in range(2)]

    def tmp():
        i = tmpn[0] % 3
        tmpn[0] += 1
        return tmpA[i], "tmpA%d" % i

    arena = big

    class Carver:
        def __init__(self):
            self.off = 0
            self.lim = BIGB

        def reset(self, base=0, lim=None):
            self.off = base
            self.lim = BIGB if lim is None else lim

        def get(self, shape, dt=F32):
            n = int(np.prod(shape[1:]))
            nbytes = n * (4 if dt == F32 else 2)
            nbytes = (nbytes + 31) // 32 * 32
            assert self.off + nbytes <= self.lim, ("arena overflow", self.off, nbytes, self.lim)
            a = arena[0:shape[0], self.off // 4:(self.off + nbytes) // 4]
            self.off += nbytes
            if dt != F32:
                a = a.bitcast(dt)
                a = a[:, 0:n]
            if len(shape) > 2:
                names = " ".join("d%d" % i for i in range(1, len(shape)))
                kw = {"d%d" % i: shape[i] for i in range(1, len(shape))}
                a = a.rearrange("p (%s) -> p %s" % (names, names), **kw)
            return a
    carver = Carver()

    def load_w(src_ap, kchunks, dst=None, key=None):
        if dst is None:
            i = wstn[0] % 3
            wstn[0] += 1
            dst, key = wst[i], "wst%d" % i
        P.dma("pool", dst[:, 0:kchunks, 0:src_ap.shape[1]],
              src_ap.rearrange("(kc p) c -> p kc c", p=128), writes=[key])
        return dst, key

    def rms_stats(src_fn, src_keys, t0, n, out_rstd, out_key, nchunks=KC, scale=1.0 / 1024, eps=1e-6, ones=None):
        ps, pk = psum()
        for c in range(nchunks):
            sq, sk = sqb[c % 2], "sqb%d" % (c % 2)
            src = src_fn(c)
            P.op("act", lambda e, sq=sq, src=src: e.activation(out=sq[:, 0:n], in_=src, func=AF.Square),
                 reads=src_keys, writes=[sk])
            P.op("pe", lambda e, sq=sq, c=c: e.matmul(ps[:, 0:n], lhsT=(ones or C["ones_b"])[:], rhs=sq[:, 0:n],
                                                       start=(c == 0), stop=(c == nchunks - 1)),
                 reads=[sk, "k_ones_b"], writes=[pk], pe_acc=(c > 0))
        P.op("act", lambda e: e.activation(out=out_rstd[:, 0:n], in_=ps[:, 0:n], func=AF.Ln, scale=scale, bias=epsc(eps)),
             reads=[pk, "epsc"], writes=[out_key])
        P.op("act", lambda e: e.activation(out=out_rstd[:, 0:n], in_=out_rstd[:, 0:n], func=AF.Exp, scale=-0.5),
             reads=[out_key], writes=[out_key])

    eps_tiles = {}

    def epsc(v):
        return eps_tiles[v][:, 0:1]

    for v in (1e-6, 1e-12, 64e-5):
        t = sb("eps%d" % len(eps_tiles), [128, 1])
        eps_tiles[v] = t
        P.op("pool", lambda e, t=t, v=v: e.memset(t[:], v), writes=["epsc"])

    def barrier():
        P.barrier()

    def dbg_out(nm, src_ap, keys):
        if nm in dbg:
            out_toks.append(P.dma("pool", dbg[nm][:], src_ap, reads=keys, writes=["dbg_" + nm]))

    pconv = sb("pconv", [128, 6, 3]); pw0 = sb("pw0", [128, 2, 2]); pa0 = sb("pa0", [128, 2, 2])
    pvec = sb("pvec", [128, 3, 2]); pln = sb("pln", [64, 2, 4])
    w2t = sb("w2t", [128, 256], BF16); a2t = sb("a2t", [128, 256], BF16); g2t = sb("g2t", [128, 256], BF16)
    omka = sb("omka", [128, 2])
    plam = sb("plam", [128, 4, 64]); psub = sb("psub", [128, 1]); lamv = sb("lamv", [128, 4]); lamt = sb("lamt", [128, 64])
    hconv = sb("hconv", [128, 6, 4]); hw1 = sb("hw1", [33, 64]); hvec = sb("hvec", [64, 3]); hw2 = sb("hw2", [64, 64])
    hsc = sb("hsc", [64, 6]); fss = sb("fss", [128, 3]); hdec = sb("hdec", [128, 2]); hbias = sb("hbias", [128, 2, 2]); fb = sb("fb", [64, 2])
    def mkap(base, offset_elems, dims):
        return AP(base.tensor, base.offset + offset_elems, [list(base.ap[0])] + [list(d) for d in dims])

    def rwkv_layer(l, rkv, lor):
        P.dma("sp", h_spill[:], hT, reads=["hT"], writes=["h_spill"])
        barrier()
        carver.lim = MB0
        for (t, src) in ((pconv, rw_conv), (pw0, rw_w0), (pa0, rw_a0), (pvec, rw_vec), (pln, rw_ln)):
            P.dma("sp", t[:], src[l], writes=["rwp"])
        for (t, src) in ((w2t, rw_w2), (a2t, rw_a2), (g2t, rw_g2)):
            P.dma("pool", t[:], src[l], writes=["rwp"])
        P.op("dve", lambda e: e.tensor_scalar(out=omka[:], in0=pvec[:, 1, :], scalar1=-1.0, scalar2=1.0, op0=ALU.mult, op1=ALU.add),
             reads=["rwp"], writes=["omka"])
        tmpc = carver.get([128, LS])
        LH = LS // 2
        NPL = NPS * LP
        WS = carver.get([128, 2, 2, LH]); NBS = carver.get([128, 2, 2, LH], BF16); KDS = carver.get([128, 2, 2, LH], BF16)
        WP = carver.get([128, 2, 2, NPL]); NBP = carver.get([128, 2, 2, NPL], BF16); KDP = carver.get([128, 2, 2, NPL], BF16)
        snames = ["s"] + ["p%d" % i for i in range(NPS)]
        R32b = {sn: carver.get([128, 2, 2, 64]) for sn in snames}
        Ast = {sn: [carver.get([128, 4, 64]) for _ in range(2)] for sn in snames}
        Bst = {sn: carver.get([128, 4, 64]) for sn in snames}
        G1t = {sn: carver.get([128, 4, 64]) for sn in snames}
        DVt = {sn: carver.get([128, 4, 64]) for sn in snames}
        KVt = {sn: carver.get([128, 4, 64]) for sn in snames}
        gT = carver.get([128, 2, 512])
        bon = mixT[:, 0:2, :]
        Y = mixT[0:64, 2:6, :]
        kkn = mixT[:, 6:8, :]
        for c in range(6):
            for (s0, L, cnd) in cfg.seqs:
                u = rkv[:, c, s0:s0 + L]
                P.op("act", lambda e: e.activation(out=tmpc[:, 0:L], in_=u, func=AF.Identity, scale=pconv[:, c, 1:2]),
                     reads=["rkv", "rwp"], writes=["tmpc"])
                P.op("dve", lambda e: e.scalar_tensor_tensor(out=tmpc[:, 1:L], in0=rkv[:, c, s0:s0 + L - 1], scalar=pconv[:, c, 0:1],
                                                             in1=tmpc[:, 1:L], op0=ALU.mult, op1=ALU.add),
                     reads=["rkv", "rwp", "tmpc"], writes=["tmpc"])
                P.op("dve", lambda e: e.scalar_tensor_tensor(out=tmpc[:, 0:L - 1], in0=rkv[:, c, s0 + 1:s0 + L], scalar=pconv[:, c, 2:3],
                                                             in1=tmpc[:, 0:L - 1], op0=ALU.mult, op1=ALU.add),
                     reads=["rkv", "rwp", "tmpc"], writes=["tmpc"])
                P.op("act", lambda e: e.copy(out=rkv[:, c, s0:s0 + L], in_=tmpc[:, 0:L]), reads=["tmpc"], writes=["rkv"])

        def gen_range(t0, n, dsts, first):
            for p in range(2):
                if first:
                    kk_t, kk_k = tmp()
                    P.op("act", lambda e: e.activation(out=kk_t[:, 0:n], in_=rkv[:, 2 + p, t0:t0 + n], func=AF.Identity, scale=pvec[:, 0, p:p + 1]),
                         reads=["rkv", "rwp"], writes=[kk_k])
                    P.op("act", lambda e: e.activation(out=sqb[0][:, 0:n], in_=kk_t[:, 0:n], func=AF.Square), reads=[kk_k], writes=["sqb0"])
                    ps, pk = psum()
                    P.op("pe", lambda e: e.matmul(ps[:, 0:n], lhsT=C["onesblk_b"][:], rhs=sqb[0][:, 0:n], start=True, stop=True),
                         reads=["sqb0", "k_onesblk_b"], writes=[pk])
                    P.op("act", lambda e: e.activation(out=rstd[:, 0:n], in_=ps[:, 0:n], func=AF.Ln, bias=epsc(1e-12)), reads=[pk, "epsc"], writes=["rstd"])
                    P.op("act", lambda e: e.activation(out=rstd[:, 0:n], in_=rstd[:, 0:n], func=AF.Exp, scale=-0.5), reads=["rstd"], writes=["rstd"])
                    P.op("dve", lambda e: e.tensor_tensor(out=kkn[:, p, t0:t0 + n], in0=kk_t[:, 0:n], in1=rstd[:, 0:n], op=ALU.mult),
                         reads=[kk_k, "rstd"], writes=["kkn"])
                    psb, psbk = psfix(3)
                for d in range(2):
                    if dsts[d] is None and not first:
                        continue
                    if dsts[d] is not None:
                        ps, pk = psum()
                        P.op("pe", lambda e: e.matmul(ps[:, 0:n], lhsT=w2t[d * 64:(d + 1) * 64, p * 128:(p + 1) * 128],
                                                      rhs=lor[d * 64:(d + 1) * 64, 0, t0:t0 + n], start=True, stop=True),
                             reads=["rwp", "lor"], writes=[pk])
                        sg, sgk = tmp()
                        P.op("act", lambda e: e.activation(out=sg[:, 0:n], in_=ps[:, 0:n], func=AF.Sigmoid, bias=pw0[:, d, p:p + 1]),
                             reads=[pk, "rwp"], writes=[sgk])
                        P.op("act", lambda e: e.activation(out=dsts[d][0](p), in_=sg[:, 0:n], func=AF.Exp, scale=-math.exp(-0.5)),
                             reads=[sgk], writes=[dsts[d][3]])
                    ps, pk = psum()
                    P.op("pe", lambda e: e.matmul(ps[:, 0:n], lhsT=a2t[d * 64:(d + 1) * 64, p * 128:(p + 1) * 128],
                                                  rhs=lor[d * 64:(d + 1) * 64, 1, t0:t0 + n], start=True, stop=True),
                         reads=["rwp", "lor"], writes=[pk])
                    av, avk = tmp()
                    P.op("act", lambda e: e.activation(out=av[:, 0:n], in_=ps[:, 0:n], func=AF.Sigmoid, bias=pa0[:, d, p:p + 1]),
                         reads=[pk, "rwp"], writes=[avk])
                    if dsts[d] is not None:
                        P.op("dve", lambda e: e.scalar_tensor_tensor(out=dsts[d][1](p), in0=av[:, 0:n], scalar=-1.0, in1=kkn[:, p, t0:t0 + n],
                                                                     op0=ALU.mult, op1=ALU.mult), reads=[avk, "kkn"], writes=[dsts[d][3]])
                    P.op("dve", lambda e: e.tensor_scalar(out=av[:, 0:n], in0=av[:, 0:n], scalar1=pvec[:, 1, p:p + 1], scalar2=omka[:, p:p + 1],
                                                          op0=ALU.mult, op1=ALU.add), reads=[avk, "rwp", "omka"], writes=[avk])
                    P.op("dve", lambda e: e.tensor_tensor(out=av[:, 0:n], in0=av[:, 0:n], in1=rkv[:, 2 + p, t0:t0 + n], op=ALU.mult),
                         reads=[avk, "rkv"], writes=[avk])
                    if dsts[d] is not None:
                        P.op("act", lambda e: e.copy(out=dsts[d][2](p), in_=av[:, 0:n]), reads=[avk], writes=[dsts[d][3]])
                    if first:
                        P.op("dve", lambda e: e.scalar_tensor_tensor(out=sqb[1][:, 0:n], in0=av[:, 0:n], scalar=pvec[:, 2, p:p + 1],
                                                                     in1=rkv[:, p, t0:t0 + n], op0=ALU.mult, op1=ALU.mult),
                             reads=[avk, "rwp", "rkv"], writes=["sqb1"])
                        P.op("pe", lambda e: e.matmul(psb[:, 0:n], lhsT=C["onesblk_b"][:], rhs=sqb[1][:, 0:n], start=(d == 0), stop=(d == 1)),
                             reads=["sqb1", "k_onesblk_b"], writes=[psbk], pe_acc=(d == 1))
                if first:
                    P.op("dve", lambda e: e.tensor_tensor(out=bon[:, p, t0:t0 + n], in0=psb[:, 0:n], in1=rkv[:, 4 + p, t0:t0 + n], op=ALU.mult),
                         reads=[psbk, "rkv"], writes=["bon"])

        PIECE = min(512, LH)

        def s_dsts(t0, n, phase):
            half = 0 if t0 < LH else 1
            d = half if phase == 0 else 1 - half
            col = t0 - half * LH
            out = [None, None]
            out[d] = (lambda p: WS[:, p, d, col:col + n], lambda p: NBS[:, p, d, col:col + n], lambda p: KDS[:, p, d, col:col + n], "opS")
            return out
        for t0 in range(0, LS, PIECE):
            gen_range(t0, PIECE, s_dsts(t0, PIECE, 0), True)
        for q0 in range(0, NPL, PIECE if NPL >= PIECE else NPL):
            n = min(PIECE, NPL - q0)
            both = [(lambda p, d=d: WP[:, p, d, q0:q0 + n], lambda p, d=d: NBP[:, p, d, q0:q0 + n], lambda p, d=d: KDP[:, p, d, q0:q0 + n], "opP") for d in range(2)]
            gen_range(LS + q0, n, both, True)
        barrier()
        streams = [("s", 0, LS, True, 0)] + [("p%d" % i, LS + i * LP, LP, False, i * LP) for i in range(NPS)]
        ypsums = {}
        for (sn, s0, L, is_s, q0) in streams:
            if is_s:
                P.dma("sp", Ast[sn][0], st_in[l], writes=["A_" + sn])
            else:
                P.op("pool", lambda e: e.memset(Ast[sn][0], 0.0), writes=["A_" + sn])
            ypsums[sn] = psfix(len(ypsums))
        maxL = max(L for (_, _, L, _, _) in streams)
        for i in range(maxL):
            if i == LH:
                for t0 in range(0, LS, PIECE):
                    gen_range(t0, PIECE, s_dsts(t0, PIECE, 1), False)
            for (sn, s0, L, is_s, q0) in streams:
                if i >= L:
                    continue
                cur, nxt = Ast[sn][i % 2], Ast[sn][(i + 1) % 2]
                ak = "A_" + sn
                dstr = L - 1 - 2 * i
                if is_s:
                    il = i % LH
                    Wb, NBb, KDb, opk, Lb, off = WS, NBS, KDS, "opS", LH, il
                    dl = LH - 1 - 2 * il
                else:
                    Wb, NBb, KDb, opk, Lb, off = WP, NBP, KDP, "opP", NPL, q0 + i
                    dl = L - 1 - 2 * i

                def gop(base3, pstride, dextra):
                    return mkap(base3, s0 + i, [[pstride, 2], [dextra + dstr, 2], [0, 64]])

                def lop(buf):
                    return mkap(buf[:, 0, 0, 0:1], off, [[2 * Lb, 2], [Lb + dl, 2], [0, 64]])
                kk_op = gop(kkn[:, 0, 0:1], kkn.ap[1][0], 0)
                v_op = gop(rkv[:, 4, 0:1], NT, 0)
                w_op = lop(Wb)
                kd_op = lop(KDb)
                slot = i % 64
                if slot == 0:
                    i0 = i
                    P.op("act", lambda e: e.copy(out=R32b[sn][:, :, 0, :], in_=rkv[:, 0:2, s0 + i0:s0 + i0 + 64]), reads=["rkv"], writes=["R32" + sn])
                    P.op("act", lambda e: e.copy(out=R32b[sn][:, :, 1, :], in_=rkv[:, 0:2, s0 + L - 64 - i0:s0 + L - i0]), reads=["rkv"], writes=["R32" + sn])
                P.op("dve", lambda e: e.tensor_tensor(out=G1t[sn], in0=cur, in1=kk_op, op=ALU.mult), reads=[ak, "kkn"], writes=["G1" + sn])
                ps1, ps1k = psum()
                P.op("pe", lambda e: e.matmul(ps1[:, 0:256], lhsT=C["onesblk_f"][:], rhs=G1t[sn].rearrange("p g v -> p (g v)"), start=True, stop=True),
                     reads=["G1" + sn, "k_onesblk_f"], writes=[ps1k])
                P.op("pool", lambda e: e.tensor_tensor(out=DVt[sn], in0=mkap(C["i2_f"][:, 0:1], 0, [[0, 4], [1, 64]]), in1=v_op, op=ALU.mult),
                     reads=["rkv", "k_i2_f"], writes=["DV" + sn])
                ps2, ps2k = psum()
                P.op("pe", lambda e: e.matmul(ps2[:, 0:256], lhsT=C["onesblk_f"][:], rhs=DVt[sn].rearrange("p g v -> p (g v)"), start=True, stop=True),
                     reads=["DV" + sn, "k_onesblk_f"], writes=[ps2k])
                P.op("dve", lambda e: e.tensor_tensor(out=KVt[sn], in0=ps2[:, 0:256].rearrange("p (g v) -> p g v", g=4), in1=kd_op, op=ALU.mult),
                     reads=[ps2k, opk], writes=["KV" + sn])
                P.op("pool", lambda e: e.tensor_tensor(out=Bst[sn], in0=cur, in1=w_op, op=ALU.mult), reads=[ak, opk], writes=["B" + sn])
                P.op("pool", lambda e: e.tensor_tensor(out=Bst[sn], in0=Bst[sn], in1=KVt[sn], op=ALU.add), reads=["B" + sn, "KV" + sn], writes=["B" + sn])
                for g in range(4):
                    p, d = g // 2, g % 2
                    lcol = off if d == 0 else off + dl
                    P.op("dve", lambda e: e.scalar_tensor_tensor(out=nxt[:, g, :], in0=ps1[:, g * 64:(g + 1) * 64], scalar=NBb[:, p, d, lcol:lcol + 1],
                                                                 in1=Bst[sn][:, g, :], op0=ALU.mult, op1=ALU.add),
                         reads=[ps1k, opk, "B" + sn], writes=[ak])
                yp, ypk = ypsums[sn]
                for g in range(4):
                    p, d = g // 2, g % 2
                    rcol = slot if d == 0 else 63 - slot
                    for j in range(2):
                        col = ((slot * 2 + d) * 2 + p) * 2 + j
                        P.op("pe", lambda e: e.matmul(yp[0:64, col:col + 1], lhsT=nxt[j * 64:(j + 1) * 64, g, :], rhs=R32b[sn][j * 64:(j + 1) * 64, p, d, rcol:rcol + 1],
                                                      start=True, stop=True), reads=[ak, "R32" + sn], writes=[ypk], pe_acc=True)
                if slot == 63 or i == L - 1:
                    ns = slot + 1
                    i0 = i - slot
                    first = i0 < L // 2
                    ypv = yp[0:64, 0:ns * 8].rearrange("v (s d h) -> v s d h", d=2, h=4)
                    for d in range(2):
                        if d == 0:
                            dst = mkap(Y[:, 0, 0:1], s0 + i0, [[1, ns], [Y.ap[1][0], 4]])
                        else:
                            dst = mkap(Y[:, 0, 0:1], s0 + L - 1 - i0, [[-1, ns], [Y.ap[1][0], 4]])
                        if first:
                            P.op("act", lambda e: e.copy(out=dst, in_=ypv[:, :, d, :]), reads=[ypk], writes=["Y"])
                        else:
                            P.op("dve", lambda e: e.tensor_tensor(out=dst, in0=ypv[:, :, d, :], in1=dst, op=ALU.add), reads=[ypk, "Y"], writes=["Y"])
        for si, (sn, s0, L, is_s, q0) in enumerate(streams):
            if is_s:
                continue
            fin = Ast[sn][L % 2]
            for g in range(4):
                p, d = g // 2, g % 2
                ps, pk = psum()
                P.op("pe", lambda e: e.transpose(ps[0:64, 0:128], fin[:, g, :], C["ident_f"][:]), reads=["A_" + sn, "k_ident_f"], writes=[pk])
                tt, tk = tmp()
                P.op("act", lambda e: e.copy(out=tt[0:64, 0:128], in_=ps[0:64, 0:128]), reads=[pk], writes=[tk])
                for j in range(2):
                    out_toks.append(P.dma("sp", st_out[l, si - 1, d, 2 * p + j], tt[0:64, j * 64:(j + 1) * 64], reads=[tk], writes=["st_out"]))
        for (t0, n, cnd) in cfg.tiles:
            for p in range(2):
                ps, pk = psum()
                P.op("pe", lambda e: e.matmul(ps[:, 0:n], lhsT=g2t[:, p * 128:(p + 1) * 128], rhs=lor[:, 2, t0:t0 + n], start=True, stop=True),
                     reads=["rwp", "lor"], writes=[pk])
                P.op("act", lambda e: e.copy(out=gT[:, p, 0:n], in_=ps[:, 0:n]), reads=[pk], writes=["gT"])
            for p in range(2):
                pso, psok = psfix(3)
                for j in range(2):
                    h = 2 * p + j
                    yv = Y[:, h, t0:t0 + n]
                    psm, psmk = psum()
                    yc, yck = tmp()
                    P.op("act", lambda e: e.copy(out=yc[0:64, 0:n], in_=yv), reads=["Y"], writes=[yck])
                    P.op("pe", lambda e: e.matmul(psm[0:64, 0:n], lhsT=C["ones_f"][0:64, 0:64], rhs=yc[0:64, 0:n], start=True, stop=True),
                         reads=[yck, "k_ones_f"], writes=[psmk])
                    P.op("dve", lambda e: e.scalar_tensor_tensor(out=yc[0:64, 0:n], in0=psm[0:64, 0:n], scalar=-1.0 / 64, in1=yc[0:64, 0:n],
                                                                 op0=ALU.mult, op1=ALU.add), reads=[psmk, yck], writes=[yck])
                    sq, sqk = tmp()
                    P.op("act", lambda e: e.activation(out=sq[0:64, 0:n], in_=yc[0:64, 0:n], func=AF.Square), reads=[yck], writes=[sqk])
                    psv, psvk = psum()
                    P.op("pe", lambda e: e.matmul(psv[0:64, 0:n], lhsT=C["ones_f"][0:64, 0:64], rhs=sq[0:64, 0:n], start=True, stop=True),
                         reads=[sqk, "k_ones_f"], writes=[psvk])
                    P.op("act", lambda e: e.activation(out=sq[0:64, 0:n], in_=psv[0:64, 0:n], func=AF.Ln, scale=1.0 / 64, bias=eps_tiles[64e-5][0:64, 0:1]),
                         reads=[psvk, "epsc"], writes=[sqk])
                    P.op("act", lambda e: e.activation(out=sq[0:64, 0:n], in_=sq[0:64, 0:n], func=AF.Exp, scale=-0.5), reads=[sqk], writes=[sqk])
                    P.op("dve", lambda e: e.tensor_tensor(out=yc[0:64, 0:n], in0=yc[0:64, 0:n], in1=sq[0:64, 0:n], op=ALU.mult), reads=[yck, sqk], writes=[yck])
                    P.op("act", lambda e: e.activation(out=yc[0:64, 0:n], in_=yc[0:64, 0:n], func=AF.Identity, scale=pln[:, 0, h:h + 1], bias=pln[:, 1, h:h + 1]),
                         reads=[yck, "rwp"], writes=[yck])
                    P.op("pe", lambda e: e.matmul(pso[:, 0:n], lhsT=C["sel_f"][:, j, :], rhs=yc[0:64, 0:n], start=(j == 0), stop=(j == 1)),
                         reads=[yck, "k_sel_f"], writes=[psok], pe_acc=(j == 1))
                tt, tk = tmp()
                P.op("dve", lambda e: e.tensor_tensor(out=tt[:, 0:n], in0=pso[:, 0:n], in1=bon[:, p, t0:t0 + n], op=ALU.add), reads=[psok, "bon"], writes=[tk])
                P.op("dve", lambda e: e.tensor_tensor(out=mixT[:, p, t0:t0 + n], in0=tt[:, 0:n], in1=gT[:, p, 0:n], op=ALU.mult), reads=[tk, "gT"], writes=["mixT"])
        barrier()
        P.dma("sp", hT, h_spill[:], reads=["h_spill"], writes=["hT"])
        dbg_out("ya%d" % l, mixT[:, 0:2, :], ["mixT"])


    def attn_layer(l, lam0):
        carver.reset(0, HB0)
        NKC = PAST // 128
        qT = carver.get([128, 4, NT], BF16)
        kT = carver.get([128, 4, NT], BF16)
        kcT = carver.get([128, 4, PAST], BF16)
        vS = carver.get([128, NKC + LS // 128, 4, 128], BF16)
        vP = carver.get([128, NPS * LP // 128, 4, 128], BF16)
        stg = [carver.get([128, 512]) for _ in range(2)]
        wv = carver.get([128, KC, 512], BF16)
        pTs = [carver.get([128, 512], BF16) for _ in range(3)]
        xb = [carver.get([128, 512], BF16) for _ in range(2)]
        of = carver.get([128, 512])
        P.dma("sp", plam[:], dlam[l], writes=["plam"])
        P.dma("sp", psub[:], dsub[l], writes=["psub"])
        P.dma("pool", kcT, ckT[l].rearrange("h p t -> p h t"), writes=["kcT"])
        P.dma("pool", vS[:, 0:NKC], cv[l], writes=["vS"])
        for i in range(2):
            P.op("dve", lambda e: e.tensor_tensor(out=lamt[:], in0=plam[:, 2 * i, :], in1=plam[:, 2 * i + 1, :], op=ALU.mult), reads=["plam"], writes=["lamt"])
            P.op("dve", lambda e: e.reduce_sum(out=lamv[:, i:i + 1], in_=lamt[:], axis=AX.X), reads=["lamt"], writes=["lamv"])
        P.op("act", lambda e: e.activation(out=lamv[:, 0:2], in_=lamv[:, 0:2], func=AF.Exp), reads=["lamv"], writes=["lamv"])
        P.op("dve", lambda e: e.tensor_tensor(out=lamv[:, 2:3], in0=lamv[:, 0:1], in1=lamv[:, 1:2], op=ALU.subtract), reads=["lamv"], writes=["lamv"])
        P.op("dve", lambda e: e.tensor_scalar(out=lamv[:, 3:4], in0=lamv[:, 2:3], scalar1=-1.0, scalar2=-lam0, op0=ALU.mult, op1=ALU.add), reads=["lamv"], writes=["lamv"])
        P.op("dve", lambda e: e.tensor_scalar(out=psub[:], in0=psub[:], scalar1=(1.0 - lam0), scalar2=None, op0=ALU.mult), reads=["psub"], writes=["psub"])

        def cbqk(ps, pk, ci, t0, n, cnd):
            dst = (qT if ci < 4 else kT)[:, ci % 4, t0:t0 + n]
            dk = "qT" if ci < 4 else "kT"
            if cnd == 1 or os.environ.get("NOROPE"):
                P.op("act", lambda e: e.copy(out=dst, in_=ps[:, 0:n]), reads=[pk], writes=[dk])
                return
            xbt, xbk = xb[ci % 2], "xb%d" % (ci % 2)
            P.op("act", lambda e: e.copy(out=xbt[:, 0:n], in_=ps[:, 0:n]), reads=[pk], writes=[xbk])
            pr, prk = psum()
            P.op("pe", lambda e: e.matmul(pr[:, 0:n], lhsT=C["ropeR"][:], rhs=xbt[:, 0:n], start=True, stop=True), reads=[xbk, "k_ropeR"], writes=[prk])
            t1, t1k = tmp()
            P.op("dve", lambda e: e.tensor_tensor(out=t1[:, 0:n], in0=xbt[:, 0:n], in1=C["cos"][:, t0:t0 + n], op=ALU.mult), reads=[xbk, "k_cos"], writes=[t1k])
            t2, t2k = tmp()
            P.op("dve", lambda e: e.tensor_tensor(out=t2[:, 0:n], in0=pr[:, 0:n], in1=C["sin"][:, t0:t0 + n], op=ALU.mult), reads=[prk, "k_sin"], writes=[t2k])
            P.op("dve", lambda e: e.tensor_tensor(out=dst, in0=t1[:, 0:n], in1=t2[:, 0:n], op=ALU.add), reads=[t1k, t2k], writes=[dk])
        STG = int(os.environ.get("ATT_STAGE", "9"))
        if STG < 1:
            return
        proj_fm(w_in[l], 1152, 8, hT, "hT", cbqk)
        if STG < 2:
            return

        def proj_tm(col0, tok_ranges, cb):
            P.dma("pool", wv, w_in[l][:, col0:col0 + 512].rearrange("(kc p) c -> p kc c", p=128), writes=["wv"])
            for (t0, info) in tok_ranges:
                ps, pk = psum()
                for c in range(KC):
                    P.op("pe", lambda e: e.matmul(ps[:, 0:512], lhsT=hT[:, c, t0:t0 + 128], rhs=wv[:, c, :], start=(c == 0), stop=(c == KC - 1)),
                         reads=["hT", "wv"], writes=[pk], pe_acc=(c > 0))
                cb(ps, pk, t0, info)
        prm_chunks = [(LS + i * 128, i) for i in range(NPS * LP // 128)]
        sam_chunks = [(i * 128, i) for i in range(LS // 128)]
        scnt = [0]

        def cbk(ps, pk, t0, i):
            sg, sgk = stg[scnt[0] % 2], "stg%d" % (scnt[0] % 2)
            scnt[0] += 1
            P.op("act", lambda e: e.copy(out=sg[:], in_=ps[:, 0:512]), reads=[pk], writes=[sgk])
            out_toks.append(P.dma("sp", kc_out[l, i * 128:(i + 1) * 128, :], sg[:], reads=[sgk], writes=["kc_out"]))
        proj_tm(1664, prm_chunks, cbk)

        def cbv_p(ps, pk, t0, i):
            sg, sgk = stg[scnt[0] % 2], "stg%d" % (scnt[0] % 2)
            scnt[0] += 1
            P.op("act", lambda e: e.copy(out=sg[:], in_=ps[:, 0:512]), reads=[pk], writes=[sgk])
            out_toks.append(P.dma("sp", vc_out[l, i * 128:(i + 1) * 128, :], sg[:], reads=[sgk], writes=["vc_out"]))
            P.op("dve", lambda e: e.tensor_copy(out=vP[:, i].rearrange("p h e -> p (h e)"), in_=sg[:]), reads=[sgk], writes=["vP"])

        def cbv_s(ps, pk, t0, i):
            P.op("act", lambda e: e.copy(out=vS[:, NKC + i].rearrange("p h e -> p (h e)"), in_=ps[:, 0:512]), reads=[pk], writes=["vS"])
        proj_tm(2176, prm_chunks, cbv_p)
        proj_tm_v = None
        for (t0, i) in sam_chunks:
            ps, pk = psum()
            for c in range(KC):
                P.op("pe", lambda e: e.matmul(ps[:, 0:512], lhsT=hT[:, c, t0:t0 + 128], rhs=wv[:, c, :], start=(c == 0), stop=(c == KC - 1)),
                     reads=["hT", "wv"], writes=[pk], pe_acc=(c > 0))
            cbv_s(ps, pk, t0, i)

        if STG < 3:
            return
        jobs = []
        for q0 in range(0, LS, 512):
            nq = min(512, LS - q0)
            ks = [("c", c) for c in range(NKC)] + [("s", c) for c in range(LS // 128)]
            jobs.append((q0, nq, ks))
        for i in range(NPS):
            s0 = LS + i * LP
            jobs.append((s0, LP, [("p", i * (LP // 128) + c) for c in range(LP // 128)]))
        pcnt = [0]
        for (q0, nq, ks) in jobs:
            for h in range(4):
                O = [psfix(0), psfix(1)]
                Z = [psfix(2), psfix(3)]
                for ki, (kind, c) in enumerate(ks):
                    if kind == "c":
                        kap = lambda m: kcT[m * 64:(m + 1) * 64, h, c * 128:(c + 1) * 128]
                        vap = vS[:, c, h, :]
                        kkey, vkey = "kcT", "vS"
                    elif kind == "s":
                        kap = lambda m: kT[m * 64:(m + 1) * 64, h, c * 128:(c + 1) * 128]
                        vap = vS[:, NKC + c, h, :]
                        kkey, vkey = "kT", "vS"
                    else:
                        kap = lambda m: kT[m * 64:(m + 1) * 64, h, LS + c * 128:LS + (c + 1) * 128]
                        vap = vP[:, c, h, :]
                        kkey, vkey = "kT", "vP"
                    for m in range(2):
                        ps, pk = psum()
                        P.op("pe", lambda e: e.matmul(ps[:, 0:nq], lhsT=kap(m), rhs=qT[m * 64:(m + 1) * 64, h, q0:q0 + nq], start=True, stop=True),
                             reads=[kkey, "qT"], writes=[pk])
                        pT, pTk = pTs[pcnt[0] % 3], "pT%d" % (pcnt[0] % 3)
                        pcnt[0] += 1
                        P.op("act", lambda e: e.activation(out=pT[:, 0:nq], in_=ps[:, 0:nq], func=AF.Exp, scale=0.125), reads=[pk], writes=[pTk])
                        P.op("pe", lambda e: e.matmul(O[m][0][:, 0:nq], lhsT=vap, rhs=pT[:, 0:nq], start=(ki == 0), stop=(ki == len(ks) - 1)),
                             reads=[vkey, pTk], writes=[O[m][1]], pe_acc=(ki > 0))
                        P.op("pe", lambda e: e.matmul(Z[m][0][:, 0:nq], lhsT=C["ones_b"][:], rhs=pT[:, 0:nq], start=(ki == 0), stop=(ki == len(ks) - 1)),
                             reads=["k_ones_b", pTk], writes=[Z[m][1]], pe_acc=(ki > 0))
                o_m = []
                for m in range(2):
                    rz, rzk = tmp()
                    P.op("dve", lambda e: e.reciprocal(out=rz[:, 0:nq], in_=Z[m][0][:, 0:nq]), reads=[Z[m][1]], writes=[rzk])
                    P.op("dve", lambda e: e.tensor_tensor(out=rz[:, 0:nq], in0=O[m][0][:, 0:nq], in1=rz[:, 0:nq], op=ALU.mult), reads=[O[m][1], rzk], writes=[rzk])
                    o_m.append((rz, rzk))
                P.op("dve", lambda e: e.scalar_tensor_tensor(out=of[:, 0:nq], in0=o_m[1][0][:, 0:nq], scalar=lamv[:, 3:4], in1=o_m[0][0][:, 0:nq],
                                                             op0=ALU.mult, op1=ALU.add), reads=[o_m[0][1], o_m[1][1], "lamv"], writes=["of"])
                rms_stats(lambda c: of[:, 0:nq], ["of"], q0, nq, rstd, "rstd", nchunks=1, scale=1.0 / 128)
                tt, tk = tmp()
                P.op("dve", lambda e: e.tensor_tensor(out=tt[:, 0:nq], in0=of[:, 0:nq], in1=rstd[:, 0:nq], op=ALU.mult), reads=["of", "rstd"], writes=[tk])
                P.op("act", lambda e: e.activation(out=mixT[:, 2 + h, q0:q0 + nq], in_=tt[:, 0:nq], func=AF.Identity, scale=psub[:, 0:1]),
                     reads=[tk, "psub"], writes=["mixT"])
        dbg_out("yb%d" % l, mixT[:, 2:6, :], ["mixT"])

    def hyena_layer(l):
        carver.reset(0, HB0)
        uh = carver.get([128, 6, NT], BF16)

        def cbC(ps, pk, ci, t0, n, cnd):
            P.op("act", lambda e: e.copy(out=uh[:, ci, t0:t0 + n], in_=ps[:, 0:n]), reads=[pk], writes=["uh"])
        proj_fm(w_in[l], 2688, 6, hT, "hT", cbC)
        barrier()
        carver.lim = MB0
        hw3b = carver.get([64, 1024], BF16)
        ndecb = carver.get([128, 256])
        for (t, src) in ((hconv, hy_conv), (hw1, hy_w1), (hvec, hy_vec), (hw2, hy_w2), (hbias, hy_bias)):
            P.dma("sp", t[:], src[l], writes=["hyp"])
        P.dma("sp", ndecb, hy_decb[l], writes=["ndecb"])
        P.dma("pool", hw3b, hy_w3[l], writes=["hyp"])
        EV = carver.get([128, 2, 1024])
        P.op("dve", lambda e: e.tensor_scalar(out=EV[:, 0, 0:256], in0=ndecb, scalar1=-1.0, scalar2=None, op0=ALU.mult), reads=["ndecb"], writes=["EV"])
        P.op("dve", lambda e: e.tensor_tensor(out=ndecb, in0=ndecb, in1=EV[:, 0, 0:256], op=ALU.min), reads=["ndecb", "EV"], writes=["ndecb"])
        for (col, bi, fac) in ((0, None, 0.5), (1, None, 0.25), (2, 0, 0.5), (3, 0, 0.25), (4, 2, 0.5), (5, 2, 0.25)):
            if bi is None:
                P.op("dve", lambda e: e.tensor_scalar(out=hsc[:, col:col + 1], in0=hvec[:, 1:2], scalar1=fac, scalar2=None, op0=ALU.mult), reads=["hyp"], writes=["hsc"])
            else:
                P.op("dve", lambda e: e.scalar_tensor_tensor(out=hsc[:, col:col + 1], in0=hvec[:, 1:2], scalar=fac, in1=hvec[:, bi:bi + 1], op0=ALU.mult, op1=ALU.mult),
                     reads=["hyp"], writes=["hsc"])
        XS = carver.get([128, 3 * LS])
        tmpc = XS[:, 0:LS]
        embT = XS[0:33, LS:2 * LS]
        h1 = XS[0:64, 2 * LS:3 * LS]
        for c in range(6):
            for (s0, L, cnd) in cfg.seqs:
                P.op("act", lambda e: e.activation(out=tmpc[:, 0:L], in_=uh[:, c, s0:s0 + L], func=AF.Identity, scale=hconv[:, c, 1:2], bias=hconv[:, c, 3:4]),
                     reads=["uh", "hyp"], writes=["XS"])
                P.op("dve", lambda e: e.scalar_tensor_tensor(out=tmpc[:, 1:L], in0=uh[:, c, s0:s0 + L - 1], scalar=hconv[:, c, 0:1], in1=tmpc[:, 1:L], op0=ALU.mult, op1=ALU.add),
                     reads=["uh", "hyp", "XS"], writes=["XS"])
                P.op("dve", lambda e: e.scalar_tensor_tensor(out=tmpc[:, 0:L - 1], in0=uh[:, c, s0 + 1:s0 + L], scalar=hconv[:, c, 2:3], in1=tmpc[:, 0:L - 1], op0=ALU.mult, op1=ALU.add),
                     reads=["uh", "hyp", "XS"], writes=["XS"])
                P.op("act", lambda e: e.copy(out=uh[:, c, s0:s0 + L], in_=tmpc[:, 0:L]), reads=["XS"], writes=["uh"])
        z1 = carver.get([128, 2, NT], BF16)
        h2 = carver.get([64, LS], BF16)
        SCM = LS // 128
        FCM = (LS + 1 + 127) // 128
        Zfull = carver.get([128, max(SCM * 768, (LP // 128) * (512 + NPS * 256))], BF16)
        FYM = max(FCM, NPS * ((LP + 1 + 127) // 128))
        YRf = carver.get([128, FYM * 256], BF16)
        YIf = carver.get([128, FYM * 256], BF16)
        Hc = carver.get([128, 2, 256])
        Eexp = carver.get([128, 256])
        rsb = carver.get([128, 512])
        sqt = carver.get([128, 512], BF16)
        tlT = carver.get([128, SCM])
        tabs = XS.bitcast(BF16)
        TBN = (3 * LS * 2) // 4
        tabv = [tabs[:, i * TBN:(i + 1) * TBN] for i in range(4)]
        for (nm, L, s0, ns) in (("s", LS, 0, 1), ("p", LP, LS, NPS)):
            SC = L // 128
            FC = (L + 1 + 127) // 128
            ncols = 512 + ns * 256
            Z = Zfull[:, 0:SC * ncols].rearrange("p (s c) -> p s c", s=SC)
            YR = [YRf[:, j * FC * 256:(j + 1) * FC * 256].rearrange("p (f c) -> p f c", f=FC) for j in range(ns)]
            YI = [YIf[:, j * FC * 256:(j + 1) * FC * 256].rearrange("p (f c) -> p f c", f=FC) for j in range(ns)]
            barrier()
            P.dma("sp", embT[:, 0:L], din["c_embT_" + nm][:], writes=["XS"])
            P.dma("sp", tlT[:, 0:SC], din["c_tlinT_" + nm][:], writes=["tlT"])
            tls = [(a, min(512, L - a)) for a in range(0, L, 512)]

            def sin_layer(w_t, kdim, src, srck, dst, dstk, c0):
                for (a, n) in tls:
                    ps, pk = psum()
                    P.op("pe", lambda e: e.matmul(ps[0:64, 0:n], lhsT=w_t, rhs=src[0:kdim, a:a + n], start=True, stop=True), reads=["hyp", srck], writes=[pk])
                    s2, s2k = tmp()
                    s4, s4k = tmp()
                    P.op("act", lambda e: e.activation(out=s2[0:64, 0:n], in_=ps[0:64, 0:n], func=AF.Sin, scale=hsc[:, 0:1], bias=hsc[:, c0:c0 + 1]), reads=[pk, "hsc"], writes=[s2k])
                    P.op("act", lambda e: e.activation(out=s4[0:64, 0:n], in_=ps[0:64, 0:n], func=AF.Sin, scale=hsc[:, 1:2], bias=hsc[:, c0 + 1:c0 + 2]), reads=[pk, "hsc"], writes=[s4k])
                    P.op("dve", lambda e: e.tensor_tensor(out=s4[0:64, 0:n], in0=s4[0:64, 0:n], in1=s4[0:64, 0:n], op=ALU.mult), reads=[s4k], writes=[s4k])
                    P.op("dve", lambda e: e.tensor_scalar(out=s4[0:64, 0:n], in0=s4[0:64, 0:n], scalar1=-2.0, scalar2=1.0, op0=ALU.mult, op1=ALU.add), reads=[s4k], writes=[s4k])
                    P.op("dve", lambda e: e.scalar_tensor_tensor(out=dst[:, a:a + n], in0=s2[0:64, 0:n], scalar=2.0, in1=s4[0:64, 0:n], op0=ALU.mult, op1=ALU.mult),
                         reads=[s2k, s4k], writes=[dstk])
            sin_layer(hw1[:], 33, embT, "XS", h1, "XS", 2)
            sin_layer(hw2[:], 64, h1, "XS", h2, "h2", 4)
            barrier()
            fwdC, fwdS, invC, invS = (din["c_fwdC_" + nm], din["c_fwdS_" + nm], din["c_invC_" + nm], din["c_invS_" + nm])
            for o in range(2):
                psS, psSk = psfix(0)
                for sc in range(SC):
                    ps, pk = psum()
                    P.op("pe", lambda e: e.matmul(ps[:, 0:512], lhsT=h2[:, sc * 128:(sc + 1) * 128], rhs=hw3b[:, o * 512:(o + 1) * 512], start=True, stop=True),
                         reads=["h2", "hyp"], writes=[pk])
                    P.op("act", lambda e: e.activation(out=Eexp, in_=ndecb, func=AF.Exp, scale=tlT[:, sc:sc + 1]), reads=["ndecb", "tlT"], writes=["Eexp"])
                    P.op("dve", lambda e: e.tensor_tensor(out=Z[:, sc, 0:512].rearrange("p (d c) -> p d c", d=2), in0=ps[:, 0:512].rearrange("p (d c) -> p d c", d=2),
                                                          in1=mkap(Eexp[:, 0:1], 0, [[0, 2], [1, 256]]), op=ALU.mult), reads=[pk, "Eexp"], writes=["Zf"])
                    if sc == 0:
                        P.op("dve", lambda e: e.memset(Z[0:1, 0, 256:512], 0.0), reads=["Zf"], writes=["Zf"])
                    P.op("act", lambda e: e.activation(out=sqt, in_=Z[:, sc, 0:512], func=AF.Square), reads=["Zf"], writes=["sqt"])
                    P.op("pe", lambda e: e.matmul(psS[:, 0:512], lhsT=C["ones_b"][:], rhs=sqt, start=(sc == 0), stop=(sc == SC - 1)),
                         reads=["sqt", "k_ones_b"], writes=[psSk], pe_acc=(sc > 0))
                P.op("act", lambda e: e.copy(out=rsb, in_=psS[:, 0:512]), reads=[psSk], writes=["rsb"])
                P.op("dve", lambda e: e.tensor_tensor(out=rsb[:, 0:256], in0=rsb[:, 0:256], in1=rsb[:, 256:512], op=ALU.add), reads=["rsb"], writes=["rsb"])
                P.op("act", lambda e: e.activation(out=rsb[:, 0:256], in_=rsb[:, 0:256], func=AF.Ln, bias=epsc(1e-6)), reads=["rsb", "epsc"], writes=["rsb"])
                P.op("act", lambda e: e.activation(out=rsb[:, 0:256], in_=rsb[:, 0:256], func=AF.Exp, scale=-0.5), reads=["rsb"], writes=["rsb"])
                for sc in range(SC):
                    P.op("dve", lambda e: e.tensor_tensor(out=Z[:, sc, 0:512].rearrange("p (d c) -> p d c", d=2), in0=Z[:, sc, 0:512].rearrange("p (d c) -> p d c", d=2),
                                                          in1=mkap(rsb[:, 0:1], 0, [[0, 2], [1, 256]]), op=ALU.mult), reads=["Zf", "rsb"], writes=["Zf"])
                zsrc_all, zk = (uh[:, 4:6, :], "uh") if o == 0 else (z1, "z1")
                for j in range(ns):
                    for sc in range(SC):
                        for c in range(2):
                            t0 = s0 + j * L + sc * 128
                            ps, pk = psum()
                            pst = ps[:, 0:64].bitcast(BF16)
                            P.op("pe", lambda e: e.transpose(pst, zsrc_all[:, c, t0:t0 + 128], C["ident_b"][:]), reads=[zk, "k_ident_b"], writes=[pk])
                            P.op("act", lambda e: e.copy(out=Z[:, sc, 512 + j * 256 + c * 128:512 + j * 256 + (c + 1) * 128], in_=pst), reads=[pk], writes=["Zd"])
                blocks = [(cb, min(512, ncols - cb)) for cb in range(0, ncols, 512)]
                for fc in range(FC):
                    tC, tS = tabv[(fc % 2) * 2], tabv[(fc % 2) * 2 + 1]
                    tCk, tSk = "tab%d" % ((fc % 2) * 2), "tab%d" % ((fc % 2) * 2 + 1)
                    tCv = tC[:, 0:SC * 128].rearrange("p (s f) -> p s f", s=SC)
                    tSv = tS[:, 0:SC * 128].rearrange("p (s f) -> p s f", s=SC)
                    P.dma("sp", tCv, fwdC[fc], writes=[tCk])
                    P.dma("act", tSv, fwdS[fc], writes=[tSk])
                    for sc in range(SC):
                        for ri, (tv, tk_) in enumerate(((tCv, tCk), (tSv, tSk))):
                            for bi_, (cb, w) in enumerate(blocks):
                                bank, bk = psfix(ri * 2 + bi_)
                                P.op("pe", lambda e: e.matmul(bank[:, 0:w], lhsT=tv[:, sc, :], rhs=Z[:, sc, cb:cb + w], start=(sc == 0), stop=(sc == SC - 1)),
                                     reads=[tk_, "Zf", "Zd"], writes=[bk], pe_acc=(sc > 0))
                    for ri in range(2):
                        for bi_, (cb, w) in enumerate(blocks):
                            bank, bk = psfix(ri * 2 + bi_)
                            P.op("act", lambda e: e.copy(out=EV[:, ri, cb:cb + w], in_=bank[:, 0:w]), reads=[bk], writes=["EV"])
                    P.op("dve", lambda e: e.tensor_tensor(out=Hc[:, 0, :], in0=EV[:, 0, 0:256], in1=EV[:, 0, 256:512], op=ALU.add), reads=["EV"], writes=["Hc"])
                    P.op("dve", lambda e: e.tensor_tensor(out=Hc[:, 1, :], in0=EV[:, 1, 0:256], in1=EV[:, 1, 256:512], op=ALU.subtract), reads=["EV"], writes=["Hc"])
                    for j in range(ns):
                        vre = EV[:, 0, 512 + j * 256:512 + (j + 1) * 256]
                        vim = EV[:, 1, 512 + j * 256:512 + (j + 1) * 256]
                        t1, t1k = tmp()
                        t2, t2k = tmp()
                        P.op("dve", lambda e: e.tensor_tensor(out=t1[:, 0:256], in0=vre, in1=Hc[:, 0, :], op=ALU.mult), reads=["EV", "Hc"], writes=[t1k])
                        P.op("dve", lambda e: e.tensor_tensor(out=t2[:, 0:256], in0=vim, in1=Hc[:, 1, :], op=ALU.mult), reads=["EV", "Hc"], writes=[t2k])
                        P.op("dve", lambda e: e.tensor_tensor(out=YR[j][:, fc, :], in0=t1[:, 0:256], in1=t2[:, 0:256], op=ALU.subtract), reads=[t1k, t2k], writes=["YR"])
                        P.op("dve", lambda e: e.tensor_tensor(out=t1[:, 256:512], in0=vre, in1=Hc[:, 1, :], op=ALU.mult), reads=["EV", "Hc"], writes=[t1k])
                        P.op("dve", lambda e: e.tensor_tensor(out=t2[:, 256:512], in0=vim, in1=Hc[:, 0, :], op=ALU.mult), reads=["EV", "Hc"], writes=[t2k])
                        P.op("dve", lambda e: e.tensor_tensor(out=YI[j][:, fc, :], in0=t1[:, 256:512], in1=t2[:, 256:512], op=ALU.add), reads=[t1k, t2k], writes=["YI"])
                ttl = [(a, min(512, L - a)) for a in range(0, L, 512)]
                accs = {}
                bi2 = 0
                for j in range(ns):
                    for c in range(2):
                        for ti in range(len(ttl)):
                            accs[(j, c, ti)] = (PS[bi2], "ps%d" % bi2)
                            bi2 += 1
                assert bi2 <= 8
                for fc in range(FC):
                    tC, tS = tabv[(fc % 2) * 2], tabv[(fc % 2) * 2 + 1]
                    tCk, tSk = "tab%d" % ((fc % 2) * 2), "tab%d" % ((fc % 2) * 2 + 1)
                    P.dma("sp", tC[:, 0:L], invC[fc], writes=[tCk])
                    P.dma("act", tS[:, 0:L], invS[fc], writes=[tSk])
                    for j in range(ns):
                        for c in range(2):
                            for ti, (a, w) in enumerate(ttl):
                                acc, acck = accs[(j, c, ti)]
                                P.op("pe", lambda e: e.matmul(acc[:, 0:w], lhsT=YR[j][:, fc, c * 128:(c + 1) * 128], rhs=tC[:, a:a + w], start=(fc == 0), stop=False),
                                     reads=["YR", tCk], writes=[acck], pe_acc=(fc > 0))
                                P.op("pe", lambda e: e.matmul(acc[:, 0:w], lhsT=YI[j][:, fc, c * 128:(c + 1) * 128], rhs=tS[:, a:a + w], start=False, stop=(fc == FC - 1)),
                                     reads=["YI", tSk], writes=[acck], pe_acc=True)
                for j in range(ns):
                    for c in range(2):
                        for ti, (a, w) in enumerate(ttl):
                            acc, acck = accs[(j, c, ti)]
                            g0 = s0 + j * L + a
                            zs = zsrc_all[:, c, g0:g0 + w]
                            xg = uh[:, (0 if o == 0 else 2) + c, g0:g0 + w]
                            tt, tk = tmp()
                            P.op("dve", lambda e: e.scalar_tensor_tensor(out=tt[:, 0:w], in0=zs, scalar=hbias[:, o, c:c + 1], in1=acc[:, 0:w], op0=ALU.mult, op1=ALU.add),
                                 reads=[zk, "hyp", acck], writes=[tk])
                            dstz, dk = (z1[:, c, g0:g0 + w], "z1") if o == 0 else (mixT[:, 6 + c, g0:g0 + w], "mixT")
                            P.op("dve", lambda e: e.tensor_tensor(out=dstz, in0=tt[:, 0:w], in1=xg, op=ALU.mult), reads=[tk, "uh"], writes=[dk])
                barrier()
        dbg_out("yc%d" % l, mixT[:, 6:8, :], ["mixT"])

    for l in range(DEPTH):
        lam0 = lam_init_of(l)
        P.dma("sp", bmod[:], b_modT[l], writes=["bmod"])
        P.dma("sp", gn[:], gains[l], writes=["gains"])
        for j in range(48):
            wm, wk = load_w(w_mod[l][:, j * 128:(j + 1) * 128], KC)
            ps, pk = psum()
            for c in range(KC):
                P.op("pe", lambda e, wm=wm, c=c, ps=ps: e.matmul(ps[:, 0:2], lhsT=wm[:, c, :], rhs=scv[:, c, :],
                                                                start=(c == 0), stop=(c == KC - 1)),
                     reads=[wk, "scvec"], writes=[pk], pe_acc=(c > 0))
            P.op("dve", lambda e, ps=ps, j=j: e.tensor_scalar(out=modT[:, j, :], in0=ps[:, 0:2], scalar1=bmod[:, j:j + 1],
                                                              scalar2=None, op0=ALU.add),
                 reads=[pk, "bmod"], writes=["modT"])
        for (o, jsc, gi) in ((0, 8, 0), (2, 32, 2)):
            for cnd in range(2):
                P.op("dve", lambda e, o=o, jsc=jsc, gi=gi, cnd=cnd: e.scalar_tensor_tensor(
                    out=sc_all[:, o, :, cnd], in0=modT[:, jsc:jsc + 8, cnd], scalar=1.0, in1=gn[:, gi, :],
                    op0=ALU.add, op1=ALU.mult), reads=["modT", "gains"], writes=["sc_all"])
        for (o, jgt, gi) in ((1, 16, 1), (3, 40, 3)):
            for cnd in range(2):
                P.op("dve", lambda e, o=o, jgt=jgt, gi=gi, cnd=cnd: e.tensor_tensor(
                    out=sc_all[:, o, :, cnd], in0=modT[:, jgt:jgt + 8, cnd], in1=gn[:, gi, :], op=ALU.mult),
                    reads=["modT", "gains"], writes=["sc_all"])

        def norm_mod(src_tile_fn, src_keys, dst, dst_key_fn, sci, shj):
            for (t0, n, cnd) in cfg.tiles:
                rms_stats(lambda c: src_tile_fn(c, t0, n), src_keys, t0, n, rstd, "rstd")
                for c in range(KC):
                    tt, tk = tmp()
                    P.op("dve", lambda e, c=c, tt=tt: e.tensor_tensor(out=tt[:, 0:n], in0=src_tile_fn(c, t0, n), in1=rstd[:, 0:n],
                                                                     op=ALU.mult), reads=src_keys + ["rstd"], writes=[tk])
                    P.op("act", lambda e, c=c, tt=tt: e.activation(out=dst[:, c, t0:t0 + n], in_=tt[:, 0:n], func=AF.Identity,
                                                                  scale=sc_all[:, sci, c, cnd:cnd + 1],
                                                                  bias=modT[:, shj + c, cnd:cnd + 1]),
                         reads=[tk, "sc_all", "modT"], writes=[dst_key_fn(c, t0)])

        norm_mod(lambda c, t0, n: xT[:, c, t0:t0 + n], ["xT"], hT, lambda c, t0: "hT", 0, 0)
        if l == 0:
            dbg_out("h0", hT, ["hT"])
        P.dma("sp", x_spill[:], xT, reads=["xT"], writes=["x_spill"])
        barrier()

        def proj_fm(wsrc, col0, ncols_chunks, rhs, rhs_key, cb, kchunks=KC):
            for ci in range(ncols_chunks):
                wt, wk = load_w(wsrc[:, col0 + ci * 128: col0 + (ci + 1) * 128], kchunks)
                for (t0, n, cnd) in cfg.tiles:
                    ps, pk = psum()
                    for c in range(kchunks):
                        P.op("pe", lambda e, wt=wt, c=c, ps=ps, t0=t0, n=n: e.matmul(
                            ps[:, 0:n], lhsT=wt[:, c, :], rhs=rhs[:, c, t0:t0 + n], start=(c == 0), stop=(c == kchunks - 1)),
                            reads=[wk, rhs_key], writes=[pk], pe_acc=(c > 0))
                    cb(ps, pk, ci, t0, n, cnd)

        carver.reset(0, HB0)
        rkv = lor = None
        if "rwkv" in mixers:
            rkv = carver.get([128, 6, NT], BF16)
            lor = carver.get([128, 3, NT], BF16)

        def cbA(ps, pk, ci, t0, n, cnd):
            if ci < 6:
                P.op("act", lambda e: e.copy(out=rkv[:, ci, t0:t0 + n], in_=ps[:, 0:n]), reads=[pk], writes=["rkv"])
            else:
                fn = {6: AF.Tanh, 7: AF.Identity, 8: AF.Sigmoid}[ci]
                P.op("act", lambda e: e.activation(out=lor[:, ci - 6, t0:t0 + n], in_=ps[:, 0:n], func=fn),
                     reads=[pk], writes=["lor"])
        if "rwkv" in mixers:
            proj_fm(w_in[l], 0, 9, hT, "hT", cbA)
            rwkv_layer(l, rkv, lor)
        else:
            P.op("pool", lambda e: e.memset(mixT[:, 0:2, :], 0.0), writes=["mixT"])

        if "attn" in mixers:
            attn_layer(l, lam0)
        else:
            P.op("pool", lambda e: e.memset(mixT[:, 2:6, :], 0.0), writes=["mixT"])

        if "hyena" in mixers:
            hyena_layer(l)
        else:
            P.op("pool", lambda e: e.memset(mixT[:, 6:8, :], 0.0), writes=["mixT"])

        barrier()
        P.dma("sp", xT, x_spill[:], reads=["x_spill"], writes=["xT"])
        carver.reset(XB, MB0)
        mo = carver.get([128, KC, 512])
        for (t0, n, cnd) in cfg.tiles:
            for ci in range(KC):
                wt, wk = load_w(w_out[l][:, ci * 128:(ci + 1) * 128], KC)
                ps, pk = psum()
                for c in range(KC):
                    P.op("pe", lambda e, wt=wt, c=c, ps=ps: e.matmul(ps[:, 0:n], lhsT=wt[:, c, :], rhs=mixT[:, c, t0:t0 + n],
                                                                    start=(c == 0), stop=(c == KC - 1)),
                         reads=[wk, "mixT"], writes=[pk], pe_acc=(c > 0))
                P.op("act", lambda e, ci=ci, ps=ps: e.copy(out=mo[:, ci, 0:n], in_=ps[:, 0:n]), reads=[pk], writes=["mo"])
            rms_stats(lambda c: mo[:, c, 0:n], ["mo"], t0, n, rstd, "rstd")
            for c in range(KC):
                tt, tk = tmp()
                P.op("dve", lambda e, c=c, tt=tt: e.tensor_tensor(out=tt[:, 0:n], in0=mo[:, c, 0:n], in1=rstd[:, 0:n], op=ALU.mult),
                     reads=["mo", "rstd"], writes=[tk])
                P.op("dve", lambda e, c=c, tt=tt: e.scalar_tensor_tensor(
                    out=xT[:, c, t0:t0 + n], in0=tt[:, 0:n], scalar=sc_all[:, 1, c, cnd:cnd + 1], in1=xT[:, c, t0:t0 + n],
                    op0=ALU.mult, op1=ALU.add), reads=[tk, "sc_all", "xT"], writes=["xT"])
        if l == 0:
            dbg_out("x1", xT, ["xT"])

        barrier()
        carver.reset(XB, BIGB)
        h2 = carver.get([128, KC, 512], BF16)
        f1 = carver.get([128, 32, 512], BF16)
        fo = carver.get([128, KC, 512])
        wff2 = [carver.get([128, 32, 128], BF16) for _ in range(2)]
        for (t0, n, cnd) in cfg.tiles:
            rms_stats(lambda c: xT[:, c, t0:t0 + n], ["xT"], t0, n, rstd, "rstd")
            for c in range(KC):
                tt, tk = tmp()
                P.op("dve", lambda e, c=c, tt=tt: e.tensor_tensor(out=tt[:, 0:n], in0=xT[:, c, t0:t0 + n], in1=rstd[:, 0:n],
                                                                 op=ALU.mult), reads=["xT", "rstd"], writes=[tk])
                P.op("act", lambda e, c=c, tt=tt: e.activation(out=h2[:, c, 0:n], in_=tt[:, 0:n], func=AF.Identity,
                                                              scale=sc_all[:, 2, c, cnd:cnd + 1], bias=modT[:, 24 + c, cnd:cnd + 1]),
                     reads=[tk, "sc_all", "modT"], writes=["h2"])
            for ci in range(32):
                wt, wk = load_w(w_ff1[l][:, ci * 128:(ci + 1) * 128], KC)
                ps, pk = psum()
                for c in range(KC):
                    P.op("pe", lambda e, wt=wt, c=c, ps=ps: e.matmul(ps[:, 0:n], lhsT=wt[:, c, :], rhs=h2[:, c, 0:n],
                                                                    start=(c == 0), stop=(c == KC - 1)),
                         reads=[wk, "h2"], writes=[pk], pe_acc=(c > 0))
                tt, tk = tmp()
                P.op("act", lambda e, ps=ps, tt=tt: e.activation(out=tt[:, 0:n], in_=ps[:, 0:n], func=AF.Relu), reads=[pk], writes=[tk])
                P.op("dve", lambda e, ci=ci, tt=tt: e.tensor_tensor(out=f1[:, ci, 0:n], in0=tt[:, 0:n], in1=tt[:, 0:n], op=ALU.mult),
                     reads=[tk], writes=["f1"])
            for ci in range(KC):
                wt, wk = wff2[ci % 2], "wff2_%d" % (ci % 2)
                load_w(w_ff2[l][:, ci * 128:(ci + 1) * 128], 32, dst=wt, key=wk)
                ps, pk = psum()
                for c in range(32):
                    P.op("pe", lambda e, wt=wt, c=c, ps=ps: e.matmul(ps[:, 0:n], lhsT=wt[:, c, :], rhs=f1[:, c, 0:n],
                                                                    start=(c == 0), stop=(c == 31)),
                         reads=[wk, "f1"], writes=[pk], pe_acc=(c > 0))
                P.op("act", lambda e, ci=ci, ps=ps: e.copy(out=fo[:, ci, 0:n], in_=ps[:, 0:n]), reads=[pk], writes=["fo"])
            rms_stats(lambda c: fo[:, c, 0:n], ["fo"], t0, n, rstd, "rstd")
            for c in range(KC):
                tt, tk = tmp()
                P.op("dve", lambda e, c=c, tt=tt: e.tensor_tensor(out=tt[:, 0:n], in0=fo[:, c, 0:n], in1=rstd[:, 0:n], op=ALU.mult),
                     reads=["fo", "rstd"], writes=[tk])
                P.op("dve", lambda e, c=c, tt=tt: e.scalar_tensor_tensor(
                    out=xT[:, c, t0:t0 + n], in0=tt[:, 0:n], scalar=sc_all[:, 3, c, cnd:cnd + 1], in1=xT[:, c, t0:t0 + n],
                    op0=ALU.mult, op1=ALU.add), reads=[tk, "sc_all", "xT"], writes=["xT"])

    out_toks.append(P.dma("sp", yT_out[:], xT, reads=["xT"], writes=["yT_out"]))
    P.finish_wait("sp", out_toks)
    P.emit(st)
    st.close()
    return nc


def _lay(a):
    return np.ascontiguousarray(a)


def prep_inputs(cfg, inp, consts):
    D = cfg.depth
    f = np.float32
    shared = {}
    for k, v in consts.items():
        shared["c_" + k] = v
    shared["w_mod"] = _lay(inp["w_mod"][:D])
    shared["b_modT"] = _lay(inp["b_mod"][:D].reshape(D, 48, 128).transpose(0, 2, 1))
    g4 = np.stack([inp["g_mix_pre"][:D], inp["g_mix_post"][:D], inp["g_ffn_pre"][:D], inp["g_ffn_post"][:D]], axis=1)
    shared["gains"] = _lay(g4.reshape(D, 4, 8, 128).transpose(0, 3, 1, 2))
    for k in ("w_in", "w_out", "w_ff1", "w_ff2"):
        shared[k] = _lay(inp[k][:D])
    shared["rw_conv"] = _lay(inp["rwkv_conv"][:D].reshape(D, 3, 6, 128).transpose(0, 3, 2, 1))
    shared["rw_w0"] = _lay(inp["rwkv_w0"][:D].reshape(D, 2, 2, 128).transpose(0, 3, 1, 2))
    shared["rw_a0"] = _lay(inp["rwkv_a0"][:D].reshape(D, 2, 2, 128).transpose(0, 3, 1, 2))
    shared["rw_w2"] = _lay(inp["rwkv_w2"][:D].reshape(D, 128, 256))
    shared["rw_a2"] = _lay(inp["rwkv_a2"][:D].reshape(D, 128, 256))
    shared["rw_g2"] = _lay(inp["rwkv_g2"][:D])
    v3 = np.stack([inp["rwkv_kk"][:D], inp["rwkv_ka"][:D], inp["rwkv_rk"][:D].reshape(D, 256)], axis=1)
    shared["rw_vec"] = _lay(v3.reshape(D, 3, 2, 128).transpose(0, 3, 1, 2))
    ln = np.stack([inp["rwkv_ln_w"][:D], inp["rwkv_ln_b"][:D]], axis=1)
    shared["rw_ln"] = _lay(ln.reshape(D, 2, 4, 64).transpose(0, 3, 1, 2))
    lam = np.stack([inp["diff_lq1"][:D], inp["diff_lk1"][:D], inp["diff_lq2"][:D], inp["diff_lk2"][:D]], axis=1)
    shared["dlam"] = _lay(np.broadcast_to(lam[:, None], (D, 128, 4, 64)))
    shared["dsub"] = _lay(inp["diff_subln"][:D].reshape(D, 128, 1))
    hc = np.concatenate([inp["hy_conv_w"][:D], inp["hy_conv_b"][:D][:, None]], axis=1)
    shared["hy_conv"] = _lay(hc.reshape(D, 4, 6, 128).transpose(0, 3, 2, 1))
    shared["hy_w1"] = _lay(inp["hy_w1"][:D])
    shared["hy_vec"] = _lay(np.stack([inp["hy_b1"][:D], inp["hy_freq"][:D], inp["hy_b2"][:D]], axis=2))
    shared["hy_w2"] = _lay(inp["hy_w2"][:D])
    shared["hy_w3"] = _lay(inp["hy_w3"][:D])
    shared["hy_decb"] = _lay(np.broadcast_to(inp["hy_decay"][:D][:, None, :], (D, 128, 256)))
    shared["hy_bias"] = _lay(inp["hy_bias"][:D].reshape(D, 2, 2, 128).transpose(0, 3, 1, 2))
    maps = []
    nb_s = inp["x_sample"].shape[0]
    for core in range(8):
        b = core % nb_s
        m = dict(shared)
        xs = inp["x_sample"][b]
        xp = inp["x_prompt"][cfg.nps * core:cfg.nps * (core + 1)].reshape(-1, 1024)
        x = np.concatenate([xs, xp], axis=0)
        m["xT"] = _lay(x.T.reshape(8, 128, cfg.nt).transpose(1, 0, 2))
        cv2 = np.stack([inp["c"][b], inp["c_ctx"]], axis=1)
        m["cvecT"] = _lay(cv2.reshape(8, 128, 2).transpose(1, 0, 2))
        st0 = inp["state_rwkv"][b][:D]
        m["st_in"] = _lay(st0.reshape(D, 2, 2, 2, 64, 64).transpose(0, 3, 5, 2, 1, 4).reshape(D, 128, 4, 64))
        ck = inp["cache_k"][b][:D]
        m["ckT"] = _lay(ck.reshape(D, cfg.past, 4, 128).transpose(0, 2, 3, 1))
        cvv = inp["cache_v"][b][:D]
        m["cv"] = _lay(cvv.reshape(D, cfg.past // 128, 128, 4, 128).transpose(0, 2, 1, 3, 4))
        maps.append({k: np.ascontiguousarray(v) for k, v in m.items()})
    return maps


_CACHE = {}


def kernel(**inputs):
    inp = {k: np.asarray(v) for k, v in inputs.items()}
    cfg = Cfg()
    consts = host_consts(cfg)
    if "nc" not in _CACHE:
        _CACHE["nc"] = build_program(cfg, consts, mixers=("rwkv", "attn", "hyena"))
    nc = _CACHE["nc"]
    maps = prep_inputs(cfg, inp, consts)
    res = run_bass_kernel_spmd(nc, maps, core_ids=list(range(8)))
    R = res.results
    D = cfg.depth
    B = inp["x_prompt"].shape[0]
    y_prompt = np.zeros((B, cfg.lp, 1024), np.float32)
    y_sample = np.zeros((inp["x_sample"].shape[0], cfg.ls, 1024), np.float32)
    new_state = np.zeros((B, D, 2, 4, 64, 64), np.float32)
    new_k = np.zeros((B, D, cfg.lp, 4, 2, 64), np.float32)
    new_v = np.zeros((B, D, cfg.lp, 4, 128), np.float32)
    for core in range(8):
        yT = np.asarray(R[core]["yT"])
        y = yT.transpose(1, 0, 2).reshape(1024, cfg.nt).T
        if core < y_sample.shape[0]:
            y_sample[core] = y[:cfg.ls]
        for i in range(cfg.nps):
            bp = cfg.nps * core + i
            y_prompt[bp] = y[cfg.ls + i * cfg.lp: cfg.ls + (i + 1) * cfg.lp]
            new_state[bp] = np.asarray(R[core]["st_out"])[:, i]
            new_k[bp] = np.asarray(R[core]["kc_out"])[:, i * cfg.lp:(i + 1) * cfg.lp].reshape(D, cfg.lp, 4, 2, 64)
            new_v[bp] = np.asarray(R[core]["vc_out"])[:, i * cfg.lp:(i + 1) * cfg.lp].reshape(D, cfg.lp, 4, 128)
    return (y_prompt, y_sample, new_state, new_k, new_v)
```

```python
import math
import os
from contextlib import ExitStack
import numpy as np
import ml_dtypes
import concourse.bass as bass
import concourse.mybir as mybir
from concourse.bass_types import AP
from concourse.bass_utils import run_bass_kernel_spmd

F32 = mybir.dt.float32
BF16 = mybir.dt.bfloat16
AF = mybir.ActivationFunctionType
ALU = mybir.AluOpType
AX = mybir.AxisListType

ENGS = ("pe", "act", "dve", "pool", "sp")
NDSEM = 16


class _Rec:
    def __getattr__(self, name):
        def f(*a, **k):
            self.call = (name, a, k)
            return self
        return f


class Prog:
    def __init__(self, nc):
        self.nc = nc
        self.ops = {e: [] for e in ENGS}
        self.cnt = {e: 0 for e in ENGS}
        self.waited = {e: {} for e in ENGS}
        self.lastw = {}
        self.readers = {}
        self.dq = {e: {"next": 0, "use": [0] * NDSEM} for e in ("sp", "act", "pool")}
        self.sems = {}
        self.nps = 0

    def _need(self, e, deps):
        waits = []
        w = self.waited[e]
        best = {}
        for (sk, v) in deps:
            if v > best.get(sk, 0):
                best[sk] = v
        for sk, v in best.items():
            if w.get(sk, 0) >= v:
                continue
            w[sk] = v
            waits.append((sk, v))
        return waits

    def _deps(self, reads, writes):
        deps = []
        for k in reads:
            t = self.lastw.get(k)
            if t is not None:
                deps.append(t)
        for k in writes:
            t = self.lastw.get(k)
            if t is not None:
                deps.append(t)
            for sk, v in self.readers.get(k, {}).items():
                deps.append((sk, v))
        return deps

    def _commit(self, tok, reads, writes):
        for k in reads:
            r = self.readers.setdefault(k, {})
            if tok[1] > r.get(tok[0], 0):
                r[tok[0]] = tok[1]
        for k in writes:
            self.lastw[k] = tok
            self.readers[k] = {}

    def op(self, e, fn, reads=(), writes=(), pe_acc=False):
        rec = _Rec()
        fn(rec)
        call = rec.call
        fn = lambda eh, call=call: getattr(eh, call[0])(*call[1], **call[2])
        deps = self._deps(reads, writes)
        if pe_acc:
            deps = [d for d in deps if d[0] != ("c", "pe")]
        waits = self._need(e, deps)
        self.cnt[e] += 1
        tok = (("c", e), self.cnt[e])
        self.ops[e].append((waits, fn, ("c", e)))
        self._commit(tok, reads, writes)
        return tok

    def dma(self, q, out, in_, reads=(), writes=(), **kw):
        d = self.dq[q]
        i = d["next"]
        d["next"] = (i + 1) % NDSEM
        deps = self._deps(reads, writes)
        if d["use"][i] > 0:
            deps.append((("d", q, i), 16 * d["use"][i]))
        waits = self._need(q, deps)
        d["use"][i] += 1
        tok = (("d", q, i), 16 * d["use"][i])

        def fn(eh, out=out, in_=in_, kw=kw):
            return eh.dma_start(out=out, in_=in_, **kw)
        self.ops[q].append((waits, fn, ("d", q, i)))
        self._commit(tok, reads, writes)
        return tok

    def barrier(self):
        toks = [(("c", e), self.cnt[e]) for e in ENGS if self.cnt[e] > 0]
        for q in ("sp", "act", "pool"):
            for i in range(NDSEM):
                u = self.dq[q]["use"][i]
                if u > 0:
                    toks.append((("d", q, i), 16 * u))
        for e in ENGS:
            waits = self._need(e, list(toks))
            if waits:
                self.ops[e].append((waits, None, None))

    def finish_wait(self, e, tokens):
        waits = self._need(e, list(tokens))
        self.ops[e].append((waits, None, None))

    def emit(self, st):
        nc = self.nc
        for e in ENGS:
            self.sems[("c", e)] = st.enter_context(nc.semaphore("c_" + e))
        for q in ("sp", "act", "pool"):
            for i in range(NDSEM):
                self.sems[("d", q, i)] = st.enter_context(nc.semaphore("d_%s_%d" % (q, i)))
        block = st.enter_context(nc.Block())

        def mk(e):
            def body(eh):
                for (waits, fn, inc) in self.ops[e]:
                    for (sk, v) in waits:
                        eh.wait_ge(self.sems[sk], v)
                    if fn is None:
                        continue
                    ins = fn(eh)
                    ins.then_inc(self.sems[inc], 1 if inc[0] == "c" else 16)
            return body
        block.tensor(mk("pe"))
        block.scalar(mk("act"))
        block.vector(mk("dve"))
        block.gpsimd(mk("pool"))
        block.sync(mk("sp"))


class Cfg:
    def __init__(self, depth=4, ls=2048, lp=256, past=256, nps=2):
        self.depth, self.ls, self.lp, self.past, self.nps = depth, ls, lp, past, nps
        self.nt = ls + nps * lp
        self.d = 1024
        self.kc = 8
        self.incols = 3456
        self.dff = 4096
        self.tiles = []
        for s in range(0, ls, 512):
            self.tiles.append((s, min(512, ls - s), 0))
        for s in range(ls, self.nt, 512):
            self.tiles.append((s, min(512, self.nt - s), 1))
        self.seqs = [(0, ls, 0)] + [(ls + i * lp, lp, 1) for i in range(nps)]


def lam_init_of(l):
    return 0.8 - 0.6 * math.exp(-0.3 * l)


def host_consts(cfg):
    c = {}
    c["ident_f"] = np.eye(128, dtype=np.float32)
    c["ident_b"] = np.eye(128, dtype=np.float32).astype(ml_dtypes.bfloat16)
    ob = np.zeros((128, 128), np.float32)
    ob[:64, :64] = 1.0
    ob[64:, 64:] = 1.0
    c["onesblk_f"] = ob
    c["onesblk_b"] = ob.astype(ml_dtypes.bfloat16)
    c["ones_b"] = np.ones((128, 128), np.float32).astype(ml_dtypes.bfloat16)
    c["ones_f"] = np.ones((128, 128), np.float32)
    c["i2_f"] = np.concatenate([np.eye(64, dtype=np.float32)] * 2, axis=0)
    sel = np.zeros((64, 2, 128), np.float32)
    for j in range(2):
        sel[np.arange(64), j, j * 64 + np.arange(64)] = 1.0
    c["sel_f"] = sel
    R = np.zeros((128, 128), np.float32)
    for m in range(2):
        b = m * 64
        for i in range(16):
            R[b + 16 + i, b + i] = -1.0
            R[b + i, b + 16 + i] = 1.0
            R[b + 48 + i, b + 32 + i] = -1.0
            R[b + 32 + i, b + 48 + i] = 1.0
    c["ropeR"] = R.astype(ml_dtypes.bfloat16)
    L = cfg.ls
    rows = L // 64
    row = np.repeat(np.arange(rows), 64).astype(np.float32)
    col = np.tile(np.arange(64), rows).astype(np.float32)
    inv = (10000.0 ** (-np.arange(0, 32, 2, dtype=np.float32) / 32)).astype(np.float32)
    ang_r = row[:, None] * inv[None]
    ang_c = col[:, None] * inv[None]
    ang = np.concatenate([ang_r, ang_r, ang_c, ang_c], axis=-1)
    cos = np.cos(ang).astype(np.float32).T
    sin = np.sin(ang).astype(np.float32).T
    c["cos"] = np.concatenate([cos, cos], axis=0).astype(ml_dtypes.bfloat16)
    c["sin"] = np.concatenate([sin, sin], axis=0).astype(ml_dtypes.bfloat16)
    for nm, Lx in (("s", cfg.ls), ("p", cfg.lp)):
        t = np.linspace(0.0, 1.0, Lx, dtype=np.float32)[:, None]
        ang2 = (np.float32(2.0 * math.pi / Lx) * np.arange(Lx, dtype=np.float32))[:, None]
        bands = np.linspace(1e-4, 15, 16, dtype=np.float32)[None, :]
        emb = np.concatenate([t, np.cos(bands * ang2), -np.sin(bands * ang2)], axis=-1).astype(np.float32)
        c["embT_" + nm] = emb.T.copy()
        SC = Lx // 128
        FC = (Lx + 1 + 127) // 128
        c["tlinT_" + nm] = t[:, 0].reshape(SC, 128).T.copy()
        n2 = 2 * Lx
        f = np.arange(FC * 128, dtype=np.int64)
        sidx = np.arange(Lx, dtype=np.int64)
        m = (sidx[:, None] * f[None, :]) % n2
        ang3 = 2.0 * np.pi * m.astype(np.float64) / n2
        valid = (f <= Lx).astype(np.float64)[None, :]
        Cf = np.cos(ang3) * valid
        Sf = -np.sin(ang3) * valid
        def lay_f(M):
            return np.ascontiguousarray(M.reshape(SC, 128, FC, 128).transpose(2, 1, 0, 3)).astype(ml_dtypes.bfloat16)
        c["fwdC_" + nm] = lay_f(Cf)
        c["fwdS_" + nm] = lay_f(Sf)
        wgt = np.where((f == 0) | (f == Lx), 1.0, 2.0) * (f <= Lx) / n2
        Ci = (np.cos(ang3) * wgt[None, :]).T
        Si = (-np.sin(ang3) * wgt[None, :]).T
        c["invC_" + nm] = np.ascontiguousarray(Ci.reshape(FC, 128, Lx)).astype(ml_dtypes.bfloat16)
        c["invS_" + nm] = np.ascontiguousarray(Si.reshape(FC, 128, Lx)).astype(ml_dtypes.bfloat16)
    return c


CONST_SHAPES = None


def build_program(cfg, consts, debug=(), mixers=()):
    nc = bass.Bass("TRN2", target_bir_lowering=False)
    D, KC, NT, LS, LP = cfg.d, cfg.kc, cfg.nt, cfg.ls, cfg.lp
    DEPTH = cfg.depth
    NPS = cfg.nps
    PAST = cfg.past
    din = {}

    def inp(name, shape, dt=F32):
        din[name] = nc.dram_tensor(name, list(shape), dt, kind="ExternalInput").ap()
        return din[name]

    def outp(name, shape):
        din[name] = nc.dram_tensor(name, list(shape), F32, kind="ExternalOutput").ap()
        return din[name]

    for k, v in consts.items():
        inp("c_" + k, v.shape, BF16 if v.dtype == ml_dtypes.bfloat16 else F32)
    xT_in = inp("xT", [128, KC, NT])
    cvec = inp("cvecT", [128, KC, 2])
    w_mod = inp("w_mod", [DEPTH, D, 6 * D])
    b_modT = inp("b_modT", [DEPTH, 128, 48])
    gains = inp("gains", [DEPTH, 128, 4, KC])
    w_in = inp("w_in", [DEPTH, D, cfg.incols])
    w_out = inp("w_out", [DEPTH, D, D])
    w_ff1 = inp("w_ff1", [DEPTH, D, cfg.dff])
    w_ff2 = inp("w_ff2", [DEPTH, cfg.dff, D])
    rw_conv = inp("rw_conv", [DEPTH, 128, 6, 3])
    rw_w0 = inp("rw_w0", [DEPTH, 128, 2, 2])
    rw_a0 = inp("rw_a0", [DEPTH, 128, 2, 2])
    rw_w2 = inp("rw_w2", [DEPTH, 128, 256])
    rw_a2 = inp("rw_a2", [DEPTH, 128, 256])
    rw_g2 = inp("rw_g2", [DEPTH, 128, 256])
    rw_vec = inp("rw_vec", [DEPTH, 128, 3, 2])
    rw_ln = inp("rw_ln", [DEPTH, 64, 2, 4])
    st_in = inp("st_in", [DEPTH, 128, 4, 64])
    ckT = inp("ckT", [DEPTH, 4, 128, PAST])
    cv = inp("cv", [DEPTH, 128, PAST // 128, 4, 128])
    dlam = inp("dlam", [DEPTH, 128, 4, 64])
    dsub = inp("dsub", [DEPTH, 128, 1])
    hy_conv = inp("hy_conv", [DEPTH, 128, 6, 4])
    hy_w1 = inp("hy_w1", [DEPTH, 33, 64])
    hy_vec = inp("hy_vec", [DEPTH, 64, 3])
    hy_w2 = inp("hy_w2", [DEPTH, 64, 64])
    hy_w3 = inp("hy_w3", [DEPTH, 64, 1024])
    hy_decb = inp("hy_decb", [DEPTH, 128, 256])
    hy_bias = inp("hy_bias", [DEPTH, 128, 2, 2])

    yT_out = outp("yT", [128, KC, NT])
    st_out = outp("st_out", [DEPTH, NPS, 2, 4, 64, 64])
    kc_out = outp("kc_out", [DEPTH, NPS * LP, 512])
    vc_out = outp("vc_out", [DEPTH, NPS * LP, 512])
    dbg = {}
    for nm, shape in debug:
        dbg[nm] = outp("dbg_" + nm, shape)
    x_spill = nc.dram_tensor("x_spill", [128, KC, NT], F32, kind="Internal").ap()

    st = ExitStack()
    P = Prog(nc)
    out_toks = []

    def sb(name, shape, dt=F32):
        return st.enter_context(nc.sbuf_tensor("s_" + name, list(shape), dt))

    C = {}
    for k, v in consts.items():
        if k[:4] in ("embT", "tlin", "fwdC", "fwdS", "invC", "invS"):
            continue
        if k in ("cos", "sin") and "attn" not in mixers:
            continue
        t = sb("k_" + k, v.shape, BF16 if v.dtype == ml_dtypes.bfloat16 else F32)
        P.dma("sp", t[:], din["c_" + k][:], writes=["k_" + k])
        C[k] = t
    CK = {k: ["k_" + k] for k in C}

    BIGB = getattr(cfg, 'bigb', 170 * 1024)
    XB = KC * NT * 4
    MB0 = BIGB - KC * NT * 2
    HB0 = MB0 - KC * NT * 2
    big = sb("big", [128, BIGB // 4])
    xT = big[:, 0:XB // 4].rearrange("p (c t) -> p c t", c=KC)
    P.dma("sp", xT, xT_in[:], writes=["xT"])
    cv_t = sb("cvec", [128, KC, 2])
    P.dma("sp", cv_t[:], cvec[:], writes=["cvec"])
    scv = sb("scvec", [128, KC, 2], BF16)
    P.op("act", lambda e: e.activation(out=scv[:], in_=cv_t[:], func=AF.Silu), reads=["cvec"], writes=["scvec"])

    PS = [st.enter_context(nc.psum_tensor("ps%d" % i, [128, 512], F32)) for i in range(8)]
    psn = [0]

    def psum():
        i = psn[0] % 4
        psn[0] += 1
        return PS[i], "ps%d" % i

    def psfix(i):
        return PS[4 + i], "ps%d" % (4 + i)

    hT = big[:, HB0 // 4:MB0 // 4].bitcast(BF16).rearrange("p (c t) -> p c t", c=KC)
    mixT = big[:, MB0 // 4:BIGB // 4].bitcast(BF16).rearrange("p (c t) -> p c t", c=KC)
    h_spill = nc.dram_tensor("h_spill", [128, KC, NT], BF16, kind="Internal").ap()
    modT = sb("modT", [128, 48, 2])
    bmod = sb("bmod", [128, 48])
    gn = sb("gains", [128, 4, KC])
    sc_all = sb("sc_all", [128, 4, KC, 2])
    wst = [sb("wst%d" % i, [128, KC, 128], BF16) for i in range(3)]
    wstn = [0]
    tmpA = [sb("tmpA%d" % i, [128, 512]) for i in range(3)]
    tmpn = [0]
    rstd = sb("rstd", [128, 512])
    sqb = [sb("sqb%d" % i, [128, 512], BF16) for i in range(2)]

    def tmp():
        i = tmpn[0] % 3
        tmpn[0] += 1
        return tmpA[i], "tmpA%d" % i

    arena = big

    class Carver:
        def __init__(self):
            self.off = 0
            self.lim = BIGB

        def reset(self, base=0, lim=None):
            self.off = base
            self.lim = BIGB if lim is None else lim

        def get(self, shape, dt=F32):
            n = int(np.prod(shape[1:]))
            nbytes = n * (4 if dt == F32 else 2)
            nbytes = (nbytes + 31) // 32 * 32
            assert self.off + nbytes <= self.lim, ("arena overflow", self.off, nbytes, self.lim)
            a = arena[0:shape[0], self.off // 4:(self.off + nbytes) // 4]
            self.off += nbytes
            if dt != F32:
                a = a.bitcast(dt)
                a = a[:, 0:n]
            if len(shape) > 2:
                names = " ".join("d%d" % i for i in range(1, len(shape)))
                kw = {"d%d" % i: shape[i] for i in range(1, len(shape))}
                a = a.rearrange("p (%s) -> p %s" % (names, names), **kw)
            return a
    carver = Carver()

    def load_w(src_ap, kchunks, dst=None, key=None):
        if dst is None:
            i = wstn[0] % 3
            wstn[0] += 1
            dst, key = wst[i], "wst%d" % i
        P.dma("pool", dst[:, 0:kchunks, 0:src_ap.shape[1]],
              src_ap.rearrange("(kc p) c -> p kc c", p=128), writes=[key])
        return dst, key

    def rms_stats(src_fn, src_keys, t0, n, out_rstd, out_key, nchunks=KC, scale=1.0 / 1024, eps=1e-6, ones=None):
        ps, pk = psum()
        for c in range(nchunks):
            sq, sk = sqb[c % 2], "sqb%d" % (c % 2)
            src = src_fn(c)
            P.op("act", lambda e, sq=sq, src=src: e.activation(out=sq[:, 0:n], in_=src, func=AF.Square),
                 reads=src_keys, writes=[sk])
            P.op("pe", lambda e, sq=sq, c=c: e.matmul(ps[:, 0:n], lhsT=(ones or C["ones_b"])[:], rhs=sq[:, 0:n],
                                                       start=(c == 0), stop=(c == nchunks - 1)),
                 reads=[sk, "k_ones_b"], writes=[pk], pe_acc=(c > 0))
        P.op("act", lambda e: e.activation(out=out_rstd[:, 0:n], in_=ps[:, 0:n], func=AF.Ln, scale=scale, bias=epsc(eps)),
             reads=[pk, "epsc"], writes=[out_key])
        P.op("act", lambda e: e.activation(out=out_rstd[:, 0:n], in_=out_rstd[:, 0:n], func=AF.Exp, scale=-0.5),
             reads=[out_key], writes=[out_key])

    eps_tiles = {}

    def epsc(v):
        return eps_tiles[v][:, 0:1]

    for v in (1e-6, 1e-12, 64e-5):
        t = sb("eps%d" % len(eps_tiles), [128, 1])
        eps_tiles[v] = t
        P.op("pool", lambda e, t=t, v=v: e.memset(t[:], v), writes=["epsc"])

    def barrier():
        P.barrier()

    def dbg_out(nm, src_ap, keys):
        if nm in dbg:
            out_toks.append(P.dma("pool", dbg[nm][:], src_ap, reads=keys, writes=["dbg_" + nm]))

    pconv = sb("pconv", [128, 6, 3]); pw0 = sb("pw0", [128, 2, 2]); pa0 = sb("pa0", [128, 2, 2])
    pvec = sb("pvec", [128, 3, 2]); pln = sb("pln", [64, 2, 4])
    w2t = sb("w2t", [128, 256], BF16); a2t = sb("a2t", [128, 256], BF16); g2t = sb("g2t", [128, 256], BF16)
    omka = sb("omka", [128, 2])
    plam = sb("plam", [128, 4, 64]); psub = sb("psub", [128, 1]); lamv = sb("lamv", [128, 4]); lamt = sb("lamt", [128, 64])
    hconv = sb("hconv", [128, 6, 4]); hw1 = sb("hw1", [33, 64]); hvec = sb("hvec", [64, 3]); hw2 = sb("hw2", [64, 64])
    hsc = sb("hsc", [64, 6]); fss = sb("fss", [128, 3]); hdec = sb("hdec", [128, 2]); hbias = sb("hbias", [128, 2, 2]); fb = sb("fb", [64, 2])
    def mkap(base, offset_elems, dims):
        return AP(base.tensor, base.offset + offset_elems, [list(base.ap[0])] + [list(d) for d in dims])

    def rwkv_layer(l, rkv, lor):
        P.dma("sp", h_spill[:], hT, reads=["hT"], writes=["h_spill"])
        barrier()
        carver.lim = MB0
        for (t, src) in ((pconv, rw_conv), (pw0, rw_w0), (pa0, rw_a0), (pvec, rw_vec), (pln, rw_ln)):
            P.dma("sp", t[:], src[l], writes=["rwp"])
        for (t, src) in ((w2t, rw_w2), (a2t, rw_a2), (g2t, rw_g2)):
            P.dma("pool", t[:], src[l], writes=["rwp"])
        P.op("dve", lambda e: e.tensor_scalar(out=omka[:], in0=pvec[:, 1, :], scalar1=-1.0, scalar2=1.0, op0=ALU.mult, op1=ALU.add),
             reads=["rwp"], writes=["omka"])
        tmpc = carver.get([128, LS])
        LH = LS // 2
        NPL = NPS * LP
        WS = carver.get([128, 2, 2, LH]); NBS = carver.get([128, 2, 2, LH], BF16); KDS = carver.get([128, 2, 2, LH], BF16)
        WP = carver.get([128, 2, 2, NPL]); NBP = carver.get([128, 2, 2, NPL], BF16); KDP = carver.get([128, 2, 2, NPL], BF16)
        snames = ["s"] + ["p%d" % i for i in range(NPS)]
        Ast = {sn: [carver.get([128, 4, 64]) for _ in range(2)] for sn in snames}
        Bst = {sn: carver.get([128, 4, 64]) for sn in snames}
        G1t = {sn: carver.get([128, 4, 64]) for sn in snames}
        DVt = {sn: carver.get([128, 4, 64], BF16) for sn in snames}
        Abf = {sn: [carver.get([128, 4, 64], BF16) for _ in range(2)] for sn in snames}
        KVt = {sn: carver.get([128, 4, 64]) for sn in snames}
        gT = carver.get([128, 2, 512])
        bon = mixT[:, 0:2, :]
        Y = mixT[0:64, 2:6, :]
        kkn = mixT[:, 6:8, :]
        for c in range(6):
            for (s0, L, cnd) in cfg.seqs:
                u = rkv[:, c, s0:s0 + L]
                P.op("act", lambda e: e.activation(out=tmpc[:, 0:L], in_=u, func=AF.Identity, scale=pconv[:, c, 1:2]),
                     reads=["rkv", "rwp"], writes=["tmpc"])
                P.op("dve", lambda e: e.scalar_tensor_tensor(out=tmpc[:, 1:L], in0=rkv[:, c, s0:s0 + L - 1], scalar=pconv[:, c, 0:1],
                                                             in1=tmpc[:, 1:L], op0=ALU.mult, op1=ALU.add),
                     reads=["rkv", "rwp", "tmpc"], writes=["tmpc"])
                P.op("dve", lambda e: e.scalar_tensor_tensor(out=tmpc[:, 0:L - 1], in0=rkv[:, c, s0 + 1:s0 + L], scalar=pconv[:, c, 2:3],
                                                             in1=tmpc[:, 0:L - 1], op0=ALU.mult, op1=ALU.add),
                     reads=["rkv", "rwp", "tmpc"], writes=["tmpc"])
                P.op("act", lambda e: e.copy(out=rkv[:, c, s0:s0 + L], in_=tmpc[:, 0:L]), reads=["tmpc"], writes=["rkv"])

        def gen_range(t0, n, dsts, first):
            for p in range(2):
                if first:
                    kk_t, kk_k = tmp()
                    P.op("act", lambda e: e.activation(out=kk_t[:, 0:n], in_=rkv[:, 2 + p, t0:t0 + n], func=AF.Identity, scale=pvec[:, 0, p:p + 1]),
                         reads=["rkv", "rwp"], writes=[kk_k])
                    P.op("act", lambda e: e.activation(out=sqb[0][:, 0:n], in_=kk_t[:, 0:n], func=AF.Square), reads=[kk_k], writes=["sqb0"])
                    ps, pk = psum()
                    P.op("pe", lambda e: e.matmul(ps[:, 0:n], lhsT=C["onesblk_b"][:], rhs=sqb[0][:, 0:n], start=True, stop=True),
                         reads=["sqb0", "k_onesblk_b"], writes=[pk])
                    P.op("act", lambda e: e.activation(out=rstd[:, 0:n], in_=ps[:, 0:n], func=AF.Ln, bias=epsc(1e-12)), reads=[pk, "epsc"], writes=["rstd"])
                    P.op("act", lambda e: e.activation(out=rstd[:, 0:n], in_=rstd[:, 0:n], func=AF.Exp, scale=-0.5), reads=["rstd"], writes=["rstd"])
                    P.op("dve", lambda e: e.tensor_tensor(out=kkn[:, p, t0:t0 + n], in0=kk_t[:, 0:n], in1=rstd[:, 0:n], op=ALU.mult),
                         reads=[kk_k, "rstd"], writes=["kkn"])
                    psb, psbk = psfix(3)
                for d in range(2):
                    if dsts[d] is None and not first:
                        continue
                    if dsts[d] is not None:
                        ps, pk = psum()
                        P.op("pe", lambda e: e.matmul(ps[:, 0:n], lhsT=w2t[d * 64:(d + 1) * 64, p * 128:(p + 1) * 128],
                                                      rhs=lor[d * 64:(d + 1) * 64, 0, t0:t0 + n], start=True, stop=True),
                             reads=["rwp", "lor"], writes=[pk])
                        sg, sgk = tmp()
                        P.op("act", lambda e: e.activation(out=sg[:, 0:n], in_=ps[:, 0:n], func=AF.Sigmoid, bias=pw0[:, d, p:p + 1]),
                             reads=[pk, "rwp"], writes=[sgk])
                        P.op("act", lambda e: e.activation(out=dsts[d][0](p), in_=sg[:, 0:n], func=AF.Exp, scale=-math.exp(-0.5)),
                             reads=[sgk], writes=[dsts[d][3]])
                    ps, pk = psum()
                    P.op("pe", lambda e: e.matmul(ps[:, 0:n], lhsT=a2t[d * 64:(d + 1) * 64, p * 128:(p + 1) * 128],
                                                  rhs=lor[d * 64:(d + 1) * 64, 1, t0:t0 + n], start=True, stop=True),
                         reads=["rwp", "lor"], writes=[pk])
                    av, avk = tmp()
                    P.op("act", lambda e: e.activation(out=av[:, 0:n], in_=ps[:, 0:n], func=AF.Sigmoid, bias=pa0[:, d, p:p + 1]),
                         reads=[pk, "rwp"], writes=[avk])
                    if dsts[d] is not None:
                        P.op("dve", lambda e: e.scalar_tensor_tensor(out=dsts[d][1](p), in0=av[:, 0:n], scalar=-1.0, in1=kkn[:, p, t0:t0 + n],
                                                                     op0=ALU.mult, op1=ALU.mult), reads=[avk, "kkn"], writes=[dsts[d][3]])
                    P.op("dve", lambda e: e.tensor_scalar(out=av[:, 0:n], in0=av[:, 0:n], scalar1=pvec[:, 1, p:p + 1], scalar2=omka[:, p:p + 1],
                                                          op0=ALU.mult, op1=ALU.add), reads=[avk, "rwp", "omka"], writes=[avk])
                    P.op("dve", lambda e: e.tensor_tensor(out=av[:, 0:n], in0=av[:, 0:n], in1=rkv[:, 2 + p, t0:t0 + n], op=ALU.mult),
                         reads=[avk, "rkv"], writes=[avk])
                    if dsts[d] is not None:
                        P.op("act", lambda e: e.copy(out=dsts[d][2](p), in_=av[:, 0:n]), reads=[avk], writes=[dsts[d][3]])
                    if first:
                        P.op("dve", lambda e: e.scalar_tensor_tensor(out=sqb[1][:, 0:n], in0=av[:, 0:n], scalar=pvec[:, 2, p:p + 1],
                                                                     in1=rkv[:, p, t0:t0 + n], op0=ALU.mult, op1=ALU.mult),
                             reads=[avk, "rwp", "rkv"], writes=["sqb1"])
                        P.op("pe", lambda e: e.matmul(psb[:, 0:n], lhsT=C["onesblk_b"][:], rhs=sqb[1][:, 0:n], start=(d == 0), stop=(d == 1)),
                             reads=["sqb1", "k_onesblk_b"], writes=[psbk], pe_acc=(d == 1))
                if first:
                    P.op("dve", lambda e: e.tensor_tensor(out=bon[:, p, t0:t0 + n], in0=psb[:, 0:n], in1=rkv[:, 4 + p, t0:t0 + n], op=ALU.mult),
                         reads=[psbk, "rkv"], writes=["bon"])

        PIECE = min(512, LH)

        def s_dsts(t0, n, phase):
            half = 0 if t0 < LH else 1
            d = half if phase == 0 else 1 - half
            col = t0 - half * LH
            out = [None, None]
            out[d] = (lambda p: WS[:, p, d, col:col + n], lambda p: NBS[:, p, d, col:col + n], lambda p: KDS[:, p, d, col:col + n], "opS")
            return out
        for t0 in range(0, LS, PIECE):
            gen_range(t0, PIECE, s_dsts(t0, PIECE, 0), True)
        for q0 in range(0, NPL, PIECE if NPL >= PIECE else NPL):
            n = min(PIECE, NPL - q0)
            both = [(lambda p, d=d: WP[:, p, d, q0:q0 + n], lambda p, d=d: NBP[:, p, d, q0:q0 + n], lambda p, d=d: KDP[:, p, d, q0:q0 + n], "opP") for d in range(2)]
            gen_range(LS + q0, n, both, True)
        barrier()
        streams = [("s", 0, LS, True, 0)] + [("p%d" % i, LS + i * LP, LP, False, i * LP) for i in range(NPS)]
        ypsums = {}
        for (sn, s0, L, is_s, q0) in streams:
            if is_s:
                P.dma("sp", Ast[sn][0], st_in[l], writes=["A_" + sn])
            else:
                P.op("pool", lambda e: e.memset(Ast[sn][0], 0.0), writes=["A_" + sn])
            ypsums[sn] = psfix(len(ypsums))
        pend = {}

        def emit_y(sn, s0, L, i, nxt):
            ak = "A_" + sn
            slot = i % 64
            yp, ypk = ypsums[sn]
            for g in range(4):
                p, d = g // 2, g % 2
                rcol = slot if d == 0 else 63 - slot
                for j in range(2):
                    col = ((slot * 2 + d) * 2 + p) * 2 + j
                    tcol = s0 + (i if d == 0 else L - 1 - i)
                    P.op("pe", lambda e: e.matmul(yp[0:64, col:col + 1], lhsT=nxt[j * 64:(j + 1) * 64, g, :], rhs=rkv[j * 64:(j + 1) * 64, p, tcol:tcol + 1],
                                                  start=True, stop=True), reads=["Abf%s%d" % (sn, (i + 1) % 2), "rkv"], writes=[ypk], pe_acc=True)
            if slot == 63 or i == L - 1:
                ns = slot + 1
                i0 = i - slot
                first = i0 < L // 2
                ypv = yp[0:64, 0:ns * 8].rearrange("v (s d h) -> v s d h", d=2, h=4)
                for d in range(2):
                    if d == 0:
                        dst = mkap(Y[:, 0, 0:1], s0 + i0, [[1, ns], [Y.ap[1][0], 4]])
                    else:
                        dst = mkap(Y[:, 0, 0:1], s0 + L - 1 - i0, [[-1, ns], [Y.ap[1][0], 4]])
                    if first:
                        P.op("act", lambda e: e.copy(out=dst, in_=ypv[:, :, d, :]), reads=[ypk], writes=["Y"])
                    else:
                        P.op("dve", lambda e: e.tensor_tensor(out=dst, in0=ypv[:, :, d, :], in1=dst, op=ALU.add), reads=[ypk, "Y"], writes=["Y"])
        maxL = max(L for (_, _, L, _, _) in streams)
        for i in range(maxL):
            if i == LH:
                for t0 in range(0, LS, PIECE):
                    gen_range(t0, PIECE, s_dsts(t0, PIECE, 1), False)
            for (sn, s0, L, is_s, q0) in streams:
                if i >= L:
                    continue
                cur, nxt = Ast[sn][i % 2], Ast[sn][(i + 1) % 2]
                ak = "A_" + sn
                dstr = L - 1 - 2 * i
                if is_s:
                    il = i % LH
                    Wb, NBb, KDb, opk, Lb, off = WS, NBS, KDS, "opS", LH, il
                    dl = LH - 1 - 2 * il
                else:
                    Wb, NBb, KDb, opk, Lb, off = WP, NBP, KDP, "opP", NPL, q0 + i
                    dl = L - 1 - 2 * i

                def gop(base3, pstride, dextra):
                    return mkap(base3, s0 + i, [[pstride, 2], [dextra + dstr, 2], [0, 64]])

                def lop(buf):
                    return mkap(buf[:, 0, 0, 0:1], off, [[2 * Lb, 2], [Lb + dl, 2], [0, 64]])
                kk_op = gop(kkn[:, 0, 0:1], kkn.ap[1][0], 0)
                v_op = gop(rkv[:, 4, 0:1], NT, 0)
                w_op = lop(Wb)
                kd_op = lop(KDb)
                slot = i % 64
                if slot == 0:
                    if pend.get(sn) is not None:
                        emit_y(*pend[sn])
                        pend[sn] = None
                P.op("pool", lambda e: e.tensor_tensor(out=DVt[sn], in0=mkap(C["i2_f"][:, 0:1], 0, [[0, 4], [1, 64]]), in1=v_op, op=ALU.mult),
                     reads=["rkv", "k_i2_f"], writes=["DV" + sn])
                ps2, ps2k = psum()
                P.op("pe", lambda e: e.matmul(ps2[:, 0:256], lhsT=C["onesblk_b"][:], rhs=DVt[sn].rearrange("p g v -> p (g v)"), start=True, stop=True),
                     reads=["DV" + sn, "k_onesblk_b"], writes=[ps2k])
                P.op("dve", lambda e: e.tensor_tensor(out=KVt[sn], in0=ps2[:, 0:256].rearrange("p (g v) -> p g v", g=4), in1=kd_op, op=ALU.mult),
                     reads=[ps2k, opk], writes=["KV" + sn])
                P.op("dve", lambda e: e.tensor_tensor(out=Bst[sn], in0=cur, in1=w_op, op=ALU.mult), reads=[ak, opk], writes=["B" + sn])
                P.op("dve", lambda e: e.tensor_tensor(out=G1t[sn], in0=cur, in1=kk_op, op=ALU.mult), reads=[ak, "kkn"], writes=["G1" + sn])
                ps1, ps1k = psum()
                P.op("pe", lambda e: e.matmul(ps1[:, 0:256], lhsT=C["onesblk_f"][:], rhs=G1t[sn].rearrange("p g v -> p (g v)"), start=True, stop=True),
                     reads=["G1" + sn, "k_onesblk_f"], writes=[ps1k])
                if pend.get(sn) is not None:
                    emit_y(*pend[sn])
                    pend[sn] = None
                P.op("dve", lambda e: e.tensor_tensor(out=Bst[sn], in0=Bst[sn], in1=KVt[sn], op=ALU.add), reads=["B" + sn, "KV" + sn], writes=["B" + sn])
                for g in range(4):
                    p, d = g // 2, g % 2
                    lcol = off if d == 0 else off + dl
                    P.op("dve", lambda e: e.scalar_tensor_tensor(out=nxt[:, g, :], in0=ps1[:, g * 64:(g + 1) * 64], scalar=NBb[:, p, d, lcol:lcol + 1],
                                                                 in1=Bst[sn][:, g, :], op0=ALU.mult, op1=ALU.add),
                         reads=[ps1k, opk, "B" + sn], writes=[ak])
                abf = Abf[sn][(i + 1) % 2]
                P.op("act", lambda e: e.copy(out=abf, in_=nxt), reads=[ak], writes=["Abf%s%d" % (sn, (i + 1) % 2)])
                pend[sn] = (sn, s0, L, i, abf)
        for sn in list(pend.keys()):
            if pend[sn] is not None:
                emit_y(*pend[sn])
                pend[sn] = None
        for si, (sn, s0, L, is_s, q0) in enumerate(streams):
            if is_s:
                continue
            fin = Ast[sn][L % 2]
            for g in range(4):
                p, d = g // 2, g % 2
                ps, pk = psum()
                P.op("pe", lambda e: e.transpose(ps[0:64, 0:128], fin[:, g, :], C["ident_f"][:]), reads=["A_" + sn, "k_ident_f"], writes=[pk])
                tt, tk = tmp()
                P.op("act", lambda e: e.copy(out=tt[0:64, 0:128], in_=ps[0:64, 0:128]), reads=[pk], writes=[tk])
                for j in range(2):
                    out_toks.append(P.dma("sp", st_out[l, si - 1, d, 2 * p + j], tt[0:64, j * 64:(j + 1) * 64], reads=[tk], writes=["st_out"]))
        for (t0, n, cnd) in cfg.tiles:
            for p in range(2):
                ps, pk = psum()
                P.op("pe", lambda e: e.matmul(ps[:, 0:n], lhsT=g2t[:, p * 128:(p + 1) * 128], rhs=lor[:, 2, t0:t0 + n], start=True, stop=True),
                     reads=["rwp", "lor"], writes=[pk])
                P.op("act", lambda e: e.copy(out=gT[:, p, 0:n], in_=ps[:, 0:n]), reads=[pk], writes=["gT"])
            for p in range(2):
                pso, psok = psfix(3)
                for j in range(2):
                    h = 2 * p + j
                    yv = Y[:, h, t0:t0 + n]
                    psm, psmk = psum()
                    yc, yck = tmp()
                    P.op("act", lambda e: e.copy(out=yc[0:64, 0:n], in_=yv), reads=["Y"], writes=[yck])
                    P.op("pe", lambda e: e.matmul(psm[0:64, 0:n], lhsT=C["ones_f"][0:64, 0:64], rhs=yc[0:64, 0:n], start=True, stop=True),
                         reads=[yck, "k_ones_f"], writes=[psmk])
                    P.op("dve", lambda e: e.scalar_tensor_tensor(out=yc[0:64, 0:n], in0=psm[0:64, 0:n], scalar=-1.0 / 64, in1=yc[0:64, 0:n],
                                                                 op0=ALU.mult, op1=ALU.add), reads=[psmk, yck], writes=[yck])
                    sq, sqk = tmp()
                    P.op("act", lambda e: e.activation(out=sq[0:64, 0:n], in_=yc[0:64, 0:n], func=AF.Square), reads=[yck], writes=[sqk])
                    psv, psvk = psum()
                    P.op("pe", lambda e: e.matmul(psv[0:64, 0:n], lhsT=C["ones_f"][0:64, 0:64], rhs=sq[0:64, 0:n], start=True, stop=True),
                         reads=[sqk, "k_ones_f"], writes=[psvk])
                    P.op("act", lambda e: e.activation(out=sq[0:64, 0:n], in_=psv[0:64, 0:n], func=AF.Ln, scale=1.0 / 64, bias=eps_tiles[64e-5][0:64, 0:1]),
                         reads=[psvk, "epsc"], writes=[sqk])
                    P.op("act", lambda e: e.activation(out=sq[0:64, 0:n], in_=sq[0:64, 0:n], func=AF.Exp, scale=-0.5), reads=[sqk], writes=[sqk])
                    P.op("dve", lambda e: e.tensor_tensor(out=yc[0:64, 0:n], in0=yc[0:64, 0:n], in1=sq[0:64, 0:n], op=ALU.mult), reads=[yck, sqk], writes=[yck])
                    P.op("act", lambda e: e.activation(out=yc[0:64, 0:n], in_=yc[0:64, 0:n], func=AF.Identity, scale=pln[:, 0, h:h + 1], bias=pln[:, 1, h:h + 1]),
                         reads=[yck, "rwp"], writes=[yck])
                    P.op("pe", lambda e: e.matmul(pso[:, 0:n], lhsT=C["sel_f"][:, j, :], rhs=yc[0:64, 0:n], start=(j == 0), stop=(j == 1)),
                         reads=[yck, "k_sel_f"], writes=[psok], pe_acc=(j == 1))
                tt, tk = tmp()
                P.op("dve", lambda e: e.tensor_tensor(out=tt[:, 0:n], in0=pso[:, 0:n], in1=bon[:, p, t0:t0 + n], op=ALU.add), reads=[psok, "bon"], writes=[tk])
                P.op("dve", lambda e: e.tensor_tensor(out=mixT[:, p, t0:t0 + n], in0=tt[:, 0:n], in1=gT[:, p, 0:n], op=ALU.mult), reads=[tk, "gT"], writes=["mixT"])
        barrier()
        P.dma("sp", hT, h_spill[:], reads=["h_spill"], writes=["hT"])
        dbg_out("ya%d" % l, mixT[:, 0:2, :], ["mixT"])


    def attn_layer(l, lam0):
        carver.reset(0, HB0)
        NKC = PAST // 128
        qT = carver.get([128, 4, NT], BF16)
        kT = carver.get([128, 4, NT], BF16)
        kcT = carver.get([128, 4, PAST], BF16)
        vS = carver.get([128, NKC + LS // 128, 4, 128], BF16)
        vP = carver.get([128, NPS * LP // 128, 4, 128], BF16)
        stg = [carver.get([128, 512]) for _ in range(2)]
        wv = carver.get([128, KC, 512], BF16)
        pTs = [carver.get([128, 512], BF16) for _ in range(3)]
        xb = [carver.get([128, 512], BF16) for _ in range(2)]
        of = carver.get([128, 512])
        P.dma("sp", plam[:], dlam[l], writes=["plam"])
        P.dma("sp", psub[:], dsub[l], writes=["psub"])
        P.dma("pool", kcT, ckT[l].rearrange("h p t -> p h t"), writes=["kcT"])
        P.dma("pool", vS[:, 0:NKC], cv[l], writes=["vS"])
        for i in range(2):
            P.op("dve", lambda e: e.tensor_tensor(out=lamt[:], in0=plam[:, 2 * i, :], in1=plam[:, 2 * i + 1, :], op=ALU.mult), reads=["plam"], writes=["lamt"])
            P.op("dve", lambda e: e.reduce_sum(out=lamv[:, i:i + 1], in_=lamt[:], axis=AX.X), reads=["lamt"], writes=["lamv"])
        P.op("act", lambda e: e.activation(out=lamv[:, 0:2], in_=lamv[:, 0:2], func=AF.Exp), reads=["lamv"], writes=["lamv"])
        P.op("dve", lambda e: e.tensor_tensor(out=lamv[:, 2:3], in0=lamv[:, 0:1], in1=lamv[:, 1:2], op=ALU.subtract), reads=["lamv"], writes=["lamv"])
        P.op("dve", lambda e: e.tensor_scalar(out=lamv[:, 3:4], in0=lamv[:, 2:3], scalar1=-1.0, scalar2=-lam0, op0=ALU.mult, op1=ALU.add), reads=["lamv"], writes=["lamv"])
        P.op("dve", lambda e: e.tensor_scalar(out=psub[:], in0=psub[:], scalar1=(1.0 - lam0), scalar2=None, op0=ALU.mult), reads=["psub"], writes=["psub"])

        def cbqk(ps, pk, ci, t0, n, cnd):
            dst = (qT if ci < 4 else kT)[:, ci % 4, t0:t0 + n]
            dk = "qT" if ci < 4 else "kT"
            if cnd == 1 or os.environ.get("NOROPE"):
                P.op("act", lambda e: e.copy(out=dst, in_=ps[:, 0:n]), reads=[pk], writes=[dk])
                return
            xbt, xbk = xb[ci % 2], "xb%d" % (ci % 2)
            P.op("act", lambda e: e.copy(out=xbt[:, 0:n], in_=ps[:, 0:n]), reads=[pk], writes=[xbk])
            pr, prk = psum()
            P.op("pe", lambda e: e.matmul(pr[:, 0:n], lhsT=C["ropeR"][:], rhs=xbt[:, 0:n], start=True, stop=True), reads=[xbk, "k_ropeR"], writes=[prk])
            t1, t1k = tmp()
            P.op("dve", lambda e: e.tensor_tensor(out=t1[:, 0:n], in0=xbt[:, 0:n], in1=C["cos"][:, t0:t0 + n], op=ALU.mult), reads=[xbk, "k_cos"], writes=[t1k])
            t2, t2k = tmp()
            P.op("dve", lambda e: e.tensor_tensor(out=t2[:, 0:n], in0=pr[:, 0:n], in1=C["sin"][:, t0:t0 + n], op=ALU.mult), reads=[prk, "k_sin"], writes=[t2k])
            P.op("dve", lambda e: e.tensor_tensor(out=dst, in0=t1[:, 0:n], in1=t2[:, 0:n], op=ALU.add), reads=[t1k, t2k], writes=[dk])
        STG = int(os.environ.get("ATT_STAGE", "9"))
        if STG < 1:
            return
        proj_fm(w_in[l], 1152, 8, hT, "hT", cbqk)
        if STG < 2:
            return

        def proj_tm(col0, tok_ranges, cb):
            P.dma("pool", wv, w_in[l][:, col0:col0 + 512].rearrange("(kc p) c -> p kc c", p=128), writes=["wv"])
            for (t0, info) in tok_ranges:
                ps, pk = psum()
                for c in range(KC):
                    P.op("pe", lambda e: e.matmul(ps[:, 0:512], lhsT=hT[:, c, t0:t0 + 128], rhs=wv[:, c, :], start=(c == 0), stop=(c == KC - 1)),
                         reads=["hT", "wv"], writes=[pk], pe_acc=(c > 0))
                cb(ps, pk, t0, info)
        prm_chunks = [(LS + i * 128, i) for i in range(NPS * LP // 128)]
        sam_chunks = [(i * 128, i) for i in range(LS // 128)]
        scnt = [0]

        def cbk(ps, pk, t0, i):
            sg, sgk = stg[scnt[0] % 2], "stg%d" % (scnt[0] % 2)
            scnt[0] += 1
            P.op("act", lambda e: e.copy(out=sg[:], in_=ps[:, 0:512]), reads=[pk], writes=[sgk])
            out_toks.append(P.dma("sp", kc_out[l, i * 128:(i + 1) * 128, :], sg[:], reads=[sgk], writes=["kc_out"]))
        proj_tm(1664, prm_chunks, cbk)

        def cbv_p(ps, pk, t0, i):
            sg, sgk = stg[scnt[0] % 2], "stg%d" % (scnt[0] % 2)
            scnt[0] += 1
            P.op("act", lambda e: e.copy(out=sg[:], in_=ps[:, 0:512]), reads=[pk], writes=[sgk])
            out_toks.append(P.dma("sp", vc_out[l, i * 128:(i + 1) * 128, :], sg[:], reads=[sgk], writes=["vc_out"]))
            P.op("dve", lambda e: e.tensor_copy(out=vP[:, i].rearrange("p h e -> p (h e)"), in_=sg[:]), reads=[sgk], writes=["vP"])

        def cbv_s(ps, pk, t0, i):
            P.op("act", lambda e: e.copy(out=vS[:, NKC + i].rearrange("p h e -> p (h e)"), in_=ps[:, 0:512]), reads=[pk], writes=["vS"])
        proj_tm(2176, prm_chunks, cbv_p)
        proj_tm_v = None
        for (t0, i) in sam_chunks:
            ps, pk = psum()
            for c in range(KC):
                P.op("pe", lambda e: e.matmul(ps[:, 0:512], lhsT=hT[:, c, t0:t0 + 128], rhs=wv[:, c, :], start=(c == 0), stop=(c == KC - 1)),
                     reads=["hT", "wv"], writes=[pk], pe_acc=(c > 0))
            cbv_s(ps, pk, t0, i)

        if STG < 3:
            return
        jobs = []
        for q0 in range(0, LS, 512):
            nq = min(512, LS - q0)
            ks = [("c", c) for c in range(NKC)] + [("s", c) for c in range(LS // 128)]
            jobs.append((q0, nq, ks))
        for i in range(NPS):
            s0 = LS + i * LP
            jobs.append((s0, LP, [("p", i * (LP // 128) + c) for c in range(LP // 128)]))
        pcnt = [0]
        for (q0, nq, ks) in jobs:
            for h in range(4):
                O = [psfix(0), psfix(1)]
                Z = [psfix(2), psfix(3)]
                for ki, (kind, c) in enumerate(ks):
                    if kind == "c":
                        kap = lambda m: kcT[m * 64:(m + 1) * 64, h, c * 128:(c + 1) * 128]
                        vap = vS[:, c, h, :]
                        kkey, vkey = "kcT", "vS"
                    elif kind == "s":
                        kap = lambda m: kT[m * 64:(m + 1) * 64, h, c * 128:(c + 1) * 128]
                        vap = vS[:, NKC + c, h, :]
                        kkey, vkey = "kT", "vS"
                    else:
                        kap = lambda m: kT[m * 64:(m + 1) * 64, h, LS + c * 128:LS + (c + 1) * 128]
                        vap = vP[:, c, h, :]
                        kkey, vkey = "kT", "vP"
                    for m in range(2):
                        ps, pk = psum()
                        P.op("pe", lambda e: e.matmul(ps[:, 0:nq], lhsT=kap(m), rhs=qT[m * 64:(m + 1) * 64, h, q0:q0 + nq], start=True, stop=True),
                             reads=[kkey, "qT"], writes=[pk])
                        pT, pTk = pTs[pcnt[0] % 3], "pT%d" % (pcnt[0] % 3)
                        pcnt[0] += 1
                        P.op("act", lambda e: e.activation(out=pT[:, 0:nq], in_=ps[:, 0:nq], func=AF.Exp, scale=0.125), reads=[pk], writes=[pTk])
                        P.op("pe", lambda e: e.matmul(O[m][0][:, 0:nq], lhsT=vap, rhs=pT[:, 0:nq], start=(ki == 0), stop=(ki == len(ks) - 1)),
                             reads=[vkey, pTk], writes=[O[m][1]], pe_acc=(ki > 0))
                        P.op("pe", lambda e: e.matmul(Z[m][0][:, 0:nq], lhsT=C["ones_b"][:], rhs=pT[:, 0:nq], start=(ki == 0), stop=(ki == len(ks) - 1)),
                             reads=["k_ones_b", pTk], writes=[Z[m][1]], pe_acc=(ki > 0))
                o_m = []
                for m in range(2):
                    rz, rzk = tmp()
                    P.op("dve", lambda e: e.reciprocal(out=rz[:, 0:nq], in_=Z[m][0][:, 0:nq]), reads=[Z[m][1]], writes=[rzk])
                    P.op("dve", lambda e: e.tensor_tensor(out=rz[:, 0:nq], in0=O[m][0][:, 0:nq], in1=rz[:, 0:nq], op=ALU.mult), reads=[O[m][1], rzk], writes=[rzk])
                    o_m.append((rz, rzk))
                P.op("dve", lambda e: e.scalar_tensor_tensor(out=of[:, 0:nq], in0=o_m[1][0][:, 0:nq], scalar=lamv[:, 3:4], in1=o_m[0][0][:, 0:nq],
                                                             op0=ALU.mult, op1=ALU.add), reads=[o_m[0][1], o_m[1][1], "lamv"], writes=["of"])
                rms_stats(lambda c: of[:, 0:nq], ["of"], q0, nq, rstd, "rstd", nchunks=1, scale=1.0 / 128)
                tt, tk = tmp()
                P.op("dve", lambda e: e.tensor_tensor(out=tt[:, 0:nq], in0=of[:, 0:nq], in1=rstd[:, 0:nq], op=ALU.mult), reads=["of", "rstd"], writes=[tk])
                P.op("act", lambda e: e.activation(out=mixT[:, 2 + h, q0:q0 + nq], in_=tt[:, 0:nq], func=AF.Identity, scale=psub[:, 0:1]),
                     reads=[tk, "psub"], writes=["mixT"])
        dbg_out("yb%d" % l, mixT[:, 2:6, :], ["mixT"])

    def hyena_layer(l):
        carver.reset(0, HB0)
        uh = carver.get([128, 6, NT], BF16)

        def cbC(ps, pk, ci, t0, n, cnd):
            P.op("act", lambda e: e.copy(out=uh[:, ci, t0:t0 + n], in_=ps[:, 0:n]), reads=[pk], writes=["uh"])
        proj_fm(w_in[l], 2688, 6, hT, "hT", cbC)
        barrier()
        carver.lim = MB0
        hw3b = carver.get([64, 1024], BF16)
        ndecb = carver.get([128, 256])
        for (t, src) in ((hconv, hy_conv), (hw1, hy_w1), (hvec, hy_vec), (hw2, hy_w2), (hbias, hy_bias)):
            P.dma("sp", t[:], src[l], writes=["hyp"])
        P.dma("sp", ndecb, hy_decb[l], writes=["ndecb"])
        P.dma("pool", hw3b, hy_w3[l], writes=["hyp"])
        EV = carver.get([128, 2, 1024])
        P.op("dve", lambda e: e.tensor_scalar(out=EV[:, 0, 0:256], in0=ndecb, scalar1=-1.0, scalar2=None, op0=ALU.mult), reads=["ndecb"], writes=["EV"])
        P.op("dve", lambda e: e.tensor_tensor(out=ndecb, in0=ndecb, in1=EV[:, 0, 0:256], op=ALU.min), reads=["ndecb", "EV"], writes=["ndecb"])
        for (col, bi, fac) in ((0, None, 0.5), (1, None, 0.25), (2, 0, 0.5), (3, 0, 0.25), (4, 2, 0.5), (5, 2, 0.25)):
            if bi is None:
                P.op("dve", lambda e: e.tensor_scalar(out=hsc[:, col:col + 1], in0=hvec[:, 1:2], scalar1=fac, scalar2=None, op0=ALU.mult), reads=["hyp"], writes=["hsc"])
            else:
                P.op("dve", lambda e: e.scalar_tensor_tensor(out=hsc[:, col:col + 1], in0=hvec[:, 1:2], scalar=fac, in1=hvec[:, bi:bi + 1], op0=ALU.mult, op1=ALU.mult),
                     reads=["hyp"], writes=["hsc"])
        XS = carver.get([128, 3 * LS])
        tmpc = XS[:, 0:LS]
        embT = XS[0:33, LS:2 * LS]
        h1 = XS[0:64, 2 * LS:3 * LS]
        for c in range(6):
            for (s0, L, cnd) in cfg.seqs:
                P.op("act", lambda e: e.activation(out=tmpc[:, 0:L], in_=uh[:, c, s0:s0 + L], func=AF.Identity, scale=hconv[:, c, 1:2], bias=hconv[:, c, 3:4]),
                     reads=["uh", "hyp"], writes=["XS"])
                P.op("dve", lambda e: e.scalar_tensor_tensor(out=tmpc[:, 1:L], in0=uh[:, c, s0:s0 + L - 1], scalar=hconv[:, c, 0:1], in1=tmpc[:, 1:L], op0=ALU.mult, op1=ALU.add),
                     reads=["uh", "hyp", "XS"], writes=["XS"])
                P.op("dve", lambda e: e.scalar_tensor_tensor(out=tmpc[:, 0:L - 1], in0=uh[:, c, s0 + 1:s0 + L], scalar=hconv[:, c, 2:3], in1=tmpc[:, 0:L - 1], op0=ALU.mult, op1=ALU.add),
                     reads=["uh", "hyp", "XS"], writes=["XS"])
                P.op("act", lambda e: e.copy(out=uh[:, c, s0:s0 + L], in_=tmpc[:, 0:L]), reads=["XS"], writes=["uh"])
        z1 = carver.get([128, 2, NT], BF16)
        h2 = carver.get([64, LS], BF16)
        SCM = LS // 128
        FCM = (LS + 1 + 127) // 128
        Zfull = carver.get([128, max(SCM * 768, (LP // 128) * (512 + NPS * 256))], BF16)
        FYM = max(FCM, NPS * ((LP + 1 + 127) // 128))
        YRf = carver.get([128, FYM * 256], BF16)
        YIf = carver.get([128, FYM * 256], BF16)
        Hc = carver.get([128, 2, 256])
        Eexp = carver.get([128, 256])
        rsb = carver.get([128, 512])
        sqt = carver.get([128, 512], BF16)
        tlT = carver.get([128, SCM])
        tabs = XS.bitcast(BF16)
        TBN = (3 * LS * 2) // 4
        tabv = [tabs[:, i * TBN:(i + 1) * TBN] for i in range(4)]
        for (nm, L, s0, ns) in (("s", LS, 0, 1), ("p", LP, LS, NPS)):
            SC = L // 128
            FC = (L + 1 + 127) // 128
            ncols = 512 + ns * 256
            Z = Zfull[:, 0:SC * ncols].rearrange("p (s c) -> p s c", s=SC)
            YR = [YRf[:, j * FC * 256:(j + 1) * FC * 256].rearrange("p (f c) -> p f c", f=FC) for j in range(ns)]
            YI = [YIf[:, j * FC * 256:(j + 1) * FC * 256].rearrange("p (f c) -> p f c", f=FC) for j in range(ns)]
            barrier()
            P.dma("sp", embT[:, 0:L], din["c_embT_" + nm][:], writes=["XS"])
            P.dma("sp", tlT[:, 0:SC], din["c_tlinT_" + nm][:], writes=["tlT"])
            tls = [(a, min(512, L - a)) for a in range(0, L, 512)]

            def sin_layer(w_t, kdim, src, srck, dst, dstk, c0):
                for (a, n) in tls:
                    ps, pk = psum()
                    P.op("pe", lambda e: e.matmul(ps[0:64, 0:n], lhsT=w_t, rhs=src[0:kdim, a:a + n], start=True, stop=True), reads=["hyp", srck], writes=[pk])
                    s2, s2k = tmp()
                    s4, s4k = tmp()
                    P.op("act", lambda e: e.activation(out=s2[0:64, 0:n], in_=ps[0:64, 0:n], func=AF.Sin, scale=hsc[:, 0:1], bias=hsc[:, c0:c0 + 1]), reads=[pk, "hsc"], writes=[s2k])
                    P.op("act", lambda e: e.activation(out=s4[0:64, 0:n], in_=ps[0:64, 0:n], func=AF.Sin, scale=hsc[:, 1:2], bias=hsc[:, c0 + 1:c0 + 2]), reads=[pk, "hsc"], writes=[s4k])
                    P.op("dve", lambda e: e.tensor_tensor(out=s4[0:64, 0:n], in0=s4[0:64, 0:n], in1=s4[0:64, 0:n], op=ALU.mult), reads=[s4k], writes=[s4k])
                    P.op("dve", lambda e: e.tensor_scalar(out=s4[0:64, 0:n], in0=s4[0:64, 0:n], scalar1=-2.0, scalar2=1.0, op0=ALU.mult, op1=ALU.add), reads=[s4k], writes=[s4k])
                    P.op("dve", lambda e: e.scalar_tensor_tensor(out=dst[:, a:a + n], in0=s2[0:64, 0:n], scalar=2.0, in1=s4[0:64, 0:n], op0=ALU.mult, op1=ALU.mult),
                         reads=[s2k, s4k], writes=[dstk])
            sin_layer(hw1[:], 33, embT, "XS", h1, "XS", 2)
            sin_layer(hw2[:], 64, h1, "XS", h2, "h2", 4)
            barrier()
            fwdC, fwdS, invC, invS = (din["c_fwdC_" + nm], din["c_fwdS_" + nm], din["c_invC_" + nm], din["c_invS_" + nm])
            for o in range(2):
                psS, psSk = psfix(0)
                for sc in range(SC):
                    ps, pk = psum()
                    P.op("pe", lambda e: e.matmul(ps[:, 0:512], lhsT=h2[:, sc * 128:(sc + 1) * 128], rhs=hw3b[:, o * 512:(o + 1) * 512], start=True, stop=True),
                         reads=["h2", "hyp"], writes=[pk])
                    P.op("act", lambda e: e.activation(out=Eexp, in_=ndecb, func=AF.Exp, scale=tlT[:, sc:sc + 1]), reads=["ndecb", "tlT"], writes=["Eexp"])
                    P.op("dve", lambda e: e.tensor_tensor(out=Z[:, sc, 0:512].rearrange("p (d c) -> p d c", d=2), in0=ps[:, 0:512].rearrange("p (d c) -> p d c", d=2),
                                                          in1=mkap(Eexp[:, 0:1], 0, [[0, 2], [1, 256]]), op=ALU.mult), reads=[pk, "Eexp"], writes=["Zf"])
                    if sc == 0:
                        P.op("dve", lambda e: e.memset(Z[0:1, 0, 256:512], 0.0), reads=["Zf"], writes=["Zf"])
                    P.op("act", lambda e: e.activation(out=sqt, in_=Z[:, sc, 0:512], func=AF.Square), reads=["Zf"], writes=["sqt"])
                    P.op("pe", lambda e: e.matmul(psS[:, 0:512], lhsT=C["ones_b"][:], rhs=sqt, start=(sc == 0), stop=(sc == SC - 1)),
                         reads=["sqt", "k_ones_b"], writes=[psSk], pe_acc=(sc > 0))
                P.op("act", lambda e: e.copy(out=rsb, in_=psS[:, 0:512]), reads=[psSk], writes=["rsb"])
                P.op("dve", lambda e: e.tensor_tensor(out=rsb[:, 0:256], in0=rsb[:, 0:256], in1=rsb[:, 256:512], op=ALU.add), reads=["rsb"], writes=["rsb"])
                P.op("act", lambda e: e.activation(out=rsb[:, 0:256], in_=rsb[:, 0:256], func=AF.Ln, bias=epsc(1e-6)), reads=["rsb", "epsc"], writes=["rsb"])
                P.op("act", lambda e: e.activation(out=rsb[:, 0:256], in_=rsb[:, 0:256], func=AF.Exp, scale=-0.5), reads=["rsb"], writes=["rsb"])
                for sc in range(SC):
                    P.op("dve", lambda e: e.tensor_tensor(out=Z[:, sc, 0:512].rearrange("p (d c) -> p d c", d=2), in0=Z[:, sc, 0:512].rearrange("p (d c) -> p d c", d=2),
                                                          in1=mkap(rsb[:, 0:1], 0, [[0, 2], [1, 256]]), op=ALU.mult), reads=["Zf", "rsb"], writes=["Zf"])
                zsrc_all, zk = (uh[:, 4:6, :], "uh") if o == 0 else (z1, "z1")
                for j in range(ns):
                    for sc in range(SC):
                        for c in range(2):
                            t0 = s0 + j * L + sc * 128
                            ps, pk = psum()
                            pst = ps[:, 0:64].bitcast(BF16)
                            P.op("pe", lambda e: e.transpose(pst, zsrc_all[:, c, t0:t0 + 128], C["ident_b"][:]), reads=[zk, "k_ident_b"], writes=[pk])
                            P.op("act", lambda e: e.copy(out=Z[:, sc, 512 + j * 256 + c * 128:512 + j * 256 + (c + 1) * 128], in_=pst), reads=[pk], writes=["Zd"])
                blocks = [(cb, min(512, ncols - cb)) for cb in range(0, ncols, 512)]
                for fc in range(FC):
                    tC, tS = tabv[(fc % 2) * 2], tabv[(fc % 2) * 2 + 1]
                    tCk, tSk = "tab%d" % ((fc % 2) * 2), "tab%d" % ((fc % 2) * 2 + 1)
                    tCv = tC[:, 0:SC * 128].rearrange("p (s f) -> p s f", s=SC)
                    tSv = tS[:, 0:SC * 128].rearrange("p (s f) -> p s f", s=SC)
                    P.dma("sp", tCv, fwdC[fc], writes=[tCk])
                    P.dma("act", tSv, fwdS[fc], writes=[tSk])
                    for sc in range(SC):
                        for ri, (tv, tk_) in enumerate(((tCv, tCk), (tSv, tSk))):
                            for bi_, (cb, w) in enumerate(blocks):
                                bank, bk = psfix(ri * 2 + bi_)
                                P.op("pe", lambda e: e.matmul(bank[:, 0:w], lhsT=tv[:, sc, :], rhs=Z[:, sc, cb:cb + w], start=(sc == 0), stop=(sc == SC - 1)),
                                     reads=[tk_, "Zf", "Zd"], writes=[bk], pe_acc=(sc > 0))
                    for ri in range(2):
                        for bi_, (cb, w) in enumerate(blocks):
                            bank, bk = psfix(ri * 2 + bi_)
                            P.op("act", lambda e: e.copy(out=EV[:, ri, cb:cb + w], in_=bank[:, 0:w]), reads=[bk], writes=["EV"])
                    P.op("dve", lambda e: e.tensor_tensor(out=Hc[:, 0, :], in0=EV[:, 0, 0:256], in1=EV[:, 0, 256:512], op=ALU.add), reads=["EV"], writes=["Hc"])
                    P.op("dve", lambda e: e.tensor_tensor(out=Hc[:, 1, :], in0=EV[:, 1, 0:256], in1=EV[:, 1, 256:512], op=ALU.subtract), reads=["EV"], writes=["Hc"])
                    for j in range(ns):
                        vre = EV[:, 0, 512 + j * 256:512 + (j + 1) * 256]
                        vim = EV[:, 1, 512 + j * 256:512 + (j + 1) * 256]
                        t1, t1k = tmp()
                        t2, t2k = tmp()
                        P.op("dve", lambda e: e.tensor_tensor(out=t1[:, 0:256], in0=vre, in1=Hc[:, 0, :], op=ALU.mult), reads=["EV", "Hc"], writes=[t1k])
                        P.op("dve", lambda e: e.tensor_tensor(out=t2[:, 0:256], in0=vim, in1=Hc[:, 1, :], op=ALU.mult), reads=["EV", "Hc"], writes=[t2k])
                        P.op("dve", lambda e: e.tensor_tensor(out=YR[j][:, fc, :], in0=t1[:, 0:256], in1=t2[:, 0:256], op=ALU.subtract), reads=[t1k, t2k], writes=["YR"])
                        P.op("dve", lambda e: e.tensor_tensor(out=t1[:, 256:512], in0=vre, in1=Hc[:, 1, :], op=ALU.mult), reads=["EV", "Hc"], writes=[t1k])
                        P.op("dve", lambda e: e.tensor_tensor(out=t2[:, 256:512], in0=vim, in1=Hc[:, 0, :], op=ALU.mult), reads=["EV", "Hc"], writes=[t2k])
                        P.op("dve", lambda e: e.tensor_tensor(out=YI[j][:, fc, :], in0=t1[:, 256:512], in1=t2[:, 256:512], op=ALU.add), reads=[t1k, t2k], writes=["YI"])
                ttl = [(a, min(512, L - a)) for a in range(0, L, 512)]
                accs = {}
                bi2 = 0
                for j in range(ns):
                    for c in range(2):
                        for ti in range(len(ttl)):
                            accs[(j, c, ti)] = (PS[bi2], "ps%d" % bi2)
                            bi2 += 1
                assert bi2 <= 8
                for fc in range(FC):
                    tC, tS = tabv[(fc % 2) * 2], tabv[(fc % 2) * 2 + 1]
                    tCk, tSk = "tab%d" % ((fc % 2) * 2), "tab%d" % ((fc % 2) * 2 + 1)
                    P.dma("sp", tC[:, 0:L], invC[fc], writes=[tCk])
                    P.dma("act", tS[:, 0:L], invS[fc], writes=[tSk])
                    for j in range(ns):
                        for c in range(2):
                            for ti, (a, w) in enumerate(ttl):
                                acc, acck = accs[(j, c, ti)]
                                P.op("pe", lambda e: e.matmul(acc[:, 0:w], lhsT=YR[j][:, fc, c * 128:(c + 1) * 128], rhs=tC[:, a:a + w], start=(fc == 0), stop=False),
                                     reads=["YR", tCk], writes=[acck], pe_acc=(fc > 0))
                                P.op("pe", lambda e: e.matmul(acc[:, 0:w], lhsT=YI[j][:, fc, c * 128:(c + 1) * 128], rhs=tS[:, a:a + w], start=False, stop=(fc == FC - 1)),
                                     reads=["YI", tSk], writes=[acck], pe_acc=True)
                for j in range(ns):
                    for c in range(2):
                        for ti, (a, w) in enumerate(ttl):
                            acc, acck = accs[(j, c, ti)]
                            g0 = s0 + j * L + a
                            zs = zsrc_all[:, c, g0:g0 + w]
                            xg = uh[:, (0 if o == 0 else 2) + c, g0:g0 + w]
                            tt, tk = tmp()
                            P.op("dve", lambda e: e.scalar_tensor_tensor(out=tt[:, 0:w], in0=zs, scalar=hbias[:, o, c:c + 1], in1=acc[:, 0:w], op0=ALU.mult, op1=ALU.add),
                                 reads=[zk, "hyp", acck], writes=[tk])
                            dstz, dk = (z1[:, c, g0:g0 + w], "z1") if o == 0 else (mixT[:, 6 + c, g0:g0 + w], "mixT")
                            P.op("dve", lambda e: e.tensor_tensor(out=dstz, in0=tt[:, 0:w], in1=xg, op=ALU.mult), reads=[tk, "uh"], writes=[dk])
                barrier()
        dbg_out("yc%d" % l, mixT[:, 6:8, :], ["mixT"])

    for l in range(DEPTH):
        lam0 = lam_init_of(l)
        P.dma("sp", bmod[:], b_modT[l], writes=["bmod"])
        P.dma("sp", gn[:], gains[l], writes=["gains"])
        for j in range(48):
            wm, wk = load_w(w_mod[l][:, j * 128:(j + 1) * 128], KC)
            ps, pk = psum()
            for c in range(KC):
                P.op("pe", lambda e, wm=wm, c=c, ps=ps: e.matmul(ps[:, 0:2], lhsT=wm[:, c, :], rhs=scv[:, c, :],
                                                                start=(c == 0), stop=(c == KC - 1)),
                     reads=[wk, "scvec"], writes=[pk], pe_acc=(c > 0))
            P.op("dve", lambda e, ps=ps, j=j: e.tensor_scalar(out=modT[:, j, :], in0=ps[:, 0:2], scalar1=bmod[:, j:j + 1],
                                                              scalar2=None, op0=ALU.add),
                 reads=[pk, "bmod"], writes=["modT"])
        for (o, jsc, gi) in ((0, 8, 0), (2, 32, 2)):
            for cnd in range(2):
                P.op("dve", lambda e, o=o, jsc=jsc, gi=gi, cnd=cnd: e.scalar_tensor_tensor(
                    out=sc_all[:, o, :, cnd], in0=modT[:, jsc:jsc + 8, cnd], scalar=1.0, in1=gn[:, gi, :],
                    op0=ALU.add, op1=ALU.mult), reads=["modT", "gains"], writes=["sc_all"])
        for (o, jgt, gi) in ((1, 16, 1), (3, 40, 3)):
            for cnd in range(2):
                P.op("dve", lambda e, o=o, jgt=jgt, gi=gi, cnd=cnd: e.tensor_tensor(
                    out=sc_all[:, o, :, cnd], in0=modT[:, jgt:jgt + 8, cnd], in1=gn[:, gi, :], op=ALU.mult),
                    reads=["modT", "gains"], writes=["sc_all"])

        def norm_mod(src_tile_fn, src_keys, dst, dst_key_fn, sci, shj):
            for (t0, n, cnd) in cfg.tiles:
                rms_stats(lambda c: src_tile_fn(c, t0, n), src_keys, t0, n, rstd, "rstd")
                for c in range(KC):
                    tt, tk = tmp()
                    P.op("dve", lambda e, c=c, tt=tt: e.tensor_tensor(out=tt[:, 0:n], in0=src_tile_fn(c, t0, n), in1=rstd[:, 0:n],
                                                                     op=ALU.mult), reads=src_keys + ["rstd"], writes=[tk])
                    P.op("act", lambda e, c=c, tt=tt: e.activation(out=dst[:, c, t0:t0 + n], in_=tt[:, 0:n], func=AF.Identity,
                                                                  scale=sc_all[:, sci, c, cnd:cnd + 1],
                                                                  bias=modT[:, shj + c, cnd:cnd + 1]),
                         reads=[tk, "sc_all", "modT"], writes=[dst_key_fn(c, t0)])

        norm_mod(lambda c, t0, n: xT[:, c, t0:t0 + n], ["xT"], hT, lambda c, t0: "hT", 0, 0)
        if l == 0:
            dbg_out("h0", hT, ["hT"])
        P.dma("sp", x_spill[:], xT, reads=["xT"], writes=["x_spill"])
        barrier()

        def proj_fm(wsrc, col0, ncols_chunks, rhs, rhs_key, cb, kchunks=KC):
            for ci in range(ncols_chunks):
                wt, wk = load_w(wsrc[:, col0 + ci * 128: col0 + (ci + 1) * 128], kchunks)
                for (t0, n, cnd) in cfg.tiles:
                    ps, pk = psum()
                    for c in range(kchunks):
                        P.op("pe", lambda e, wt=wt, c=c, ps=ps, t0=t0, n=n: e.matmul(
                            ps[:, 0:n], lhsT=wt[:, c, :], rhs=rhs[:, c, t0:t0 + n], start=(c == 0), stop=(c == kchunks - 1)),
                            reads=[wk, rhs_key], writes=[pk], pe_acc=(c > 0))
                    cb(ps, pk, ci, t0, n, cnd)

        carver.reset(0, HB0)
        rkv = lor = None
        if "rwkv" in mixers:
            rkv = carver.get([128, 6, NT], BF16)
            lor = carver.get([128, 3, NT], BF16)

        def cbA(ps, pk, ci, t0, n, cnd):
            if ci < 6:
                P.op("act", lambda e: e.copy(out=rkv[:, ci, t0:t0 + n], in_=ps[:, 0:n]), reads=[pk], writes=["rkv"])
            else:
                fn = {6: AF.Tanh, 7: AF.Identity, 8: AF.Sigmoid}[ci]
                P.op("act", lambda e: e.activation(out=lor[:, ci - 6, t0:t0 + n], in_=ps[:, 0:n], func=fn),
                     reads=[pk], writes=["lor"])
        if "rwkv" in mixers:
            proj_fm(w_in[l], 0, 9, hT, "hT", cbA)
            rwkv_layer(l, rkv, lor)
        else:
            P.op("pool", lambda e: e.memset(mixT[:, 0:2, :], 0.0), writes=["mixT"])

        if "attn" in mixers:
            attn_layer(l, lam0)
        else:
            P.op("pool", lambda e: e.memset(mixT[:, 2:6, :], 0.0), writes=["mixT"])

        if "hyena" in mixers:
            hyena_layer(l)
        else:
            P.op("pool", lambda e: e.memset(mixT[:, 6:8, :], 0.0), writes=["mixT"])

        barrier()
        P.dma("sp", xT, x_spill[:], reads=["x_spill"], writes=["xT"])
        carver.reset(XB, MB0)
        mo = carver.get([128, KC, 512])
        for (t0, n, cnd) in cfg.tiles:
            for ci in range(KC):
                wt, wk = load_w(w_out[l][:, ci * 128:(ci + 1) * 128], KC)
                ps, pk = psum()
                for c in range(KC):
                    P.op("pe", lambda e, wt=wt, c=c, ps=ps: e.matmul(ps[:, 0:n], lhsT=wt[:, c, :], rhs=mixT[:, c, t0:t0 + n],
                                                                    start=(c == 0), stop=(c == KC - 1)),
                         reads=[wk, "mixT"], writes=[pk], pe_acc=(c > 0))
                P.op("act", lambda e, ci=ci, ps=ps: e.copy(out=mo[:, ci, 0:n], in_=ps[:, 0:n]), reads=[pk], writes=["mo"])
            rms_stats(lambda c: mo[:, c, 0:n], ["mo"], t0, n, rstd, "rstd")
            for c in range(KC):
                tt, tk = tmp()
                P.op("dve", lambda e, c=c, tt=tt: e.tensor_tensor(out=tt[:, 0:n], in0=mo[:, c, 0:n], in1=rstd[:, 0:n], op=ALU.mult),
                     reads=["mo", "rstd"], writes=[tk])
                P.op("dve", lambda e, c=c, tt=tt: e.scalar_tensor_tensor(
                    out=xT[:, c, t0:t0 + n], in0=tt[:, 0:n], scalar=sc_all[:, 1, c, cnd:cnd + 1], in1=xT[:, c, t0:t0 + n],
                    op0=ALU.mult, op1=ALU.add), reads=[tk, "sc_all", "xT"], writes=["xT"])
        if l == 0:
            dbg_out("x1", xT, ["xT"])

        barrier()
        carver.reset(XB, BIGB)
        h2 = carver.get([128, KC, 512], BF16)
        f1 = carver.get([128, 32, 512], BF16)
        fo = carver.get([128, KC, 512])
        wff2 = [carver.get([128, 32, 128], BF16) for _ in range(2)]
        for (t0, n, cnd) in cfg.tiles:
            rms_stats(lambda c: xT[:, c, t0:t0 + n], ["xT"], t0, n, rstd, "rstd")
            for c in range(KC):
                tt, tk = tmp()
                P.op("dve", lambda e, c=c, tt=tt: e.tensor_tensor(out=tt[:, 0:n], in0=xT[:, c, t0:t0 + n], in1=rstd[:, 0:n],
                                                                 op=ALU.mult), reads=["xT", "rstd"], writes=[tk])
                P.op("act", lambda e, c=c, tt=tt: e.activation(out=h2[:, c, 0:n], in_=tt[:, 0:n], func=AF.Identity,
                                                              scale=sc_all[:, 2, c, cnd:cnd + 1], bias=modT[:, 24 + c, cnd:cnd + 1]),
                     reads=[tk, "sc_all", "modT"], writes=["h2"])
            for ci in range(32):
                wt, wk = load_w(w_ff1[l][:, ci * 128:(ci + 1) * 128], KC)
                ps, pk = psum()
                for c in range(KC):
                    P.op("pe", lambda e, wt=wt, c=c, ps=ps: e.matmul(ps[:, 0:n], lhsT=wt[:, c, :], rhs=h2[:, c, 0:n],
                                                                    start=(c == 0), stop=(c == KC - 1)),
                         reads=[wk, "h2"], writes=[pk], pe_acc=(c > 0))
                tt, tk = tmp()
                P.op("act", lambda e, ps=ps, tt=tt: e.activation(out=tt[:, 0:n], in_=ps[:, 0:n], func=AF.Relu), reads=[pk], writes=[tk])
                P.op("dve", lambda e, ci=ci, tt=tt: e.tensor_tensor(out=f1[:, ci, 0:n], in0=tt[:, 0:n], in1=tt[:, 0:n], op=ALU.mult),
                     reads=[tk], writes=["f1"])
            for ci in range(KC):
                wt, wk = wff2[ci % 2], "wff2_%d" % (ci % 2)
                load_w(w_ff2[l][:, ci * 128:(ci + 1) * 128], 32, dst=wt, key=wk)
                ps, pk = psum()
                for c in range(32):
                    P.op("pe", lambda e, wt=wt, c=c, ps=ps: e.matmul(ps[:, 0:n], lhsT=wt[:, c, :], rhs=f1[:, c, 0:n],
                                                                    start=(c == 0), stop=(c == 31)),
                         reads=[wk, "f1"], writes=[pk], pe_acc=(c > 0))
                P.op("act", lambda e, ci=ci, ps=ps: e.copy(out=fo[:, ci, 0:n], in_=ps[:, 0:n]), reads=[pk], writes=["fo"])
            rms_stats(lambda c: fo[:, c, 0:n], ["fo"], t0, n, rstd, "rstd")
            for c in range(KC):
                tt, tk = tmp()
                P.op("dve", lambda e, c=c, tt=tt: e.tensor_tensor(out=tt[:, 0:n], in0=fo[:, c, 0:n], in1=rstd[:, 0:n], op=ALU.mult),
                     reads=["fo", "rstd"], writes=[tk])
                P.op("dve", lambda e, c=c, tt=tt: e.scalar_tensor_tensor(
                    out=xT[:, c, t0:t0 + n], in0=tt[:, 0:n], scalar=sc_all[:, 3, c, cnd:cnd + 1], in1=xT[:, c, t0:t0 + n],
                    op0=ALU.mult, op1=ALU.add), reads=[tk, "sc_all", "xT"], writes=["xT"])

    out_toks.append(P.dma("sp", yT_out[:], xT, reads=["xT"], writes=["yT_out"]))
    P.finish_wait("sp", out_toks)
    P.emit(st)
    st.close()
    return nc


def _lay(a):
    return np.ascontiguousarray(a)


def prep_inputs(cfg, inp, consts):
    D = cfg.depth
    f = np.float32
    shared = {}
    for k, v in consts.items():
        shared["c_" + k] = v
    shared["w_mod"] = _lay(inp["w_mod"][:D])
    shared["b_modT"] = _lay(inp["b_mod"][:D].reshape(D, 48, 128).transpose(0, 2, 1))
    g4 = np.stack([inp["g_mix_pre"][:D], inp["g_mix_post"][:D], inp["g_ffn_pre"][:D], inp["g_ffn_post"][:D]], axis=1)
    shared["gains"] = _lay(g4.reshape(D, 4, 8, 128).transpose(0, 3, 1, 2))
    for k in ("w_in", "w_out", "w_ff1", "w_ff2"):
        shared[k] = _lay(inp[k][:D])
    shared["rw_conv"] = _lay(inp["rwkv_conv"][:D].reshape(D, 3, 6, 128).transpose(0, 3, 2, 1))
    shared["rw_w0"] = _lay(inp["rwkv_w0"][:D].reshape(D, 2, 2, 128).transpose(0, 3, 1, 2))
    shared["rw_a0"] = _lay(inp["rwkv_a0"][:D].reshape(D, 2, 2, 128).transpose(0, 3, 1, 2))
    shared["rw_w2"] = _lay(inp["rwkv_w2"][:D].reshape(D, 128, 256))
    shared["rw_a2"] = _lay(inp["rwkv_a2"][:D].reshape(D, 128, 256))
    shared["rw_g2"] = _lay(inp["rwkv_g2"][:D])
    v3 = np.stack([inp["rwkv_kk"][:D], inp["rwkv_ka"][:D], inp["rwkv_rk"][:D].reshape(D, 256)], axis=1)
    shared["rw_vec"] = _lay(v3.reshape(D, 3, 2, 128).transpose(0, 3, 1, 2))
    ln = np.stack([inp["rwkv_ln_w"][:D], inp["rwkv_ln_b"][:D]], axis=1)
    shared["rw_ln"] = _lay(ln.reshape(D, 2, 4, 64).transpose(0, 3, 1, 2))
    lam = np.stack([inp["diff_lq1"][:D], inp["diff_lk1"][:D], inp["diff_lq2"][:D], inp["diff_lk2"][:D]], axis=1)
    shared["dlam"] = _lay(np.broadcast_to(lam[:, None], (D, 128, 4, 64)))
    shared["dsub"] = _lay(inp["diff_subln"][:D].reshape(D, 128, 1))
    hc = np.concatenate([inp["hy_conv_w"][:D], inp["hy_conv_b"][:D][:, None]], axis=1)
    shared["hy_conv"] = _lay(hc.reshape(D, 4, 6, 128).transpose(0, 3, 2, 1))
    shared["hy_w1"] = _lay(inp["hy_w1"][:D])
    shared["hy_vec"] = _lay(np.stack([inp["hy_b1"][:D], inp["hy_freq"][:D], inp["hy_b2"][:D]], axis=2))
    shared["hy_w2"] = _lay(inp["hy_w2"][:D])
    shared["hy_w3"] = _lay(inp["hy_w3"][:D])
    shared["hy_decb"] = _lay(np.broadcast_to(inp["hy_decay"][:D][:, None, :], (D, 128, 256)))
    shared["hy_bias"] = _lay(inp["hy_bias"][:D].reshape(D, 2, 2, 128).transpose(0, 3, 1, 2))
    maps = []
    nb_s = inp["x_sample"].shape[0]
    for core in range(8):
        b = core % nb_s
        m = dict(shared)
        xs = inp["x_sample"][b]
        xp = inp["x_prompt"][cfg.nps * core:cfg.nps * (core + 1)].reshape(-1, 1024)
        x = np.concatenate([xs, xp], axis=0)
        m["xT"] = _lay(x.T.reshape(8, 128, cfg.nt).transpose(1, 0, 2))
        cv2 = np.stack([inp["c"][b], inp["c_ctx"]], axis=1)
        m["cvecT"] = _lay(cv2.reshape(8, 128, 2).transpose(1, 0, 2))
        st0 = inp["state_rwkv"][b][:D]
        m["st_in"] = _lay(st0.reshape(D, 2, 2, 2, 64, 64).transpose(0, 3, 5, 2, 1, 4).reshape(D, 128, 4, 64))
        ck = inp["cache_k"][b][:D]
        m["ckT"] = _lay(ck.reshape(D, cfg.past, 4, 128).transpose(0, 2, 3, 1))
        cvv = inp["cache_v"][b][:D]
        m["cv"] = _lay(cvv.reshape(D, cfg.past // 128, 128, 4, 128).transpose(0, 2, 1, 3, 4))
        maps.append({k: np.ascontiguousarray(v) for k, v in m.items()})
    return maps


_CACHE = {}


def kernel(**inputs):
    inp = {k: np.asarray(v) for k, v in inputs.items()}
    cfg = Cfg()
    consts = host_consts(cfg)
    if "nc" not in _CACHE:
        _CACHE["nc"] = build_program(cfg, consts, mixers=("rwkv", "attn", "hyena"))
    nc = _CACHE["nc"]
    maps = prep_inputs(cfg, inp, consts)
    res = run_bass_kernel_spmd(nc, maps, core_ids=list(range(8)))
    R = res.results
    D = cfg.depth
    B = inp["x_prompt"].shape[0]
    y_prompt = np.zeros((B, cfg.lp, 1024), np.float32)
    y_sample = np.zeros((inp["x_sample"].shape[0], cfg.ls, 1024), np.float32)
    new_state = np.zeros((B, D, 2, 4, 64, 64), np.float32)
    new_k = np.zeros((B, D, cfg.lp, 4, 2, 64), np.float32)
    new_v = np.zeros((B, D, cfg.lp, 4, 128), np.float32)
    for core in range(8):
        yT = np.asarray(R[core]["yT"])
        y = yT.transpose(1, 0, 2).reshape(1024, cfg.nt).T
        if core < y_sample.shape[0]:
            y_sample[core] = y[:cfg.ls]
        for i in range(cfg.nps):
            bp = cfg.nps * core + i
            y_prompt[bp] = y[cfg.ls + i * cfg.lp: cfg.ls + (i + 1) * cfg.lp]
            new_state[bp] = np.asarray(R[core]["st_out"])[:, i]
            new_k[bp] = np.asarray(R[core]["kc_out"])[:, i * cfg.lp:(i + 1) * cfg.lp].reshape(D, cfg.lp, 4, 2, 64)
            new_v[bp] = np.asarray(R[core]["vc_out"])[:, i * cfg.lp:(i + 1) * cfg.lp].reshape(D, cfg.lp, 4, 128)
    return (y_prompt, y_sample, new_state, new_k, new_v)
```

```python
import math
import os
from contextlib import ExitStack
import numpy as np
import ml_dtypes
import concourse.bass as bass
import concourse.mybir as mybir
from concourse.bass_types import AP
from concourse.bass_utils import run_bass_kernel_spmd

F32 = mybir.dt.float32
BF16 = mybir.dt.bfloat16
AF = mybir.ActivationFunctionType
ALU = mybir.AluOpType
AX = mybir.AxisListType

ENGS = ("pe", "act", "dve", "pool", "sp")
NDSEM = 16


class _Rec:
    def __getattr__(self, name):
        def f(*a, **k):
            self.call = (name, a, k)
            return self
        return f


class Prog:
    def __init__(self, nc):
        self.nc = nc
        self.ops = {e: [] for e in ENGS}
        self.cnt = {e: 0 for e in ENGS}
        self.waited = {e: {} for e in ENGS}
        self.lastw = {}
        self.readers = {}
        self.dq = {e: {"next": 0, "use": [0] * NDSEM} for e in ("sp", "act", "pool")}
        self.sems = {}
        self.nps = 0

    def _need(self, e, deps):
        waits = []
        w = self.waited[e]
        best = {}
        for (sk, v) in deps:
            if v > best.get(sk, 0):
                best[sk] = v
        for sk, v in best.items():
            if w.get(sk, 0) >= v:
                continue
            w[sk] = v
            waits.append((sk, v))
        return waits

    def _deps(self, reads, writes):
        deps = []
        for k in reads:
            t = self.lastw.get(k)
            if t is not None:
                deps.append(t)
        for k in writes:
            t = self.lastw.get(k)
            if t is not None:
                deps.append(t)
            for sk, v in self.readers.get(k, {}).items():
                deps.append((sk, v))
        return deps

    def _commit(self, tok, reads, writes):
        for k in reads:
            r = self.readers.setdefault(k, {})
            if tok[1] > r.get(tok[0], 0):
                r[tok[0]] = tok[1]
        for k in writes:
            self.lastw[k] = tok
            self.readers[k] = {}

    def op(self, e, fn, reads=(), writes=(), pe_acc=False):
        rec = _Rec()
        fn(rec)
        call = rec.call
        fn = lambda eh, call=call: getattr(eh, call[0])(*call[1], **call[2])
        deps = self._deps(reads, writes)
        if pe_acc:
            deps = [d for d in deps if d[0] != ("c", "pe")]
        waits = self._need(e, deps)
        self.cnt[e] += 1
        tok = (("c", e), self.cnt[e])
        self.ops[e].append((waits, fn, ("c", e)))
        self._commit(tok, reads, writes)
        return tok

    def dma(self, q, out, in_, reads=(), writes=(), **kw):
        d = self.dq[q]
        i = d["next"]
        d["next"] = (i + 1) % NDSEM
        deps = self._deps(reads, writes)
        if d["use"][i] > 0:
            deps.append((("d", q, i), 16 * d["use"][i]))
        waits = self._need(q, deps)
        d["use"][i] += 1
        tok = (("d", q, i), 16 * d["use"][i])

        def fn(eh, out=out, in_=in_, kw=kw):
            return eh.dma_start(out=out, in_=in_, **kw)
        self.ops[q].append((waits, fn, ("d", q, i)))
        self._commit(tok, reads, writes)
        return tok

    def barrier(self):
        toks = [(("c", e), self.cnt[e]) for e in ENGS if self.cnt[e] > 0]
        for q in ("sp", "act", "pool"):
            for i in range(NDSEM):
                u = self.dq[q]["use"][i]
                if u > 0:
                    toks.append((("d", q, i), 16 * u))
        for e in ENGS:
            waits = self._need(e, list(toks))
            if waits:
                self.ops[e].append((waits, None, None))

    def finish_wait(self, e, tokens):
        waits = self._need(e, list(tokens))
        self.ops[e].append((waits, None, None))

    def emit(self, st):
        nc = self.nc
        for e in ENGS:
            self.sems[("c", e)] = st.enter_context(nc.semaphore("c_" + e))
        for q in ("sp", "act", "pool"):
            for i in range(NDSEM):
                self.sems[("d", q, i)] = st.enter_context(nc.semaphore("d_%s_%d" % (q, i)))
        block = st.enter_context(nc.Block())

        def mk(e):
            def body(eh):
                for (waits, fn, inc) in self.ops[e]:
                    for (sk, v) in waits:
                        eh.wait_ge(self.sems[sk], v)
                    if fn is None:
                        continue
                    ins = fn(eh)
                    ins.then_inc(self.sems[inc], 1 if inc[0] == "c" else 16)
            return body
        block.tensor(mk("pe"))
        block.scalar(mk("act"))
        block.vector(mk("dve"))
        block.gpsimd(mk("pool"))
        block.sync(mk("sp"))


class Cfg:
    def __init__(self, depth=4, ls=2048, lp=256, past=256, nps=2):
        self.depth, self.ls, self.lp, self.past, self.nps = depth, ls, lp, past, nps
        self.nt = ls + nps * lp
        self.d = 1024
        self.kc = 8
        self.incols = 3456
        self.dff = 4096
        self.tiles = []
        for s in range(0, ls, 512):
            self.tiles.append((s, min(512, ls - s), 0))
        for s in range(ls, self.nt, 512):
            self.tiles.append((s, min(512, self.nt - s), 1))
        self.seqs = [(0, ls, 0)] + [(ls + i * lp, lp, 1) for i in range(nps)]


def lam_init_of(l):
    return 0.8 - 0.6 * math.exp(-0.3 * l)


def host_consts(cfg):
    c = {}
    c["ident_f"] = np.eye(128, dtype=np.float32)
    c["ident_b"] = np.eye(128, dtype=np.float32).astype(ml_dtypes.bfloat16)
    ob = np.zeros((128, 128), np.float32)
    ob[:64, :64] = 1.0
    ob[64:, 64:] = 1.0
    c["onesblk_f"] = ob
    c["onesblk_b"] = ob.astype(ml_dtypes.bfloat16)
    c["ones_b"] = np.ones((128, 128), np.float32).astype(ml_dtypes.bfloat16)
    c["ones_f"] = np.ones((128, 128), np.float32)
    c["i2_f"] = np.concatenate([np.eye(64, dtype=np.float32)] * 2, axis=0)
    sel = np.zeros((64, 2, 128), np.float32)
    for j in range(2):
        sel[np.arange(64), j, j * 64 + np.arange(64)] = 1.0
    c["sel_f"] = sel
    R = np.zeros((128, 128), np.float32)
    for m in range(2):
        b = m * 64
        for i in range(16):
            R[b + 16 + i, b + i] = -1.0
            R[b + i, b + 16 + i] = 1.0
            R[b + 48 + i, b + 32 + i] = -1.0
            R[b + 32 + i, b + 48 + i] = 1.0
    c["ropeR"] = R.astype(ml_dtypes.bfloat16)
    L = cfg.ls
    rows = L // 64
    row = np.repeat(np.arange(rows), 64).astype(np.float32)
    col = np.tile(np.arange(64), rows).astype(np.float32)
    inv = (10000.0 ** (-np.arange(0, 32, 2, dtype=np.float32) / 32)).astype(np.float32)
    ang_r = row[:, None] * inv[None]
    ang_c = col[:, None] * inv[None]
    ang = np.concatenate([ang_r, ang_r, ang_c, ang_c], axis=-1)
    cos = np.cos(ang).astype(np.float32).T
    sin = np.sin(ang).astype(np.float32).T
    c["cos"] = np.concatenate([cos, cos], axis=0).astype(ml_dtypes.bfloat16)
    c["sin"] = np.concatenate([sin, sin], axis=0).astype(ml_dtypes.bfloat16)
    for nm, Lx in (("s", cfg.ls), ("p", cfg.lp)):
        t = np.linspace(0.0, 1.0, Lx, dtype=np.float32)[:, None]
        ang2 = (np.float32(2.0 * math.pi / Lx) * np.arange(Lx, dtype=np.float32))[:, None]
        bands = np.linspace(1e-4, 15, 16, dtype=np.float32)[None, :]
        emb = np.concatenate([t, np.cos(bands * ang2), -np.sin(bands * ang2)], axis=-1).astype(np.float32)
        c["embT_" + nm] = emb.T.copy()
        SC = Lx // 128
        FC = (Lx + 1 + 127) // 128
        c["tlinT_" + nm] = t[:, 0].reshape(SC, 128).T.copy()
        n2 = 2 * Lx
        f = np.arange(FC * 128, dtype=np.int64)
        sidx = np.arange(Lx, dtype=np.int64)
        m = (sidx[:, None] * f[None, :]) % n2
        ang3 = 2.0 * np.pi * m.astype(np.float64) / n2
        valid = (f <= Lx).astype(np.float64)[None, :]
        Cf = np.cos(ang3) * valid
        Sf = -np.sin(ang3) * valid
        def lay_f(M):
            return np.ascontiguousarray(M.reshape(SC, 128, FC, 128).transpose(2, 1, 0, 3)).astype(ml_dtypes.bfloat16)
        c["fwdC_" + nm] = lay_f(Cf)
        c["fwdS_" + nm] = lay_f(Sf)
        wgt = np.where((f == 0) | (f == Lx), 1.0, 2.0) * (f <= Lx) / n2
        Ci = (np.cos(ang3) * wgt[None, :]).T
        Si = (-np.sin(ang3) * wgt[None, :]).T
        c["invC_" + nm] = np.ascontiguousarray(Ci.reshape(FC, 128, Lx)).astype(ml_dtypes.bfloat16)
        c["invS_" + nm] = np.ascontiguousarray(Si.reshape(FC, 128, Lx)).astype(ml_dtypes.bfloat16)
    return c


CONST_SHAPES = None


def build_program(cfg, consts, debug=(), mixers=()):
    nc = bass.Bass("TRN2", target_bir_lowering=False)
    D, KC, NT, LS, LP = cfg.d, cfg.kc, cfg.nt, cfg.ls, cfg.lp
    DEPTH = cfg.depth
    NPS = cfg.nps
    PAST = cfg.past
    din = {}

    def inp(name, shape, dt=F32):
        din[name] = nc.dram_tensor(name, list(shape), dt, kind="ExternalInput").ap()
        return din[name]

    def outp(name, shape):
        din[name] = nc.dram_tensor(name, list(shape), F32, kind="ExternalOutput").ap()
        return din[name]

    for k, v in consts.items():
        inp("c_" + k, v.shape, BF16 if v.dtype == ml_dtypes.bfloat16 else F32)
    xT_in = inp("xT", [128, KC, NT])
    cvec = inp("cvecT", [128, KC, 2])
    w_mod = inp("w_mod", [DEPTH, D, 6 * D])
    b_modT = inp("b_modT", [DEPTH, 128, 48])
    gains = inp("gains", [DEPTH, 128, 4, KC])
    w_in = inp("w_in", [DEPTH, D, cfg.incols])
    w_out = inp("w_out", [DEPTH, D, D])
    w_ff1 = inp("w_ff1", [DEPTH, D, cfg.dff])
    w_ff2 = inp("w_ff2", [DEPTH, cfg.dff, D])
    rw_conv = inp("rw_conv", [DEPTH, 128, 6, 3])
    rw_w0 = inp("rw_w0", [DEPTH, 128, 2, 2])
    rw_a0 = inp("rw_a0", [DEPTH, 128, 2, 2])
    rw_w2 = inp("rw_w2", [DEPTH, 128, 256])
    rw_a2 = inp("rw_a2", [DEPTH, 128, 256])
    rw_g2 = inp("rw_g2", [DEPTH, 128, 256])
    rw_vec = inp("rw_vec", [DEPTH, 128, 3, 2])
    rw_ln = inp("rw_ln", [DEPTH, 64, 2, 4])
    st_in = inp("st_in", [DEPTH, 128, 4, 64])
    ckT = inp("ckT", [DEPTH, 4, 128, PAST])
    cv = inp("cv", [DEPTH, 128, PAST // 128, 4, 128])
    dlam = inp("dlam", [DEPTH, 128, 4, 64])
    dsub = inp("dsub", [DEPTH, 128, 1])
    hy_conv = inp("hy_conv", [DEPTH, 128, 6, 4])
    hy_w1 = inp("hy_w1", [DEPTH, 33, 64])
    hy_vec = inp("hy_vec", [DEPTH, 64, 3])
    hy_w2 = inp("hy_w2", [DEPTH, 64, 64])
    hy_w3 = inp("hy_w3", [DEPTH, 64, 1024])
    hy_decb = inp("hy_decb", [DEPTH, 128, 256])
    hy_bias = inp("hy_bias", [DEPTH, 128, 2, 2])

    yT_out = outp("yT", [128, KC, NT])
    st_out = outp("st_out", [DEPTH, NPS, 2, 4, 64, 64])
    kc_out = outp("kc_out", [DEPTH, NPS * LP, 512])
    vc_out = outp("vc_out", [DEPTH, NPS * LP, 512])
    dbg = {}
    for nm, shape in debug:
        dbg[nm] = outp("dbg_" + nm, shape)
    x_spill = nc.dram_tensor("x_spill", [128, KC, NT], F32, kind="Internal").ap()

    st = ExitStack()
    P = Prog(nc)
    out_toks = []

    def sb(name, shape, dt=F32):
        return st.enter_context(nc.sbuf_tensor("s_" + name, list(shape), dt))

    C = {}
    for k, v in consts.items():
        if k[:4] in ("embT", "tlin", "fwdC", "fwdS", "invC", "invS"):
            continue
        if k in ("cos", "sin") and "attn" not in mixers:
            continue
        t = sb("k_" + k, v.shape, BF16 if v.dtype == ml_dtypes.bfloat16 else F32)
        P.dma("sp", t[:], din["c_" + k][:], writes=["k_" + k])
        C[k] = t
    CK = {k: ["k_" + k] for k in C}

    BIGB = getattr(cfg, 'bigb', 170 * 1024)
    XB = KC * NT * 4
    MB0 = BIGB - KC * NT * 2
    HB0 = MB0 - KC * NT * 2
    big = sb("big", [128, BIGB // 4])
    xT = big[:, 0:XB // 4].rearrange("p (c t) -> p c t", c=KC)
    P.dma("sp", xT, xT_in[:], writes=["xT"])
    cv_t = sb("cvec", [128, KC, 2])
    P.dma("sp", cv_t[:], cvec[:], writes=["cvec"])
    scv = sb("scvec", [128, KC, 2], BF16)
    P.op("act", lambda e: e.activation(out=scv[:], in_=cv_t[:], func=AF.Silu), reads=["cvec"], writes=["scvec"])

    PS = [st.enter_context(nc.psum_tensor("ps%d" % i, [128, 512], F32)) for i in range(8)]
    psn = [0]

    def psum():
        i = psn[0] % 4
        psn[0] += 1
        return PS[i], "ps%d" % i

    def psfix(i):
        return PS[4 + i], "ps%d" % (4 + i)

    hT = big[:, HB0 // 4:MB0 // 4].bitcast(BF16).rearrange("p (c t) -> p c t", c=KC)
    mixT = big[:, MB0 // 4:BIGB // 4].bitcast(BF16).rearrange("p (c t) -> p c t", c=KC)
    h_spill = nc.dram_tensor("h_spill", [128, KC, NT], BF16, kind="Internal").ap()
    modT = sb("modT", [128, 48, 2])
    bmod = sb("bmod", [128, 48])
    gn = sb("gains", [128, 4, KC])
    sc_all = sb("sc_all", [128, 4, KC, 2])
    wst = [sb("wst%d" % i, [128, KC, 128], BF16) for i in range(3)]
    wstn = [0]
    tmpA = [sb("tmpA%d" % i, [128, 512]) for i in range(3)]
    tmpn = [0]
    rstd = sb("rstd", [128, 512])
    sqb = [sb("sqb%d" % i, [128, 512], BF16) for i in range(2)]

    def tmp():
        i = tmpn[0] % 3
        tmpn[0] += 1
        return tmpA[i], "tmpA%d" % i

    arena = big

    class Carver:
        def __init__(self):
            self.off = 0
            self.lim = BIGB

        def reset(self, base=0, lim=None):
            self.off = base
            self.lim = BIGB if lim is None else lim

        def get(self, shape, dt=F32):
            n = int(np.prod(shape[1:]))
            nbytes = n * (4 if dt == F32 else 2)
            nbytes = (nbytes + 31) // 32 * 32
            assert self.off + nbytes <= self.lim, ("arena overflow", self.off, nbytes, self.lim)
            a = arena[0:shape[0], self.off // 4:(self.off + nbytes) // 4]
            self.off += nbytes
            if dt != F32:
                a = a.bitcast(dt)
                a = a[:, 0:n]
            if len(shape) > 2:
                names = " ".join("d%d" % i for i in range(1, len(shape)))
                kw = {"d%d" % i: shape[i] for i in range(1, len(shape))}
                a = a.rearrange("p (%s) -> p %s" % (names, names), **kw)
            return a
    carver = Carver()

    def load_w(src_ap, kchunks, dst=None, key=None):
        if dst is None:
            i = wstn[0] % 3
            wstn[0] += 1
            dst, key = wst[i], "wst%d" % i
        P.dma("pool", dst[:, 0:kchunks, 0:src_ap.shape[1]],
              src_ap.rearrange("(kc p) c -> p kc c", p=128), writes=[key])
        return dst, key

    def rms_stats(src_fn, src_keys, t0, n, out_rstd, out_key, nchunks=KC, scale=1.0 / 1024, eps=1e-6, ones=None):
        ps, pk = psum()
        for c in range(nchunks):
            sq, sk = sqb[c % 2], "sqb%d" % (c % 2)
            src = src_fn(c)
            P.op("act", lambda e, sq=sq, src=src: e.activation(out=sq[:, 0:n], in_=src, func=AF.Square),
                 reads=src_keys, writes=[sk])
            P.op("pe", lambda e, sq=sq, c=c: e.matmul(ps[:, 0:n], lhsT=(ones or C["ones_b"])[:], rhs=sq[:, 0:n],
                                                       start=(c == 0), stop=(c == nchunks - 1)),
                 reads=[sk, "k_ones_b"], writes=[pk], pe_acc=(c > 0))
        P.op("act", lambda e: e.activation(out=out_rstd[:, 0:n], in_=ps[:, 0:n], func=AF.Ln, scale=scale, bias=epsc(eps)),
             reads=[pk, "epsc"], writes=[out_key])
        P.op("act", lambda e: e.activation(out=out_rstd[:, 0:n], in_=out_rstd[:, 0:n], func=AF.Exp, scale=-0.5),
             reads=[out_key], writes=[out_key])

    eps_tiles = {}

    def epsc(v):
        return eps_tiles[v][:, 0:1]

    for v in (1e-6, 1e-12, 64e-5):
        t = sb("eps%d" % len(eps_tiles), [128, 1])
        eps_tiles[v] = t
        P.op("pool", lambda e, t=t, v=v: e.memset(t[:], v), writes=["epsc"])

    def barrier():
        P.barrier()

    def dbg_out(nm, src_ap, keys):
        if nm in dbg:
            out_toks.append(P.dma("pool", dbg[nm][:], src_ap, reads=keys, writes=["dbg_" + nm]))

    pconv = sb("pconv", [128, 6, 3]); pw0 = sb("pw0", [128, 2, 2]); pa0 = sb("pa0", [128, 2, 2])
    pvec = sb("pvec", [128, 3, 2]); pln = sb("pln", [64, 2, 4])
    w2t = sb("w2t", [128, 256], BF16); a2t = sb("a2t", [128, 256], BF16); g2t = sb("g2t", [128, 256], BF16)
    omka = sb("omka", [128, 2])
    plam = sb("plam", [128, 4, 64]); psub = sb("psub", [128, 1]); lamv = sb("lamv", [128, 4]); lamt = sb("lamt", [128, 64])
    hconv = sb("hconv", [128, 6, 4]); hw1 = sb("hw1", [33, 64]); hvec = sb("hvec", [64, 3]); hw2 = sb("hw2", [64, 64])
    hsc = sb("hsc", [64, 6]); fss = sb("fss", [128, 3]); hdec = sb("hdec", [128, 2]); hbias = sb("hbias", [128, 2, 2]); fb = sb("fb", [64, 2])
    def mkap(base, offset_elems, dims):
        return AP(base.tensor, base.offset + offset_elems, [list(base.ap[0])] + [list(d) for d in dims])

    def rwkv_layer(l, rkv, lor):
        P.dma("sp", h_spill[:], hT, reads=["hT"], writes=["h_spill"])
        barrier()
        carver.lim = MB0
        for (t, src) in ((pconv, rw_conv), (pw0, rw_w0), (pa0, rw_a0), (pvec, rw_vec), (pln, rw_ln)):
            P.dma("sp", t[:], src[l], writes=["rwp"])
        for (t, src) in ((w2t, rw_w2), (a2t, rw_a2), (g2t, rw_g2)):
            P.dma("pool", t[:], src[l], writes=["rwp"])
        P.op("dve", lambda e: e.tensor_scalar(out=omka[:], in0=pvec[:, 1, :], scalar1=-1.0, scalar2=1.0, op0=ALU.mult, op1=ALU.add),
             reads=["rwp"], writes=["omka"])
        tmpc = carver.get([128, LS])
        LH = LS // 2
        NPL = NPS * LP
        WS = carver.get([128, 2, 2, LH]); NBS = carver.get([128, 2, 2, LH], BF16); KDS = carver.get([128, 2, 2, LH], BF16)
        WP = carver.get([128, 2, 2, NPL]); NBP = carver.get([128, 2, 2, NPL], BF16); KDP = carver.get([128, 2, 2, NPL], BF16)
        snames = ["s"] + ["p%d" % i for i in range(NPS)]
        Ast = {sn: [carver.get([128, 4, 64]) for _ in range(2)] for sn in snames}
        Bst = {sn: carver.get([128, 4, 64]) for sn in snames}
        G1t = {sn: carver.get([128, 4, 64], BF16) for sn in snames}
        DVt = {sn: carver.get([128, 4, 64], BF16) for sn in snames}
        Abf = {sn: [carver.get([128, 4, 64], BF16) for _ in range(2)] for sn in snames}
        KVt = {sn: [carver.get([128, 4, 64]) for _ in range(2)] for sn in snames}
        gT = carver.get([128, 2, 512])
        bon = mixT[:, 0:2, :]
        Y = mixT[0:64, 2:6, :]
        kkn = mixT[:, 6:8, :]
        for c in range(6):
            for (s0, L, cnd) in cfg.seqs:
                u = rkv[:, c, s0:s0 + L]
                P.op("act", lambda e: e.activation(out=tmpc[:, 0:L], in_=u, func=AF.Identity, scale=pconv[:, c, 1:2]),
                     reads=["rkv", "rwp"], writes=["tmpc"])
                P.op("dve", lambda e: e.scalar_tensor_tensor(out=tmpc[:, 1:L], in0=rkv[:, c, s0:s0 + L - 1], scalar=pconv[:, c, 0:1],
                                                             in1=tmpc[:, 1:L], op0=ALU.mult, op1=ALU.add),
                     reads=["rkv", "rwp", "tmpc"], writes=["tmpc"])
                P.op("dve", lambda e: e.scalar_tensor_tensor(out=tmpc[:, 0:L - 1], in0=rkv[:, c, s0 + 1:s0 + L], scalar=pconv[:, c, 2:3],
                                                             in1=tmpc[:, 0:L - 1], op0=ALU.mult, op1=ALU.add),
                     reads=["rkv", "rwp", "tmpc"], writes=["tmpc"])
                P.op("act", lambda e: e.copy(out=rkv[:, c, s0:s0 + L], in_=tmpc[:, 0:L]), reads=["tmpc"], writes=["rkv"])

        def gen_range(t0, n, dsts, first):
            for p in range(2):
                if first:
                    kk_t, kk_k = tmp()
                    P.op("act", lambda e: e.activation(out=kk_t[:, 0:n], in_=rkv[:, 2 + p, t0:t0 + n], func=AF.Identity, scale=pvec[:, 0, p:p + 1]),
                         reads=["rkv", "rwp"], writes=[kk_k])
                    P.op("act", lambda e: e.activation(out=sqb[0][:, 0:n], in_=kk_t[:, 0:n], func=AF.Square), reads=[kk_k], writes=["sqb0"])
                    ps, pk = psum()
                    P.op("pe", lambda e: e.matmul(ps[:, 0:n], lhsT=C["onesblk_b"][:], rhs=sqb[0][:, 0:n], start=True, stop=True),
                         reads=["sqb0", "k_onesblk_b"], writes=[pk])
                    P.op("act", lambda e: e.activation(out=rstd[:, 0:n], in_=ps[:, 0:n], func=AF.Ln, bias=epsc(1e-12)), reads=[pk, "epsc"], writes=["rstd"])
                    P.op("act", lambda e: e.activation(out=rstd[:, 0:n], in_=rstd[:, 0:n], func=AF.Exp, scale=-0.5), reads=["rstd"], writes=["rstd"])
                    P.op("dve", lambda e: e.tensor_tensor(out=kkn[:, p, t0:t0 + n], in0=kk_t[:, 0:n], in1=rstd[:, 0:n], op=ALU.mult),
                         reads=[kk_k, "rstd"], writes=["kkn"])
                    psb, psbk = psfix(3)
                for d in range(2):
                    if dsts[d] is None and not first:
                        continue
                    if dsts[d] is not None:
                        ps, pk = psum()
                        P.op("pe", lambda e: e.matmul(ps[:, 0:n], lhsT=w2t[d * 64:(d + 1) * 64, p * 128:(p + 1) * 128],
                                                      rhs=lor[d * 64:(d + 1) * 64, 0, t0:t0 + n], start=True, stop=True),
                             reads=["rwp", "lor"], writes=[pk])
                        sg, sgk = tmp()
                        P.op("act", lambda e: e.activation(out=sg[:, 0:n], in_=ps[:, 0:n], func=AF.Sigmoid, bias=pw0[:, d, p:p + 1]),
                             reads=[pk, "rwp"], writes=[sgk])
                        P.op("act", lambda e: e.activation(out=dsts[d][0](p), in_=sg[:, 0:n], func=AF.Exp, scale=-math.exp(-0.5)),
                             reads=[sgk], writes=[dsts[d][3]])
                    ps, pk = psum()
                    P.op("pe", lambda e: e.matmul(ps[:, 0:n], lhsT=a2t[d * 64:(d + 1) * 64, p * 128:(p + 1) * 128],
                                                  rhs=lor[d * 64:(d + 1) * 64, 1, t0:t0 + n], start=True, stop=True),
                         reads=["rwp", "lor"], writes=[pk])
                    av, avk = tmp()
                    P.op("act", lambda e: e.activation(out=av[:, 0:n], in_=ps[:, 0:n], func=AF.Sigmoid, bias=pa0[:, d, p:p + 1]),
                         reads=[pk, "rwp"], writes=[avk])
                    if dsts[d] is not None:
                        P.op("dve", lambda e: e.scalar_tensor_tensor(out=dsts[d][1](p), in0=av[:, 0:n], scalar=-1.0, in1=kkn[:, p, t0:t0 + n],
                                                                     op0=ALU.mult, op1=ALU.mult), reads=[avk, "kkn"], writes=[dsts[d][3]])
                    P.op("dve", lambda e: e.tensor_scalar(out=av[:, 0:n], in0=av[:, 0:n], scalar1=pvec[:, 1, p:p + 1], scalar2=omka[:, p:p + 1],
                                                          op0=ALU.mult, op1=ALU.add), reads=[avk, "rwp", "omka"], writes=[avk])
                    P.op("dve", lambda e: e.tensor_tensor(out=av[:, 0:n], in0=av[:, 0:n], in1=rkv[:, 2 + p, t0:t0 + n], op=ALU.mult),
                         reads=[avk, "rkv"], writes=[avk])
                    if dsts[d] is not None:
                        P.op("act", lambda e: e.copy(out=dsts[d][2](p), in_=av[:, 0:n]), reads=[avk], writes=[dsts[d][3]])
                    if first:
                        P.op("dve", lambda e: e.scalar_tensor_tensor(out=sqb[1][:, 0:n], in0=av[:, 0:n], scalar=pvec[:, 2, p:p + 1],
                                                                     in1=rkv[:, p, t0:t0 + n], op0=ALU.mult, op1=ALU.mult),
                             reads=[avk, "rwp", "rkv"], writes=["sqb1"])
                        P.op("pe", lambda e: e.matmul(psb[:, 0:n], lhsT=C["onesblk_b"][:], rhs=sqb[1][:, 0:n], start=(d == 0), stop=(d == 1)),
                             reads=["sqb1", "k_onesblk_b"], writes=[psbk], pe_acc=(d == 1))
                if first:
                    P.op("dve", lambda e: e.tensor_tensor(out=bon[:, p, t0:t0 + n], in0=psb[:, 0:n], in1=rkv[:, 4 + p, t0:t0 + n], op=ALU.mult),
                         reads=[psbk, "rkv"], writes=["bon"])

        PIECE = min(512, LH)

        def s_dsts(t0, n, phase):
            half = 0 if t0 < LH else 1
            d = half if phase == 0 else 1 - half
            col = t0 - half * LH
            out = [None, None]
            out[d] = (lambda p: WS[:, p, d, col:col + n], lambda p: NBS[:, p, d, col:col + n], lambda p: KDS[:, p, d, col:col + n], "opS")
            return out
        for t0 in range(0, LS, PIECE):
            gen_range(t0, PIECE, s_dsts(t0, PIECE, 0), True)
        for q0 in range(0, NPL, PIECE if NPL >= PIECE else NPL):
            n = min(PIECE, NPL - q0)
            both = [(lambda p, d=d: WP[:, p, d, q0:q0 + n], lambda p, d=d: NBP[:, p, d, q0:q0 + n], lambda p, d=d: KDP[:, p, d, q0:q0 + n], "opP") for d in range(2)]
            gen_range(LS + q0, n, both, True)
        barrier()
        streams = [("s", 0, LS, True, 0)] + [("p%d" % i, LS + i * LP, LP, False, i * LP) for i in range(NPS)]
        ypsums = {}
        for (sn, s0, L, is_s, q0) in streams:
            if is_s:
                P.dma("sp", Ast[sn][0], st_in[l], writes=["A_" + sn])
            else:
                P.op("pool", lambda e: e.memset(Ast[sn][0], 0.0), writes=["A_" + sn])
            ypsums[sn] = psfix(len(ypsums))
        pend = {}

        def emit_y(sn, s0, L, i, nxt):
            ak = "A_" + sn
            slot = i % 64
            yp, ypk = ypsums[sn]
            for g in range(4):
                p, d = g // 2, g % 2
                rcol = slot if d == 0 else 63 - slot
                for j in range(2):
                    col = ((slot * 2 + d) * 2 + p) * 2 + j
                    tcol = s0 + (i if d == 0 else L - 1 - i)
                    P.op("pe", lambda e: e.matmul(yp[0:64, col:col + 1], lhsT=nxt[j * 64:(j + 1) * 64, g, :], rhs=rkv[j * 64:(j + 1) * 64, p, tcol:tcol + 1],
                                                  start=True, stop=True), reads=["Abf%s%d" % (sn, (i + 1) % 2), "rkv"], writes=[ypk], pe_acc=True)
            if slot == 63 or i == L - 1:
                ns = slot + 1
                i0 = i - slot
                first = i0 < L // 2
                ypv = yp[0:64, 0:ns * 8].rearrange("v (s d h) -> v s d h", d=2, h=4)
                for d in range(2):
                    if d == 0:
                        dst = mkap(Y[:, 0, 0:1], s0 + i0, [[1, ns], [Y.ap[1][0], 4]])
                    else:
                        dst = mkap(Y[:, 0, 0:1], s0 + L - 1 - i0, [[-1, ns], [Y.ap[1][0], 4]])
                    if first:
                        P.op("act", lambda e: e.copy(out=dst, in_=ypv[:, :, d, :]), reads=[ypk], writes=["Y"])
                    else:
                        P.op("dve", lambda e: e.tensor_tensor(out=dst, in0=ypv[:, :, d, :], in1=dst, op=ALU.add), reads=[ypk, "Y"], writes=["Y"])
        maxL = max(L for (_, _, L, _, _) in streams)
        def step_ctx(st_, i):
            (sn, s0, L, is_s, q0) = st_
            dstr = L - 1 - 2 * i
            if is_s:
                il = i % LH
                Wb, NBb, KDb, opk, Lb, off = WS, NBS, KDS, "opS", LH, il
                dl = LH - 1 - 2 * il
            else:
                Wb, NBb, KDb, opk, Lb, off = WP, NBP, KDP, "opP", NPL, q0 + i
                dl = L - 1 - 2 * i

            def gop(base3, pstride, dextra):
                return mkap(base3, s0 + i, [[pstride, 2], [dextra + dstr, 2], [0, 64]])

            def lop(buf):
                return mkap(buf[:, 0, 0, 0:1], off, [[2 * Lb, 2], [Lb + dl, 2], [0, 64]])
            return dict(sn=sn, s0=s0, L=L, i=i, cur=Ast[sn][i % 2], nxt=Ast[sn][(i + 1) % 2], ak="A_" + sn,
                        kk_op=gop(kkn[:, 0, 0:1], kkn.ap[1][0], 0), v_op=gop(rkv[:, 4, 0:1], NT, 0),
                        w_op=lop(Wb), kd_op=lop(KDb), NBb=NBb, off=off, dl=dl, opk=opk)

        def prep(st_, i):
            c = step_ctx(st_, i)
            sn = c["sn"]
            kvb, kvk = KVt[sn][i % 2], "KV%s%d" % (sn, i % 2)
            P.op("pool", lambda e: e.tensor_tensor(out=DVt[sn], in0=mkap(C["i2_f"][:, 0:1], 0, [[0, 4], [1, 64]]), in1=c["v_op"], op=ALU.mult),
                 reads=["rkv", "k_i2_f"], writes=["DV" + sn])
            ps2, ps2k = psum()
            P.op("pe", lambda e: e.matmul(ps2[:, 0:256], lhsT=C["onesblk_b"][:], rhs=DVt[sn].rearrange("p g v -> p (g v)"), start=True, stop=True),
                 reads=["DV" + sn, "k_onesblk_b"], writes=[ps2k])
            P.op("dve", lambda e: e.tensor_tensor(out=kvb, in0=ps2[:, 0:256].rearrange("p (g v) -> p g v", g=4), in1=c["kd_op"], op=ALU.mult),
                 reads=[ps2k, c["opk"]], writes=[kvk])
        prepped = {}
        for i in range(maxL):
            if i == LH:
                for t0 in range(0, LS, PIECE):
                    gen_range(t0, PIECE, s_dsts(t0, PIECE, 1), False)
            for st_ in streams:
                (sn, s0, L, is_s, q0) = st_
                if i >= L:
                    continue
                c = step_ctx(st_, i)
                cur, nxt, ak, opk, off, dl, NBb = c["cur"], c["nxt"], c["ak"], c["opk"], c["off"], c["dl"], c["NBb"]
                if prepped.get(sn) != i:
                    prep(st_, i)
                    prepped[sn] = i
                slot = i % 64
                if slot == 0:
                    if pend.get(sn) is not None:
                        emit_y(*pend[sn])
                        pend[sn] = None
                P.op("dve", lambda e: e.tensor_tensor(out=G1t[sn], in0=cur, in1=c["kk_op"], op=ALU.mult), reads=[ak, "kkn"], writes=["G1" + sn])
                ps1, ps1k = psum()
                P.op("pe", lambda e: e.matmul(ps1[:, 0:256], lhsT=C["onesblk_b"][:], rhs=G1t[sn].rearrange("p g v -> p (g v)"), start=True, stop=True),
                     reads=["G1" + sn, "k_onesblk_b"], writes=[ps1k])
                if pend.get(sn) is not None:
                    emit_y(*pend[sn])
                    pend[sn] = None
                P.op("dve", lambda e: e.tensor_tensor(out=Bst[sn], in0=cur, in1=c["w_op"], op=ALU.mult), reads=[ak, opk], writes=["B" + sn])
                P.op("dve", lambda e: e.tensor_tensor(out=Bst[sn], in0=Bst[sn], in1=KVt[sn][i % 2], op=ALU.add), reads=["B" + sn, "KV%s%d" % (sn, i % 2)], writes=["B" + sn])
                for g in range(4):
                    p, d = g // 2, g % 2
                    lcol = off if d == 0 else off + dl
                    P.op("dve", lambda e: e.scalar_tensor_tensor(out=nxt[:, g, :], in0=ps1[:, g * 64:(g + 1) * 64], scalar=NBb[:, p, d, lcol:lcol + 1],
                                                                 in1=Bst[sn][:, g, :], op0=ALU.mult, op1=ALU.add),
                         reads=[ps1k, opk, "B" + sn], writes=[ak])
                abf = Abf[sn][(i + 1) % 2]
                P.op("act", lambda e: e.copy(out=abf, in_=nxt), reads=[ak], writes=["Abf%s%d" % (sn, (i + 1) % 2)])
                pend[sn] = (sn, s0, L, i, abf)
                if i + 1 < L and not (is_s and (i + 1) == LH):
                    prep(st_, i + 1)
                    prepped[sn] = i + 1
        for sn in list(pend.keys()):
            if pend[sn] is not None:
                emit_y(*pend[sn])
                pend[sn] = None
        for si, (sn, s0, L, is_s, q0) in enumerate(streams):
            if is_s:
                continue
            fin = Ast[sn][L % 2]
            for g in range(4):
                p, d = g // 2, g % 2
                ps, pk = psum()
                P.op("pe", lambda e: e.transpose(ps[0:64, 0:128], fin[:, g, :], C["ident_f"][:]), reads=["A_" + sn, "k_ident_f"], writes=[pk])
                tt, tk = tmp()
                P.op("act", lambda e: e.copy(out=tt[0:64, 0:128], in_=ps[0:64, 0:128]), reads=[pk], writes=[tk])
                for j in range(2):
                    out_toks.append(P.dma("sp", st_out[l, si - 1, d, 2 * p + j], tt[0:64, j * 64:(j + 1) * 64], reads=[tk], writes=["st_out"]))
        for (t0, n, cnd) in cfg.tiles:
            for p in range(2):
                ps, pk = psum()
                P.op("pe", lambda e: e.matmul(ps[:, 0:n], lhsT=g2t[:, p * 128:(p + 1) * 128], rhs=lor[:, 2, t0:t0 + n], start=True, stop=True),
                     reads=["rwp", "lor"], writes=[pk])
                P.op("act", lambda e: e.copy(out=gT[:, p, 0:n], in_=ps[:, 0:n]), reads=[pk], writes=["gT"])
            for p in range(2):
                pso, psok = psfix(3)
                for j in range(2):
                    h = 2 * p + j
                    yv = Y[:, h, t0:t0 + n]
                    psm, psmk = psum()
                    yc, yck = tmp()
                    P.op("act", lambda e: e.copy(out=yc[0:64, 0:n], in_=yv), reads=["Y"], writes=[yck])
                    P.op("pe", lambda e: e.matmul(psm[0:64, 0:n], lhsT=C["ones_f"][0:64, 0:64], rhs=yc[0:64, 0:n], start=True, stop=True),
                         reads=[yck, "k_ones_f"], writes=[psmk])
                    P.op("dve", lambda e: e.scalar_tensor_tensor(out=yc[0:64, 0:n], in0=psm[0:64, 0:n], scalar=-1.0 / 64, in1=yc[0:64, 0:n],
                                                                 op0=ALU.mult, op1=ALU.add), reads=[psmk, yck], writes=[yck])
                    sq, sqk = tmp()
                    P.op("act", lambda e: e.activation(out=sq[0:64, 0:n], in_=yc[0:64, 0:n], func=AF.Square), reads=[yck], writes=[sqk])
                    psv, psvk = psum()
                    P.op("pe", lambda e: e.matmul(psv[0:64, 0:n], lhsT=C["ones_f"][0:64, 0:64], rhs=sq[0:64, 0:n], start=True, stop=True),
                         reads=[sqk, "k_ones_f"], writes=[psvk])
                    P.op("act", lambda e: e.activation(out=sq[0:64, 0:n], in_=psv[0:64, 0:n], func=AF.Ln, scale=1.0 / 64, bias=eps_tiles[64e-5][0:64, 0:1]),
                         reads=[psvk, "epsc"], writes=[sqk])
                    P.op("act", lambda e: e.activation(out=sq[0:64, 0:n], in_=sq[0:64, 0:n], func=AF.Exp, scale=-0.5), reads=[sqk], writes=[sqk])
                    P.op("dve", lambda e: e.tensor_tensor(out=yc[0:64, 0:n], in0=yc[0:64, 0:n], in1=sq[0:64, 0:n], op=ALU.mult), reads=[yck, sqk], writes=[yck])
                    P.op("act", lambda e: e.activation(out=yc[0:64, 0:n], in_=yc[0:64, 0:n], func=AF.Identity, scale=pln[:, 0, h:h + 1], bias=pln[:, 1, h:h + 1]),
                         reads=[yck, "rwp"], writes=[yck])
                    P.op("pe", lambda e: e.matmul(pso[:, 0:n], lhsT=C["sel_f"][:, j, :], rhs=yc[0:64, 0:n], start=(j == 0), stop=(j == 1)),
                         reads=[yck, "k_sel_f"], writes=[psok], pe_acc=(j == 1))
                tt, tk = tmp()
                P.op("dve", lambda e: e.tensor_tensor(out=tt[:, 0:n], in0=pso[:, 0:n], in1=bon[:, p, t0:t0 + n], op=ALU.add), reads=[psok, "bon"], writes=[tk])
                P.op("dve", lambda e: e.tensor_tensor(out=mixT[:, p, t0:t0 + n], in0=tt[:, 0:n], in1=gT[:, p, 0:n], op=ALU.mult), reads=[tk, "gT"], writes=["mixT"])
        barrier()
        P.dma("sp", hT, h_spill[:], reads=["h_spill"], writes=["hT"])
        dbg_out("ya%d" % l, mixT[:, 0:2, :], ["mixT"])


    def attn_layer(l, lam0):
        carver.reset(0, HB0)
        NKC = PAST // 128
        qT = carver.get([128, 4, NT], BF16)
        kT = carver.get([128, 4, NT], BF16)
        kcT = carver.get([128, 4, PAST], BF16)
        vS = carver.get([128, NKC + LS // 128, 4, 128], BF16)
        vP = carver.get([128, NPS * LP // 128, 4, 128], BF16)
        stg = [carver.get([128, 512]) for _ in range(2)]
        wv = carver.get([128, KC, 512], BF16)
        pTs = [carver.get([128, 512], BF16) for _ in range(3)]
        xb = [carver.get([128, 512], BF16) for _ in range(2)]
        of = carver.get([128, 512])
        P.dma("sp", plam[:], dlam[l], writes=["plam"])
        P.dma("sp", psub[:], dsub[l], writes=["psub"])
        P.dma("pool", kcT, ckT[l].rearrange("h p t -> p h t"), writes=["kcT"])
        P.dma("pool", vS[:, 0:NKC], cv[l], writes=["vS"])
        for i in range(2):
            P.op("dve", lambda e: e.tensor_tensor(out=lamt[:], in0=plam[:, 2 * i, :], in1=plam[:, 2 * i + 1, :], op=ALU.mult), reads=["plam"], writes=["lamt"])
            P.op("dve", lambda e: e.reduce_sum(out=lamv[:, i:i + 1], in_=lamt[:], axis=AX.X), reads=["lamt"], writes=["lamv"])
        P.op("act", lambda e: e.activation(out=lamv[:, 0:2], in_=lamv[:, 0:2], func=AF.Exp), reads=["lamv"], writes=["lamv"])
        P.op("dve", lambda e: e.tensor_tensor(out=lamv[:, 2:3], in0=lamv[:, 0:1], in1=lamv[:, 1:2], op=ALU.subtract), reads=["lamv"], writes=["lamv"])
        P.op("dve", lambda e: e.tensor_scalar(out=lamv[:, 3:4], in0=lamv[:, 2:3], scalar1=-1.0, scalar2=-lam0, op0=ALU.mult, op1=ALU.add), reads=["lamv"], writes=["lamv"])
        P.op("dve", lambda e: e.tensor_scalar(out=psub[:], in0=psub[:], scalar1=(1.0 - lam0), scalar2=None, op0=ALU.mult), reads=["psub"], writes=["psub"])

        def cbqk(ps, pk, ci, t0, n, cnd):
            dst = (qT if ci < 4 else kT)[:, ci % 4, t0:t0 + n]
            dk = "qT" if ci < 4 else "kT"
            if cnd == 1 or os.environ.get("NOROPE"):
                P.op("act", lambda e: e.copy(out=dst, in_=ps[:, 0:n]), reads=[pk], writes=[dk])
                return
            xbt, xbk = xb[ci % 2], "xb%d" % (ci % 2)
            P.op("act", lambda e: e.copy(out=xbt[:, 0:n], in_=ps[:, 0:n]), reads=[pk], writes=[xbk])
            pr, prk = psum()
            P.op("pe", lambda e: e.matmul(pr[:, 0:n], lhsT=C["ropeR"][:], rhs=xbt[:, 0:n], start=True, stop=True), reads=[xbk, "k_ropeR"], writes=[prk])
            t1, t1k = tmp()
            P.op("dve", lambda e: e.tensor_tensor(out=t1[:, 0:n], in0=xbt[:, 0:n], in1=C["cos"][:, t0:t0 + n], op=ALU.mult), reads=[xbk, "k_cos"], writes=[t1k])
            t2, t2k = tmp()
            P.op("dve", lambda e: e.tensor_tensor(out=t2[:, 0:n], in0=pr[:, 0:n], in1=C["sin"][:, t0:t0 + n], op=ALU.mult), reads=[prk, "k_sin"], writes=[t2k])
            P.op("dve", lambda e: e.tensor_tensor(out=dst, in0=t1[:, 0:n], in1=t2[:, 0:n], op=ALU.add), reads=[t1k, t2k], writes=[dk])
        STG = int(os.environ.get("ATT_STAGE", "9"))
        if STG < 1:
            return
        proj_fm(w_in[l], 1152, 8, hT, "hT", cbqk)
        if STG < 2:
            return

        def proj_tm(col0, tok_ranges, cb):
            P.dma("pool", wv, w_in[l][:, col0:col0 + 512].rearrange("(kc p) c -> p kc c", p=128), writes=["wv"])
            for (t0, info) in tok_ranges:
                ps, pk = psum()
                for c in range(KC):
                    P.op("pe", lambda e: e.matmul(ps[:, 0:512], lhsT=hT[:, c, t0:t0 + 128], rhs=wv[:, c, :], start=(c == 0), stop=(c == KC - 1)),
                         reads=["hT", "wv"], writes=[pk], pe_acc=(c > 0))
                cb(ps, pk, t0, info)
        prm_chunks = [(LS + i * 128, i) for i in range(NPS * LP // 128)]
        sam_chunks = [(i * 128, i) for i in range(LS // 128)]
        scnt = [0]

        def cbk(ps, pk, t0, i):
            sg, sgk = stg[scnt[0] % 2], "stg%d" % (scnt[0] % 2)
            scnt[0] += 1
            P.op("act", lambda e: e.copy(out=sg[:], in_=ps[:, 0:512]), reads=[pk], writes=[sgk])
            out_toks.append(P.dma("sp", kc_out[l, i * 128:(i + 1) * 128, :], sg[:], reads=[sgk], writes=["kc_out"]))
        proj_tm(1664, prm_chunks, cbk)

        def cbv_p(ps, pk, t0, i):
            sg, sgk = stg[scnt[0] % 2], "stg%d" % (scnt[0] % 2)
            scnt[0] += 1
            P.op("act", lambda e: e.copy(out=sg[:], in_=ps[:, 0:512]), reads=[pk], writes=[sgk])
            out_toks.append(P.dma("sp", vc_out[l, i * 128:(i + 1) * 128, :], sg[:], reads=[sgk], writes=["vc_out"]))
            P.op("dve", lambda e: e.tensor_copy(out=vP[:, i].rearrange("p h e -> p (h e)"), in_=sg[:]), reads=[sgk], writes=["vP"])

        def cbv_s(ps, pk, t0, i):
            P.op("act", lambda e: e.copy(out=vS[:, NKC + i].rearrange("p h e -> p (h e)"), in_=ps[:, 0:512]), reads=[pk], writes=["vS"])
        proj_tm(2176, prm_chunks, cbv_p)
        proj_tm_v = None
        for (t0, i) in sam_chunks:
            ps, pk = psum()
            for c in range(KC):
                P.op("pe", lambda e: e.matmul(ps[:, 0:512], lhsT=hT[:, c, t0:t0 + 128], rhs=wv[:, c, :], start=(c == 0), stop=(c == KC - 1)),
                     reads=["hT", "wv"], writes=[pk], pe_acc=(c > 0))
            cbv_s(ps, pk, t0, i)

        if STG < 3:
            return
        jobs = []
        for q0 in range(0, LS, 512):
            nq = min(512, LS - q0)
            ks = [("c", c) for c in range(NKC)] + [("s", c) for c in range(LS // 128)]
            jobs.append((q0, nq, ks))
        for i in range(NPS):
            s0 = LS + i * LP
            jobs.append((s0, LP, [("p", i * (LP // 128) + c) for c in range(LP // 128)]))
        pcnt = [0]
        for (q0, nq, ks) in jobs:
            for h in range(4):
                O = [psfix(0), psfix(1)]
                Z = [psfix(2), psfix(3)]
                for ki, (kind, c) in enumerate(ks):
                    if kind == "c":
                        kap = lambda m: kcT[m * 64:(m + 1) * 64, h, c * 128:(c + 1) * 128]
                        vap = vS[:, c, h, :]
                        kkey, vkey = "kcT", "vS"
                    elif kind == "s":
                        kap = lambda m: kT[m * 64:(m + 1) * 64, h, c * 128:(c + 1) * 128]
                        vap = vS[:, NKC + c, h, :]
                        kkey, vkey = "kT", "vS"
                    else:
                        kap = lambda m: kT[m * 64:(m + 1) * 64, h, LS + c * 128:LS + (c + 1) * 128]
                        vap = vP[:, c, h, :]
                        kkey, vkey = "kT", "vP"
                    for m in range(2):
                        ps, pk = psum()
                        P.op("pe", lambda e: e.matmul(ps[:, 0:nq], lhsT=kap(m), rhs=qT[m * 64:(m + 1) * 64, h, q0:q0 + nq], start=True, stop=True),
                             reads=[kkey, "qT"], writes=[pk])
                        pT, pTk = pTs[pcnt[0] % 3], "pT%d" % (pcnt[0] % 3)
                        pcnt[0] += 1
                        P.op("act", lambda e: e.activation(out=pT[:, 0:nq], in_=ps[:, 0:nq], func=AF.Exp, scale=0.125), reads=[pk], writes=[pTk])
                        P.op("pe", lambda e: e.matmul(O[m][0][:, 0:nq], lhsT=vap, rhs=pT[:, 0:nq], start=(ki == 0), stop=(ki == len(ks) - 1)),
                             reads=[vkey, pTk], writes=[O[m][1]], pe_acc=(ki > 0))
                        P.op("pe", lambda e: e.matmul(Z[m][0][:, 0:nq], lhsT=C["ones_b"][:], rhs=pT[:, 0:nq], start=(ki == 0), stop=(ki == len(ks) - 1)),
                             reads=["k_ones_b", pTk], writes=[Z[m][1]], pe_acc=(ki > 0))
                o_m = []
                for m in range(2):
                    rz, rzk = tmp()
                    P.op("dve", lambda e: e.reciprocal(out=rz[:, 0:nq], in_=Z[m][0][:, 0:nq]), reads=[Z[m][1]], writes=[rzk])
                    P.op("dve", lambda e: e.tensor_tensor(out=rz[:, 0:nq], in0=O[m][0][:, 0:nq], in1=rz[:, 0:nq], op=ALU.mult), reads=[O[m][1], rzk], writes=[rzk])
                    o_m.append((rz, rzk))
                P.op("dve", lambda e: e.scalar_tensor_tensor(out=of[:, 0:nq], in0=o_m[1][0][:, 0:nq], scalar=lamv[:, 3:4], in1=o_m[0][0][:, 0:nq],
                                                             op0=ALU.mult, op1=ALU.add), reads=[o_m[0][1], o_m[1][1], "lamv"], writes=["of"])
                rms_stats(lambda c: of[:, 0:nq], ["of"], q0, nq, rstd, "rstd", nchunks=1, scale=1.0 / 128)
                tt, tk = tmp()
                P.op("dve", lambda e: e.tensor_tensor(out=tt[:, 0:nq], in0=of[:, 0:nq], in1=rstd[:, 0:nq], op=ALU.mult), reads=["of", "rstd"], writes=[tk])
                P.op("act", lambda e: e.activation(out=mixT[:, 2 + h, q0:q0 + nq], in_=tt[:, 0:nq], func=AF.Identity, scale=psub[:, 0:1]),
                     reads=[tk, "psub"], writes=["mixT"])
        dbg_out("yb%d" % l, mixT[:, 2:6, :], ["mixT"])

    def hyena_layer(l):
        carver.reset(0, HB0)
        uh = carver.get([128, 6, NT], BF16)

        def cbC(ps, pk, ci, t0, n, cnd):
            P.op("act", lambda e: e.copy(out=uh[:, ci, t0:t0 + n], in_=ps[:, 0:n]), reads=[pk], writes=["uh"])
        proj_fm(w_in[l], 2688, 6, hT, "hT", cbC)
        barrier()
        carver.lim = MB0
        hw3b = carver.get([64, 1024], BF16)
        ndecb = carver.get([128, 256])
        for (t, src) in ((hconv, hy_conv), (hw1, hy_w1), (hvec, hy_vec), (hw2, hy_w2), (hbias, hy_bias)):
            P.dma("sp", t[:], src[l], writes=["hyp"])
        P.dma("sp", ndecb, hy_decb[l], writes=["ndecb"])
        P.dma("pool", hw3b, hy_w3[l], writes=["hyp"])
        EV = carver.get([128, 2, 1024])
        P.op("dve", lambda e: e.tensor_scalar(out=EV[:, 0, 0:256], in0=ndecb, scalar1=-1.0, scalar2=None, op0=ALU.mult), reads=["ndecb"], writes=["EV"])
        P.op("dve", lambda e: e.tensor_tensor(out=ndecb, in0=ndecb, in1=EV[:, 0, 0:256], op=ALU.min), reads=["ndecb", "EV"], writes=["ndecb"])
        for (col, bi, fac) in ((0, None, 0.5), (1, None, 0.25), (2, 0, 0.5), (3, 0, 0.25), (4, 2, 0.5), (5, 2, 0.25)):
            if bi is None:
                P.op("dve", lambda e: e.tensor_scalar(out=hsc[:, col:col + 1], in0=hvec[:, 1:2], scalar1=fac, scalar2=None, op0=ALU.mult), reads=["hyp"], writes=["hsc"])
            else:
                P.op("dve", lambda e: e.scalar_tensor_tensor(out=hsc[:, col:col + 1], in0=hvec[:, 1:2], scalar=fac, in1=hvec[:, bi:bi + 1], op0=ALU.mult, op1=ALU.mult),
                     reads=["hyp"], writes=["hsc"])
        XS = carver.get([128, 3 * LS])
        tmpc = XS[:, 0:LS]
        embT = XS[0:33, LS:2 * LS]
        h1 = XS[0:64, 2 * LS:3 * LS]
        for c in range(6):
            for (s0, L, cnd) in cfg.seqs:
                P.op("act", lambda e: e.activation(out=tmpc[:, 0:L], in_=uh[:, c, s0:s0 + L], func=AF.Identity, scale=hconv[:, c, 1:2], bias=hconv[:, c, 3:4]),
                     reads=["uh", "hyp"], writes=["XS"])
                P.op("dve", lambda e: e.scalar_tensor_tensor(out=tmpc[:, 1:L], in0=uh[:, c, s0:s0 + L - 1], scalar=hconv[:, c, 0:1], in1=tmpc[:, 1:L], op0=ALU.mult, op1=ALU.add),
                     reads=["uh", "hyp", "XS"], writes=["XS"])
                P.op("dve", lambda e: e.scalar_tensor_tensor(out=tmpc[:, 0:L - 1], in0=uh[:, c, s0 + 1:s0 + L], scalar=hconv[:, c, 2:3], in1=tmpc[:, 0:L - 1], op0=ALU.mult, op1=ALU.add),
                     reads=["uh", "hyp", "XS"], writes=["XS"])
                P.op("act", lambda e: e.copy(out=uh[:, c, s0:s0 + L], in_=tmpc[:, 0:L]), reads=["XS"], writes=["uh"])
        z1 = carver.get([128, 2, NT], BF16)
        h2 = carver.get([64, LS], BF16)
        SCM = LS // 128
        FCM = (LS + 1 + 127) // 128
        Zfull = carver.get([128, max(SCM * 768, (LP // 128) * (512 + NPS * 256))], BF16)
        FYM = max(FCM, NPS * ((LP + 1 + 127) // 128))
        YRf = carver.get([128, FYM * 256], BF16)
        YIf = carver.get([128, FYM * 256], BF16)
        Hc = carver.get([128, 2, 256])
        Eexp = carver.get([128, 256])
        rsb = carver.get([128, 512])
        sqt = carver.get([128, 512], BF16)
        tlT = carver.get([128, SCM])
        tabs = XS.bitcast(BF16)
        TBN = (3 * LS * 2) // 4
        tabv = [tabs[:, i * TBN:(i + 1) * TBN] for i in range(4)]
        for (nm, L, s0, ns) in (("s", LS, 0, 1), ("p", LP, LS, NPS)):
            SC = L // 128
            FC = (L + 1 + 127) // 128
            ncols = 512 + ns * 256
            Z = Zfull[:, 0:SC * ncols].rearrange("p (s c) -> p s c", s=SC)
            YR = [YRf[:, j * FC * 256:(j + 1) * FC * 256].rearrange("p (f c) -> p f c", f=FC) for j in range(ns)]
            YI = [YIf[:, j * FC * 256:(j + 1) * FC * 256].rearrange("p (f c) -> p f c", f=FC) for j in range(ns)]
            barrier()
            P.dma("sp", embT[:, 0:L], din["c_embT_" + nm][:], writes=["XS"])
            P.dma("sp", tlT[:, 0:SC], din["c_tlinT_" + nm][:], writes=["tlT"])
            tls = [(a, min(512, L - a)) for a in range(0, L, 512)]

            def sin_layer(w_t, kdim, src, srck, dst, dstk, c0):
                for (a, n) in tls:
                    ps, pk = psum()
                    P.op("pe", lambda e: e.matmul(ps[0:64, 0:n], lhsT=w_t, rhs=src[0:kdim, a:a + n], start=True, stop=True), reads=["hyp", srck], writes=[pk])
                    s2, s2k = tmp()
                    s4, s4k = tmp()
                    P.op("act", lambda e: e.activation(out=s2[0:64, 0:n], in_=ps[0:64, 0:n], func=AF.Sin, scale=hsc[:, 0:1], bias=hsc[:, c0:c0 + 1]), reads=[pk, "hsc"], writes=[s2k])
                    P.op("act", lambda e: e.activation(out=s4[0:64, 0:n], in_=ps[0:64, 0:n], func=AF.Sin, scale=hsc[:, 1:2], bias=hsc[:, c0 + 1:c0 + 2]), reads=[pk, "hsc"], writes=[s4k])
                    P.op("dve", lambda e: e.tensor_tensor(out=s4[0:64, 0:n], in0=s4[0:64, 0:n], in1=s4[0:64, 0:n], op=ALU.mult), reads=[s4k], writes=[s4k])
                    P.op("dve", lambda e: e.tensor_scalar(out=s4[0:64, 0:n], in0=s4[0:64, 0:n], scalar1=-2.0, scalar2=1.0, op0=ALU.mult, op1=ALU.add), reads=[s4k], writes=[s4k])
                    P.op("dve", lambda e: e.scalar_tensor_tensor(out=dst[:, a:a + n], in0=s2[0:64, 0:n], scalar=2.0, in1=s4[0:64, 0:n], op0=ALU.mult, op1=ALU.mult),
                         reads=[s2k, s4k], writes=[dstk])
            sin_layer(hw1[:], 33, embT, "XS", h1, "XS", 2)
            sin_layer(hw2[:], 64, h1, "XS", h2, "h2", 4)
            barrier()
            fwdC, fwdS, invC, invS = (din["c_fwdC_" + nm], din["c_fwdS_" + nm], din["c_invC_" + nm], din["c_invS_" + nm])
            for o in range(2):
                psS, psSk = psfix(0)
                for sc in range(SC):
                    ps, pk = psum()
                    P.op("pe", lambda e: e.matmul(ps[:, 0:512], lhsT=h2[:, sc * 128:(sc + 1) * 128], rhs=hw3b[:, o * 512:(o + 1) * 512], start=True, stop=True),
                         reads=["h2", "hyp"], writes=[pk])
                    P.op("act", lambda e: e.activation(out=Eexp, in_=ndecb, func=AF.Exp, scale=tlT[:, sc:sc + 1]), reads=["ndecb", "tlT"], writes=["Eexp"])
                    P.op("dve", lambda e: e.tensor_tensor(out=Z[:, sc, 0:512].rearrange("p (d c) -> p d c", d=2), in0=ps[:, 0:512].rearrange("p (d c) -> p d c", d=2),
                                                          in1=mkap(Eexp[:, 0:1], 0, [[0, 2], [1, 256]]), op=ALU.mult), reads=[pk, "Eexp"], writes=["Zf"])
                    if sc == 0:
                        P.op("dve", lambda e: e.memset(Z[0:1, 0, 256:512], 0.0), reads=["Zf"], writes=["Zf"])
                    P.op("act", lambda e: e.activation(out=sqt, in_=Z[:, sc, 0:512], func=AF.Square), reads=["Zf"], writes=["sqt"])
                    P.op("pe", lambda e: e.matmul(psS[:, 0:512], lhsT=C["ones_b"][:], rhs=sqt, start=(sc == 0), stop=(sc == SC - 1)),
                         reads=["sqt", "k_ones_b"], writes=[psSk], pe_acc=(sc > 0))
                P.op("act", lambda e: e.copy(out=rsb, in_=psS[:, 0:512]), reads=[psSk], writes=["rsb"])
                P.op("dve", lambda e: e.tensor_tensor(out=rsb[:, 0:256], in0=rsb[:, 0:256], in1=rsb[:, 256:512], op=ALU.add), reads=["rsb"], writes=["rsb"])
                P.op("act", lambda e: e.activation(out=rsb[:, 0:256], in_=rsb[:, 0:256], func=AF.Ln, bias=epsc(1e-6)), reads=["rsb", "epsc"], writes=["rsb"])
                P.op("act", lambda e: e.activation(out=rsb[:, 0:256], in_=rsb[:, 0:256], func=AF.Exp, scale=-0.5), reads=["rsb"], writes=["rsb"])
                for sc in range(SC):
                    P.op("dve", lambda e: e.tensor_tensor(out=Z[:, sc, 0:512].rearrange("p (d c) -> p d c", d=2), in0=Z[:, sc, 0:512].rearrange("p (d c) -> p d c", d=2),
                                                          in1=mkap(rsb[:, 0:1], 0, [[0, 2], [1, 256]]), op=ALU.mult), reads=["Zf", "rsb"], writes=["Zf"])
                zsrc_all, zk = (uh[:, 4:6, :], "uh") if o == 0 else (z1, "z1")
                for j in range(ns):
                    for sc in range(SC):
                        for c in range(2):
                            t0 = s0 + j * L + sc * 128
                            ps, pk = psum()
                            pst = ps[:, 0:64].bitcast(BF16)
                            P.op("pe", lambda e: e.transpose(pst, zsrc_all[:, c, t0:t0 + 128], C["ident_b"][:]), reads=[zk, "k_ident_b"], writes=[pk])
                            P.op("act", lambda e: e.copy(out=Z[:, sc, 512 + j * 256 + c * 128:512 + j * 256 + (c + 1) * 128], in_=pst), reads=[pk], writes=["Zd"])
                blocks = [(cb, min(512, ncols - cb)) for cb in range(0, ncols, 512)]
                for fc in range(FC):
                    tC, tS = tabv[(fc % 2) * 2], tabv[(fc % 2) * 2 + 1]
                    tCk, tSk = "tab%d" % ((fc % 2) * 2), "tab%d" % ((fc % 2) * 2 + 1)
                    tCv = tC[:, 0:SC * 128].rearrange("p (s f) -> p s f", s=SC)
                    tSv = tS[:, 0:SC * 128].rearrange("p (s f) -> p s f", s=SC)
                    P.dma("sp", tCv, fwdC[fc], writes=[tCk])
                    P.dma("act", tSv, fwdS[fc], writes=[tSk])
                    for sc in range(SC):
                        for ri, (tv, tk_) in enumerate(((tCv, tCk), (tSv, tSk))):
                            for bi_, (cb, w) in enumerate(blocks):
                                bank, bk = psfix(ri * 2 + bi_)
                                P.op("pe", lambda e: e.matmul(bank[:, 0:w], lhsT=tv[:, sc, :], rhs=Z[:, sc, cb:cb + w], start=(sc == 0), stop=(sc == SC - 1)),
                                     reads=[tk_, "Zf", "Zd"], writes=[bk], pe_acc=(sc > 0))
                    for ri in range(2):
                        for bi_, (cb, w) in enumerate(blocks):
                            bank, bk = psfix(ri * 2 + bi_)
                            P.op("act", lambda e: e.copy(out=EV[:, ri, cb:cb + w], in_=bank[:, 0:w]), reads=[bk], writes=["EV"])
                    P.op("dve", lambda e: e.tensor_tensor(out=Hc[:, 0, :], in0=EV[:, 0, 0:256], in1=EV[:, 0, 256:512], op=ALU.add), reads=["EV"], writes=["Hc"])
                    P.op("dve", lambda e: e.tensor_tensor(out=Hc[:, 1, :], in0=EV[:, 1, 0:256], in1=EV[:, 1, 256:512], op=ALU.subtract), reads=["EV"], writes=["Hc"])
                    for j in range(ns):
                        vre = EV[:, 0, 512 + j * 256:512 + (j + 1) * 256]
                        vim = EV[:, 1, 512 + j * 256:512 + (j + 1) * 256]
                        t1, t1k = tmp()
                        t2, t2k = tmp()
                        P.op("dve", lambda e: e.tensor_tensor(out=t1[:, 0:256], in0=vre, in1=Hc[:, 0, :], op=ALU.mult), reads=["EV", "Hc"], writes=[t1k])
                        P.op("dve", lambda e: e.tensor_tensor(out=t2[:, 0:256], in0=vim, in1=Hc[:, 1, :], op=ALU.mult), reads=["EV", "Hc"], writes=[t2k])
                        P.op("dve", lambda e: e.tensor_tensor(out=YR[j][:, fc, :], in0=t1[:, 0:256], in1=t2[:, 0:256], op=ALU.subtract), reads=[t1k, t2k], writes=["YR"])
                        P.op("dve", lambda e: e.tensor_tensor(out=t1[:, 256:512], in0=vre, in1=Hc[:, 1, :], op=ALU.mult), reads=["EV", "Hc"], writes=[t1k])
                        P.op("dve", lambda e: e.tensor_tensor(out=t2[:, 256:512], in0=vim, in1=Hc[:, 0, :], op=ALU.mult), reads=["EV", "Hc"], writes=[t2k])
                        P.op("dve", lambda e: e.tensor_tensor(out=YI[j][:, fc, :], in0=t1[:, 256:512], in1=t2[:, 256:512], op=ALU.add), reads=[t1k, t2k], writes=["YI"])
                ttl = [(a, min(512, L - a)) for a in range(0, L, 512)]
                accs = {}
                bi2 = 0
                for j in range(ns):
                    for c in range(2):
                        for ti in range(len(ttl)):
                            accs[(j, c, ti)] = (PS[bi2], "ps%d" % bi2)
                            bi2 += 1
                assert bi2 <= 8
                for fc in range(FC):
                    tC, tS = tabv[(fc % 2) * 2], tabv[(fc % 2) * 2 + 1]
                    tCk, tSk = "tab%d" % ((fc % 2) * 2), "tab%d" % ((fc % 2) * 2 + 1)
                    P.dma("sp", tC[:, 0:L], invC[fc], writes=[tCk])
                    P.dma("act", tS[:, 0:L], invS[fc], writes=[tSk])
                    for j in range(ns):
                        for c in range(2):
                            for ti, (a, w) in enumerate(ttl):
                                acc, acck = accs[(j, c, ti)]
                                P.op("pe", lambda e: e.matmul(acc[:, 0:w], lhsT=YR[j][:, fc, c * 128:(c + 1) * 128], rhs=tC[:, a:a + w], start=(fc == 0), stop=False),
                                     reads=["YR", tCk], writes=[acck], pe_acc=(fc > 0))
                                P.op("pe", lambda e: e.matmul(acc[:, 0:w], lhsT=YI[j][:, fc, c * 128:(c + 1) * 128], rhs=tS[:, a:a + w], start=False, stop=(fc == FC - 1)),
                                     reads=["YI", tSk], writes=[acck], pe_acc=True)
                for j in range(ns):
                    for c in range(2):
                        for ti, (a, w) in enumerate(ttl):
                            acc, acck = accs[(j, c, ti)]
                            g0 = s0 + j * L + a
                            zs = zsrc_all[:, c, g0:g0 + w]
                            xg = uh[:, (0 if o == 0 else 2) + c, g0:g0 + w]
                            tt, tk = tmp()
                            P.op("dve", lambda e: e.scalar_tensor_tensor(out=tt[:, 0:w], in0=zs, scalar=hbias[:, o, c:c + 1], in1=acc[:, 0:w], op0=ALU.mult, op1=ALU.add),
                                 reads=[zk, "hyp", acck], writes=[tk])
                            dstz, dk = (z1[:, c, g0:g0 + w], "z1") if o == 0 else (mixT[:, 6 + c, g0:g0 + w], "mixT")
                            P.op("dve", lambda e: e.tensor_tensor(out=dstz, in0=tt[:, 0:w], in1=xg, op=ALU.mult), reads=[tk, "uh"], writes=[dk])
                barrier()
        dbg_out("yc%d" % l, mixT[:, 6:8, :], ["mixT"])

    for l in range(DEPTH):
        lam0 = lam_init_of(l)
        P.dma("sp", bmod[:], b_modT[l], writes=["bmod"])
        P.dma("sp", gn[:], gains[l], writes=["gains"])
        for j in range(48):
            wm, wk = load_w(w_mod[l][:, j * 128:(j + 1) * 128], KC)
            ps, pk = psum()
            for c in range(KC):
                P.op("pe", lambda e, wm=wm, c=c, ps=ps: e.matmul(ps[:, 0:2], lhsT=wm[:, c, :], rhs=scv[:, c, :],
                                                                start=(c == 0), stop=(c == KC - 1)),
                     reads=[wk, "scvec"], writes=[pk], pe_acc=(c > 0))
            P.op("dve", lambda e, ps=ps, j=j: e.tensor_scalar(out=modT[:, j, :], in0=ps[:, 0:2], scalar1=bmod[:, j:j + 1],
                                                              scalar2=None, op0=ALU.add),
                 reads=[pk, "bmod"], writes=["modT"])
        for (o, jsc, gi) in ((0, 8, 0), (2, 32, 2)):
            for cnd in range(2):
                P.op("dve", lambda e, o=o, jsc=jsc, gi=gi, cnd=cnd: e.scalar_tensor_tensor(
                    out=sc_all[:, o, :, cnd], in0=modT[:, jsc:jsc + 8, cnd], scalar=1.0, in1=gn[:, gi, :],
                    op0=ALU.add, op1=ALU.mult), reads=["modT", "gains"], writes=["sc_all"])
        for (o, jgt, gi) in ((1, 16, 1), (3, 40, 3)):
            for cnd in range(2):
                P.op("dve", lambda e, o=o, jgt=jgt, gi=gi, cnd=cnd: e.tensor_tensor(
                    out=sc_all[:, o, :, cnd], in0=modT[:, jgt:jgt + 8, cnd], in1=gn[:, gi, :], op=ALU.mult),
                    reads=["modT", "gains"], writes=["sc_all"])

        def norm_mod(src_tile_fn, src_keys, dst, dst_key_fn, sci, shj):
            for (t0, n, cnd) in cfg.tiles:
                rms_stats(lambda c: src_tile_fn(c, t0, n), src_keys, t0, n, rstd, "rstd")
                for c in range(KC):
                    tt, tk = tmp()
                    P.op("dve", lambda e, c=c, tt=tt: e.tensor_tensor(out=tt[:, 0:n], in0=src_tile_fn(c, t0, n), in1=rstd[:, 0:n],
                                                                     op=ALU.mult), reads=src_keys + ["rstd"], writes=[tk])
                    P.op("act", lambda e, c=c, tt=tt: e.activation(out=dst[:, c, t0:t0 + n], in_=tt[:, 0:n], func=AF.Identity,
                                                                  scale=sc_all[:, sci, c, cnd:cnd + 1],
                                                                  bias=modT[:, shj + c, cnd:cnd + 1]),
                         reads=[tk, "sc_all", "modT"], writes=[dst_key_fn(c, t0)])

        norm_mod(lambda c, t0, n: xT[:, c, t0:t0 + n], ["xT"], hT, lambda c, t0: "hT", 0, 0)
        if l == 0:
            dbg_out("h0", hT, ["hT"])
        P.dma("sp", x_spill[:], xT, reads=["xT"], writes=["x_spill"])
        barrier()

        def proj_fm(wsrc, col0, ncols_chunks, rhs, rhs_key, cb, kchunks=KC):
            for ci in range(ncols_chunks):
                wt, wk = load_w(wsrc[:, col0 + ci * 128: col0 + (ci + 1) * 128], kchunks)
                for (t0, n, cnd) in cfg.tiles:
                    ps, pk = psum()
                    for c in range(kchunks):
                        P.op("pe", lambda e, wt=wt, c=c, ps=ps, t0=t0, n=n: e.matmul(
                            ps[:, 0:n], lhsT=wt[:, c, :], rhs=rhs[:, c, t0:t0 + n], start=(c == 0), stop=(c == kchunks - 1)),
                            reads=[wk, rhs_key], writes=[pk], pe_acc=(c > 0))
                    cb(ps, pk, ci, t0, n, cnd)

        carver.reset(0, HB0)
        rkv = lor = None
        if "rwkv" in mixers:
            rkv = carver.get([128, 6, NT], BF16)
            lor = carver.get([128, 3, NT], BF16)

        def cbA(ps, pk, ci, t0, n, cnd):
            if ci < 6:
                P.op("act", lambda e: e.copy(out=rkv[:, ci, t0:t0 + n], in_=ps[:, 0:n]), reads=[pk], writes=["rkv"])
            else:
                fn = {6: AF.Tanh, 7: AF.Identity, 8: AF.Sigmoid}[ci]
                P.op("act", lambda e: e.activation(out=lor[:, ci - 6, t0:t0 + n], in_=ps[:, 0:n], func=fn),
                     reads=[pk], writes=["lor"])
        if "rwkv" in mixers:
            proj_fm(w_in[l], 0, 9, hT, "hT", cbA)
            rwkv_layer(l, rkv, lor)
        else:
            P.op("pool", lambda e: e.memset(mixT[:, 0:2, :], 0.0), writes=["mixT"])

        if "attn" in mixers:
            attn_layer(l, lam0)
        else:
            P.op("pool", lambda e: e.memset(mixT[:, 2:6, :], 0.0), writes=["mixT"])

        if "hyena" in mixers:
            hyena_layer(l)
        else:
            P.op("pool", lambda e: e.memset(mixT[:, 6:8, :], 0.0), writes=["mixT"])

        barrier()
        P.dma("sp", xT, x_spill[:], reads=["x_spill"], writes=["xT"])
        carver.reset(XB, MB0)
        mo = carver.get([128, KC, 512])
        for (t0, n, cnd) in cfg.tiles:
            for ci in range(KC):
                wt, wk = load_w(w_out[l][:, ci * 128:(ci + 1) * 128], KC)
                ps, pk = psum()
                for c in range(KC):
                    P.op("pe", lambda e, wt=wt, c=c, ps=ps: e.matmul(ps[:, 0:n], lhsT=wt[:, c, :], rhs=mixT[:, c, t0:t0 + n],
                                                                    start=(c == 0), stop=(c == KC - 1)),
                         reads=[wk, "mixT"], writes=[pk], pe_acc=(c > 0))
                P.op("act", lambda e, ci=ci, ps=ps: e.copy(out=mo[:, ci, 0:n], in_=ps[:, 0:n]), reads=[pk], writes=["mo"])
            rms_stats(lambda c: mo[:, c, 0:n], ["mo"], t0, n, rstd, "rstd")
            for c in range(KC):
                tt, tk = tmp()
                P.op("dve", lambda e, c=c, tt=tt: e.tensor_tensor(out=tt[:, 0:n], in0=mo[:, c, 0:n], in1=rstd[:, 0:n], op=ALU.mult),
                     reads=["mo", "rstd"], writes=[tk])
                P.op("dve", lambda e, c=c, tt=tt: e.scalar_tensor_tensor(
                    out=xT[:, c, t0:t0 + n], in0=tt[:, 0:n], scalar=sc_all[:, 1, c, cnd:cnd + 1], in1=xT[:, c, t0:t0 + n],
                    op0=ALU.mult, op1=ALU.add), reads=[tk, "sc_all", "xT"], writes=["xT"])
        if l == 0:
            dbg_out("x1", xT, ["xT"])

        barrier()
        carver.reset(XB, BIGB)
        h2 = carver.get([128, KC, 512], BF16)
        f1 = carver.get([128, 32, 512], BF16)
        fo = carver.get([128, KC, 512])
        wff2 = [carver.get([128, 32, 128], BF16) for _ in range(2)]
        for (t0, n, cnd) in cfg.tiles:
            rms_stats(lambda c: xT[:, c, t0:t0 + n], ["xT"], t0, n, rstd, "rstd")
            for c in range(KC):
                tt, tk = tmp()
                P.op("dve", lambda e, c=c, tt=tt: e.tensor_tensor(out=tt[:, 0:n], in0=xT[:, c, t0:t0 + n], in1=rstd[:, 0:n],
                                                                 op=ALU.mult), reads=["xT", "rstd"], writes=[tk])
                P.op("act", lambda e, c=c, tt=tt: e.activation(out=h2[:, c, 0:n], in_=tt[:, 0:n], func=AF.Identity,
                                                              scale=sc_all[:, 2, c, cnd:cnd + 1], bias=modT[:, 24 + c, cnd:cnd + 1]),
                     reads=[tk, "sc_all", "modT"], writes=["h2"])
            for ci in range(32):
                wt, wk = load_w(w_ff1[l][:, ci * 128:(ci + 1) * 128], KC)
                ps, pk = psum()
                for c in range(KC):
                    P.op("pe", lambda e, wt=wt, c=c, ps=ps: e.matmul(ps[:, 0:n], lhsT=wt[:, c, :], rhs=h2[:, c, 0:n],
                                                                    start=(c == 0), stop=(c == KC - 1)),
                         reads=[wk, "h2"], writes=[pk], pe_acc=(c > 0))
                tt, tk = tmp()
                P.op("act", lambda e, ps=ps, tt=tt: e.activation(out=tt[:, 0:n], in_=ps[:, 0:n], func=AF.Relu), reads=[pk], writes=[tk])
                P.op("dve", lambda e, ci=ci, tt=tt: e.tensor_tensor(out=f1[:, ci, 0:n], in0=tt[:, 0:n], in1=tt[:, 0:n], op=ALU.mult),
                     reads=[tk], writes=["f1"])
            for ci in range(KC):
                wt, wk = wff2[ci % 2], "wff2_%d" % (ci % 2)
                load_w(w_ff2[l][:, ci * 128:(ci + 1) * 128], 32, dst=wt, key=wk)
                ps, pk = psum()
                for c in range(32):
                    P.op("pe", lambda e, wt=wt, c=c, ps=ps: e.matmul(ps[:, 0:n], lhsT=wt[:, c, :], rhs=f1[:, c, 0:n],
                                                                    start=(c == 0), stop=(c == 31)),
                         reads=[wk, "f1"], writes=[pk], pe_acc=(c > 0))
                P.op("act", lambda e, ci=ci, ps=ps: e.copy(out=fo[:, ci, 0:n], in_=ps[:, 0:n]), reads=[pk], writes=["fo"])
            rms_stats(lambda c: fo[:, c, 0:n], ["fo"], t0, n, rstd, "rstd")
            for c in range(KC):
                tt, tk = tmp()
                P.op("dve", lambda e, c=c, tt=tt: e.tensor_tensor(out=tt[:, 0:n], in0=fo[:, c, 0:n], in1=rstd[:, 0:n], op=ALU.mult),
                     reads=["fo", "rstd"], writes=[tk])
                P.op("dve", lambda e, c=c, tt=tt: e.scalar_tensor_tensor(
                    out=xT[:, c, t0:t0 + n], in0=tt[:, 0:n], scalar=sc_all[:, 3, c, cnd:cnd + 1], in1=xT[:, c, t0:t0 + n],
                    op0=ALU.mult, op1=ALU.add), reads=[tk, "sc_all", "xT"], writes=["xT"])

    out_toks.append(P.dma("sp", yT_out[:], xT, reads=["xT"], writes=["yT_out"]))
    P.finish_wait("sp", out_toks)
    P.emit(st)
    st.close()
    return nc


def _lay(a):
    return np.ascontiguousarray(a)


def prep_inputs(cfg, inp, consts):
    D = cfg.depth
    f = np.float32
    shared = {}
    for k, v in consts.items():
        shared["c_" + k] = v
    shared["w_mod"] = _lay(inp["w_mod"][:D])
    shared["b_modT"] = _lay(inp["b_mod"][:D].reshape(D, 48, 128).transpose(0, 2, 1))
    g4 = np.stack([inp["g_mix_pre"][:D], inp["g_mix_post"][:D], inp["g_ffn_pre"][:D], inp["g_ffn_post"][:D]], axis=1)
    shared["gains"] = _lay(g4.reshape(D, 4, 8, 128).transpose(0, 3, 1, 2))
    for k in ("w_in", "w_out", "w_ff1", "w_ff2"):
        shared[k] = _lay(inp[k][:D])
    shared["rw_conv"] = _lay(inp["rwkv_conv"][:D].reshape(D, 3, 6, 128).transpose(0, 3, 2, 1))
    shared["rw_w0"] = _lay(inp["rwkv_w0"][:D].reshape(D, 2, 2, 128).transpose(0, 3, 1, 2))
    shared["rw_a0"] = _lay(inp["rwkv_a0"][:D].reshape(D, 2, 2, 128).transpose(0, 3, 1, 2))
    shared["rw_w2"] = _lay(inp["rwkv_w2"][:D].reshape(D, 128, 256))
    shared["rw_a2"] = _lay(inp["rwkv_a2"][:D].reshape(D, 128, 256))
    shared["rw_g2"] = _lay(inp["rwkv_g2"][:D])
    v3 = np.stack([inp["rwkv_kk"][:D], inp["rwkv_ka"][:D], inp["rwkv_rk"][:D].reshape(D, 256)], axis=1)
    shared["rw_vec"] = _lay(v3.reshape(D, 3, 2, 128).transpose(0, 3, 1, 2))
    ln = np.stack([inp["rwkv_ln_w"][:D], inp["rwkv_ln_b"][:D]], axis=1)
    shared["rw_ln"] = _lay(ln.reshape(D, 2, 4, 64).transpose(0, 3, 1, 2))
    lam = np.stack([inp["diff_lq1"][:D], inp["diff_lk1"][:D], inp["diff_lq2"][:D], inp["diff_lk2"][:D]], axis=1)
    shared["dlam"] = _lay(np.broadcast_to(lam[:, None], (D, 128, 4, 64)))
    shared["dsub"] = _lay(inp["diff_subln"][:D].reshape(D, 128, 1))
    hc = np.concatenate([inp["hy_conv_w"][:D], inp["hy_conv_b"][:D][:, None]], axis=1)
    shared["hy_conv"] = _lay(hc.reshape(D, 4, 6, 128).transpose(0, 3, 2, 1))
    shared["hy_w1"] = _lay(inp["hy_w1"][:D])
    shared["hy_vec"] = _lay(np.stack([inp["hy_b1"][:D], inp["hy_freq"][:D], inp["hy_b2"][:D]], axis=2))
    shared["hy_w2"] = _lay(inp["hy_w2"][:D])
    shared["hy_w3"] = _lay(inp["hy_w3"][:D])
    shared["hy_decb"] = _lay(np.broadcast_to(inp["hy_decay"][:D][:, None, :], (D, 128, 256)))
    shared["hy_bias"] = _lay(inp["hy_bias"][:D].reshape(D, 2, 2, 128).transpose(0, 3, 1, 2))
    maps = []
    nb_s = inp["x_sample"].shape[0]
    for core in range(8):
        b = core % nb_s
        m = dict(shared)
        xs = inp["x_sample"][b]
        xp = inp["x_prompt"][cfg.nps * core:cfg.nps * (core + 1)].reshape(-1, 1024)
        x = np.concatenate([xs, xp], axis=0)
        m["xT"] = _lay(x.T.reshape(8, 128, cfg.nt).transpose(1, 0, 2))
        cv2 = np.stack([inp["c"][b], inp["c_ctx"]], axis=1)
        m["cvecT"] = _lay(cv2.reshape(8, 128, 2).transpose(1, 0, 2))
        st0 = inp["state_rwkv"][b][:D]
        m["st_in"] = _lay(st0.reshape(D, 2, 2, 2, 64, 64).transpose(0, 3, 5, 2, 1, 4).reshape(D, 128, 4, 64))
        ck = inp["cache_k"][b][:D]
        m["ckT"] = _lay(ck.reshape(D, cfg.past, 4, 128).transpose(0, 2, 3, 1))
        cvv = inp["cache_v"][b][:D]
        m["cv"] = _lay(cvv.reshape(D, cfg.past // 128, 128, 4, 128).transpose(0, 2, 1, 3, 4))
        maps.append({k: np.ascontiguousarray(v) for k, v in m.items()})
    return maps


_CACHE = {}


def kernel(**inputs):
    inp = {k: np.asarray(v) for k, v in inputs.items()}
    cfg = Cfg()
    consts = host_consts(cfg)
    if "nc" not in _CACHE:
        _CACHE["nc"] = build_program(cfg, consts, mixers=("rwkv", "attn", "hyena"))
    nc = _CACHE["nc"]
    maps = prep_inputs(cfg, inp, consts)
    res = run_bass_kernel_spmd(nc, maps, core_ids=list(range(8)))
    R = res.results
    D = cfg.depth
    B = inp["x_prompt"].shape[0]
    y_prompt = np.zeros((B, cfg.lp, 1024), np.float32)
    y_sample = np.zeros((inp["x_sample"].shape[0], cfg.ls, 1024), np.float32)
    new_state = np.zeros((B, D, 2, 4, 64, 64), np.float32)
    new_k = np.zeros((B, D, cfg.lp, 4, 2, 64), np.float32)
    new_v = np.zeros((B, D, cfg.lp, 4, 128), np.float32)
    for core in range(8):
        yT = np.asarray(R[core]["yT"])
        y = yT.transpose(1, 0, 2).reshape(1024, cfg.nt).T
        if core < y_sample.shape[0]:
            y_sample[core] = y[:cfg.ls]
        for i in range(cfg.nps):
            bp = cfg.nps * core + i
            y_prompt[bp] = y[cfg.ls + i * cfg.lp: cfg.ls + (i + 1) * cfg.lp]
            new_state[bp] = np.asarray(R[core]["st_out"])[:, i]
            new_k[bp] = np.asarray(R[core]["kc_out"])[:, i * cfg.lp:(i + 1) * cfg.lp].reshape(D, cfg.lp, 4, 2, 64)
            new_v[bp] = np.asarray(R[core]["vc_out"])[:, i * cfg.lp:(i + 1) * cfg.lp].reshape(D, cfg.lp, 4, 128)
    return (y_prompt, y_sample, new_state, new_k, new_v)
```

```python
import math
import os
from contextlib import ExitStack
import numpy as np
import ml_dtypes
import concourse.bass as bass
import concourse.mybir as mybir
from concourse.bass_types import AP
from concourse.bass_utils import run_bass_kernel_spmd

F32 = mybir.dt.float32
BF16 = mybir.dt.bfloat16
AF = mybir.ActivationFunctionType
ALU = mybir.AluOpType
AX = mybir.AxisListType

ENGS = ("pe", "act", "dve", "pool", "sp")
NDSEM = 16


class _Rec:
    def __getattr__(self, name):
        def f(*a, **k):
            self.call = (name, a, k)
            return self
        return f


class Prog:
    def __init__(self, nc):
        self.nc = nc
        self.ops = {e: [] for e in ENGS}
        self.cnt = {e: 0 for e in ENGS}
        self.waited = {e: {} for e in ENGS}
        self.lastw = {}
        self.readers = {}
        self.dq = {e: {"next": 0, "use": [0] * NDSEM} for e in ("sp", "act", "pool")}
        self.sems = {}
        self.nps = 0

    def _need(self, e, deps):
        waits = []
        w = self.waited[e]
        best = {}
        for (sk, v) in deps:
            if v > best.get(sk, 0):
                best[sk] = v
        for sk, v in best.items():
            if w.get(sk, 0) >= v:
                continue
            w[sk] = v
            waits.append((sk, v))
        return waits

    def _deps(self, reads, writes):
        deps = []
        for k in reads:
            t = self.lastw.get(k)
            if t is not None:
                deps.append(t)
        for k in writes:
            t = self.lastw.get(k)
            if t is not None:
                deps.append(t)
            for sk, v in self.readers.get(k, {}).items():
                deps.append((sk, v))
        return deps

    def _commit(self, tok, reads, writes):
        for k in reads:
            r = self.readers.setdefault(k, {})
            if tok[1] > r.get(tok[0], 0):
                r[tok[0]] = tok[1]
        for k in writes:
            self.lastw[k] = tok
            self.readers[k] = {}

    def op(self, e, fn, reads=(), writes=(), pe_acc=False):
        rec = _Rec()
        fn(rec)
        call = rec.call
        fn = lambda eh, call=call: getattr(eh, call[0])(*call[1], **call[2])
        deps = self._deps(reads, writes)
        if pe_acc:
            deps = [d for d in deps if d[0] != ("c", "pe")]
        waits = self._need(e, deps)
        self.cnt[e] += 1
        tok = (("c", e), self.cnt[e])
        self.ops[e].append((waits, fn, ("c", e)))
        self._commit(tok, reads, writes)
        return tok

    def dma(self, q, out, in_, reads=(), writes=(), **kw):
        d = self.dq[q]
        i = d["next"]
        d["next"] = (i + 1) % NDSEM
        deps = self._deps(reads, writes)
        if d["use"][i] > 0:
            deps.append((("d", q, i), 16 * d["use"][i]))
        waits = self._need(q, deps)
        d["use"][i] += 1
        tok = (("d", q, i), 16 * d["use"][i])

        def fn(eh, out=out, in_=in_, kw=kw):
            return eh.dma_start(out=out, in_=in_, **kw)
        self.ops[q].append((waits, fn, ("d", q, i)))
        self._commit(tok, reads, writes)
        return tok

    def barrier(self):
        toks = [(("c", e), self.cnt[e]) for e in ENGS if self.cnt[e] > 0]
        for q in ("sp", "act", "pool"):
            for i in range(NDSEM):
                u = self.dq[q]["use"][i]
                if u > 0:
                    toks.append((("d", q, i), 16 * u))
        for e in ENGS:
            waits = self._need(e, list(toks))
            if waits:
                self.ops[e].append((waits, None, None))

    def finish_wait(self, e, tokens):
        waits = self._need(e, list(tokens))
        self.ops[e].append((waits, None, None))

    def emit(self, st):
        nc = self.nc
        for e in ENGS:
            self.sems[("c", e)] = st.enter_context(nc.semaphore("c_" + e))
        for q in ("sp", "act", "pool"):
            for i in range(NDSEM):
                self.sems[("d", q, i)] = st.enter_context(nc.semaphore("d_%s_%d" % (q, i)))
        block = st.enter_context(nc.Block())

        def mk(e):
            def body(eh):
                for (waits, fn, inc) in self.ops[e]:
                    for (sk, v) in waits:
                        eh.wait_ge(self.sems[sk], v)
                    if fn is None:
                        continue
                    ins = fn(eh)
                    ins.then_inc(self.sems[inc], 1 if inc[0] == "c" else 16)
            return body
        block.tensor(mk("pe"))
        block.scalar(mk("act"))
        block.vector(mk("dve"))
        block.gpsimd(mk("pool"))
        block.sync(mk("sp"))


class Cfg:
    def __init__(self, depth=4, ls=2048, lp=256, past=256, nps=2):
        self.depth, self.ls, self.lp, self.past, self.nps = depth, ls, lp, past, nps
        self.nt = ls + nps * lp
        self.d = 1024
        self.kc = 8
        self.incols = 3456
        self.dff = 4096
        self.tiles = []
        for s in range(0, ls, 512):
            self.tiles.append((s, min(512, ls - s), 0))
        for s in range(ls, self.nt, 512):
            self.tiles.append((s, min(512, self.nt - s), 1))
        self.seqs = [(0, ls, 0)] + [(ls + i * lp, lp, 1) for i in range(nps)]


def lam_init_of(l):
    return 0.8 - 0.6 * math.exp(-0.3 * l)


def host_consts(cfg):
    c = {}
    c["ident_f"] = np.eye(128, dtype=np.float32)
    c["ident_b"] = np.eye(128, dtype=np.float32).astype(ml_dtypes.bfloat16)
    ob = np.zeros((128, 128), np.float32)
    ob[:64, :64] = 1.0
    ob[64:, 64:] = 1.0
    c["onesblk_f"] = ob
    c["onesblk_b"] = ob.astype(ml_dtypes.bfloat16)
    c["ones_b"] = np.ones((128, 128), np.float32).astype(ml_dtypes.bfloat16)
    c["ones_f"] = np.ones((128, 128), np.float32)
    c["i2_f"] = np.concatenate([np.eye(64, dtype=np.float32)] * 2, axis=0)
    sel = np.zeros((64, 2, 128), np.float32)
    for j in range(2):
        sel[np.arange(64), j, j * 64 + np.arange(64)] = 1.0
    c["sel_f"] = sel
    R = np.zeros((128, 128), np.float32)
    for m in range(2):
        b = m * 64
        for i in range(16):
            R[b + 16 + i, b + i] = -1.0
            R[b + i, b + 16 + i] = 1.0
            R[b + 48 + i, b + 32 + i] = -1.0
            R[b + 32 + i, b + 48 + i] = 1.0
    c["ropeR"] = R.astype(ml_dtypes.bfloat16)
    L = cfg.ls
    rows = L // 64
    row = np.repeat(np.arange(rows), 64).astype(np.float32)
    col = np.tile(np.arange(64), rows).astype(np.float32)
    inv = (10000.0 ** (-np.arange(0, 32, 2, dtype=np.float32) / 32)).astype(np.float32)
    ang_r = row[:, None] * inv[None]
    ang_c = col[:, None] * inv[None]
    ang = np.concatenate([ang_r, ang_r, ang_c, ang_c], axis=-1)
    cos = np.cos(ang).astype(np.float32).T
    sin = np.sin(ang).astype(np.float32).T
    c["cos"] = np.concatenate([cos, cos], axis=0).astype(ml_dtypes.bfloat16)
    c["sin"] = np.concatenate([sin, sin], axis=0).astype(ml_dtypes.bfloat16)
    for nm, Lx in (("s", cfg.ls), ("p", cfg.lp)):
        t = np.linspace(0.0, 1.0, Lx, dtype=np.float32)[:, None]
        ang2 = (np.float32(2.0 * math.pi / Lx) * np.arange(Lx, dtype=np.float32))[:, None]
        bands = np.linspace(1e-4, 15, 16, dtype=np.float32)[None, :]
        emb = np.concatenate([t, np.cos(bands * ang2), -np.sin(bands * ang2)], axis=-1).astype(np.float32)
        c["embT_" + nm] = emb.T.copy()
        SC = Lx // 128
        FC = (Lx + 1 + 127) // 128
        c["tlinT_" + nm] = t[:, 0].reshape(SC, 128).T.copy()
        n2 = 2 * Lx
        f = np.arange(FC * 128, dtype=np.int64)
        sidx = np.arange(Lx, dtype=np.int64)
        m = (sidx[:, None] * f[None, :]) % n2
        ang3 = 2.0 * np.pi * m.astype(np.float64) / n2
        valid = (f <= Lx).astype(np.float64)[None, :]
        Cf = np.cos(ang3) * valid
        Sf = -np.sin(ang3) * valid
        def lay_f(M):
            return np.ascontiguousarray(M.reshape(SC, 128, FC, 128).transpose(2, 1, 0, 3)).astype(ml_dtypes.bfloat16)
        c["fwdC_" + nm] = lay_f(Cf)
        c["fwdS_" + nm] = lay_f(Sf)
        wgt = np.where((f == 0) | (f == Lx), 1.0, 2.0) * (f <= Lx) / n2
        Ci = (np.cos(ang3) * wgt[None, :]).T
        Si = (-np.sin(ang3) * wgt[None, :]).T
        c["invC_" + nm] = np.ascontiguousarray(Ci.reshape(FC, 128, Lx)).astype(ml_dtypes.bfloat16)
        c["invS_" + nm] = np.ascontiguousarray(Si.reshape(FC, 128, Lx)).astype(ml_dtypes.bfloat16)
    return c


CONST_SHAPES = None


def build_program(cfg, consts, debug=(), mixers=()):
    nc = bass.Bass("TRN2", target_bir_lowering=False)
    D, KC, NT, LS, LP = cfg.d, cfg.kc, cfg.nt, cfg.ls, cfg.lp
    DEPTH = cfg.depth
    NPS = cfg.nps
    PAST = cfg.past
    din = {}

    def inp(name, shape, dt=F32):
        din[name] = nc.dram_tensor(name, list(shape), dt, kind="ExternalInput").ap()
        return din[name]

    def outp(name, shape):
        din[name] = nc.dram_tensor(name, list(shape), F32, kind="ExternalOutput").ap()
        return din[name]

    for k, v in consts.items():
        inp("c_" + k, v.shape, BF16 if v.dtype == ml_dtypes.bfloat16 else F32)
    xT_in = inp("xT", [128, KC, NT])
    cvec = inp("cvecT", [128, KC, 2])
    w_mod = inp("w_mod", [DEPTH, D, 6 * D])
    b_modT = inp("b_modT", [DEPTH, 128, 48])
    gains = inp("gains", [DEPTH, 128, 4, KC])
    w_in = inp("w_in", [DEPTH, D, cfg.incols])
    w_out = inp("w_out", [DEPTH, D, D])
    w_ff1 = inp("w_ff1", [DEPTH, D, cfg.dff])
    w_ff2 = inp("w_ff2", [DEPTH, cfg.dff, D])
    rw_conv = inp("rw_conv", [DEPTH, 128, 6, 3])
    rw_w0 = inp("rw_w0", [DEPTH, 128, 2, 2])
    rw_a0 = inp("rw_a0", [DEPTH, 128, 2, 2])
    rw_w2 = inp("rw_w2", [DEPTH, 128, 256])
    rw_a2 = inp("rw_a2", [DEPTH, 128, 256])
    rw_g2 = inp("rw_g2", [DEPTH, 128, 256])
    rw_vec = inp("rw_vec", [DEPTH, 128, 3, 2])
    rw_ln = inp("rw_ln", [DEPTH, 64, 2, 4])
    st_in = inp("st_in", [DEPTH, 128, 4, 64])
    ckT = inp("ckT", [DEPTH, 4, 128, PAST])
    cv = inp("cv", [DEPTH, 128, PAST // 128, 4, 128])
    dlam = inp("dlam", [DEPTH, 128, 4, 64])
    dsub = inp("dsub", [DEPTH, 128, 1])
    hy_conv = inp("hy_conv", [DEPTH, 128, 6, 4])
    hy_w1 = inp("hy_w1", [DEPTH, 33, 64])
    hy_vec = inp("hy_vec", [DEPTH, 64, 3])
    hy_w2 = inp("hy_w2", [DEPTH, 64, 64])
    hy_w3 = inp("hy_w3", [DEPTH, 64, 1024])
    hy_decb = inp("hy_decb", [DEPTH, 128, 256])
    hy_bias = inp("hy_bias", [DEPTH, 128, 2, 2])

    yT_out = outp("yT", [128, KC, NT])
    st_out = outp("st_out", [DEPTH, NPS, 2, 4, 64, 64])
    kc_out = outp("kc_out", [DEPTH, NPS * LP, 512])
    vc_out = outp("vc_out", [DEPTH, NPS * LP, 512])
    dbg = {}
    for nm, shape in debug:
        dbg[nm] = outp("dbg_" + nm, shape)
    x_spill = nc.dram_tensor("x_spill", [128, KC, NT], F32, kind="Internal").ap()

    st = ExitStack()
    P = Prog(nc)
    out_toks = []

    def sb(name, shape, dt=F32):
        return st.enter_context(nc.sbuf_tensor("s_" + name, list(shape), dt))

    C = {}
    for k, v in consts.items():
        if k[:4] in ("embT", "tlin", "fwdC", "fwdS", "invC", "invS"):
            continue
        if k in ("cos", "sin") and "attn" not in mixers:
            continue
        t = sb("k_" + k, v.shape, BF16 if v.dtype == ml_dtypes.bfloat16 else F32)
        P.dma("sp", t[:], din["c_" + k][:], writes=["k_" + k])
        C[k] = t
    CK = {k: ["k_" + k] for k in C}

    BIGB = getattr(cfg, 'bigb', 170 * 1024)
    XB = KC * NT * 4
    MB0 = BIGB - KC * NT * 2
    HB0 = MB0 - KC * NT * 2
    big = sb("big", [128, BIGB // 4])
    xT = big[:, 0:XB // 4].rearrange("p (c t) -> p c t", c=KC)
    P.dma("sp", xT, xT_in[:], writes=["xT"])
    cv_t = sb("cvec", [128, KC, 2])
    P.dma("sp", cv_t[:], cvec[:], writes=["cvec"])
    scv = sb("scvec", [128, KC, 2], BF16)
    P.op("act", lambda e: e.activation(out=scv[:], in_=cv_t[:], func=AF.Silu), reads=["cvec"], writes=["scvec"])

    PS = [st.enter_context(nc.psum_tensor("ps%d" % i, [128, 512], F32)) for i in range(8)]
    psn = [0]

    def psum():
        i = psn[0] % 4
        psn[0] += 1
        return PS[i], "ps%d" % i

    def psfix(i):
        return PS[4 + i], "ps%d" % (4 + i)

    hT = big[:, HB0 // 4:MB0 // 4].bitcast(BF16).rearrange("p (c t) -> p c t", c=KC)
    mixT = big[:, MB0 // 4:BIGB // 4].bitcast(BF16).rearrange("p (c t) -> p c t", c=KC)
    h_spill = nc.dram_tensor("h_spill", [128, KC, NT], BF16, kind="Internal").ap()
    modT = sb("modT", [128, 48, 2])
    bmod = sb("bmod", [128, 48])
    gn = sb("gains", [128, 4, KC])
    sc_all = sb("sc_all", [128, 4, KC, 2])
    wst = [sb("wst%d" % i, [128, KC, 128], BF16) for i in range(3)]
    wstn = [0]
    tmpA = [sb("tmpA%d" % i, [128, 512]) for i in range(3)]
    tmpn = [0]
    rstd = sb("rstd", [128, 512])
    sqb = [sb("sqb%d" % i, [128, 512], BF16) for i in range(2)]

    def tmp():
        i = tmpn[0] % 3
        tmpn[0] += 1
        return tmpA[i], "tmpA%d" % i

    arena = big

    class Carver:
        def __init__(self):
            self.off = 0
            self.lim = BIGB

        def reset(self, base=0, lim=None):
            self.off = base
            self.lim = BIGB if lim is None else lim

        def get(self, shape, dt=F32):
            n = int(np.prod(shape[1:]))
            nbytes = n * (4 if dt == F32 else 2)
            nbytes = (nbytes + 31) // 32 * 32
            assert self.off + nbytes <= self.lim, ("arena overflow", self.off, nbytes, self.lim)
            a = arena[0:shape[0], self.off // 4:(self.off + nbytes) // 4]
            self.off += nbytes
            if dt != F32:
                a = a.bitcast(dt)
                a = a[:, 0:n]
            if len(shape) > 2:
                names = " ".join("d%d" % i for i in range(1, len(shape)))
                kw = {"d%d" % i: shape[i] for i in range(1, len(shape))}
                a = a.rearrange("p (%s) -> p %s" % (names, names), **kw)
            return a
    carver = Carver()

    def load_w(src_ap, kchunks, dst=None, key=None):
        if dst is None:
            i = wstn[0] % 3
            wstn[0] += 1
            dst, key = wst[i], "wst%d" % i
        P.dma("pool", dst[:, 0:kchunks, 0:src_ap.shape[1]],
              src_ap.rearrange("(kc p) c -> p kc c", p=128), writes=[key])
        return dst, key

    def rms_stats(src_fn, src_keys, t0, n, out_rstd, out_key, nchunks=KC, scale=1.0 / 1024, eps=1e-6, ones=None):
        ps, pk = psum()
        for c in range(nchunks):
            sq, sk = sqb[c % 2], "sqb%d" % (c % 2)
            src = src_fn(c)
            P.op("act", lambda e, sq=sq, src=src: e.activation(out=sq[:, 0:n], in_=src, func=AF.Square),
                 reads=src_keys, writes=[sk])
            P.op("pe", lambda e, sq=sq, c=c: e.matmul(ps[:, 0:n], lhsT=(ones or C["ones_b"])[:], rhs=sq[:, 0:n],
                                                       start=(c == 0), stop=(c == nchunks - 1)),
                 reads=[sk, "k_ones_b"], writes=[pk], pe_acc=(c > 0))
        P.op("act", lambda e: e.activation(out=out_rstd[:, 0:n], in_=ps[:, 0:n], func=AF.Ln, scale=scale, bias=epsc(eps)),
             reads=[pk, "epsc"], writes=[out_key])
        P.op("act", lambda e: e.activation(out=out_rstd[:, 0:n], in_=out_rstd[:, 0:n], func=AF.Exp, scale=-0.5),
             reads=[out_key], writes=[out_key])

    eps_tiles = {}

    def epsc(v):
        return eps_tiles[v][:, 0:1]

    for v in (1e-6, 1e-12, 64e-5):
        t = sb("eps%d" % len(eps_tiles), [128, 1])
        eps_tiles[v] = t
        P.op("pool", lambda e, t=t, v=v: e.memset(t[:], v), writes=["epsc"])

    def barrier():
        P.barrier()

    def dbg_out(nm, src_ap, keys):
        if nm in dbg:
            out_toks.append(P.dma("pool", dbg[nm][:], src_ap, reads=keys, writes=["dbg_" + nm]))

    pconv = sb("pconv", [128, 6, 3]); pw0 = sb("pw0", [128, 2, 2]); pa0 = sb("pa0", [128, 2, 2])
    pvec = sb("pvec", [128, 3, 2]); pln = sb("pln", [64, 2, 4])
    w2t = sb("w2t", [128, 256], BF16); a2t = sb("a2t", [128, 256], BF16); g2t = sb("g2t", [128, 256], BF16)
    omka = sb("omka", [128, 2])
    plam = sb("plam", [128, 4, 64]); psub = sb("psub", [128, 1]); lamv = sb("lamv", [128, 4]); lamt = sb("lamt", [128, 64])
    hconv = sb("hconv", [128, 6, 4]); hw1 = sb("hw1", [33, 64]); hvec = sb("hvec", [64, 3]); hw2 = sb("hw2", [64, 64])
    hsc = sb("hsc", [64, 6]); fss = sb("fss", [128, 3]); hdec = sb("hdec", [128, 2]); hbias = sb("hbias", [128, 2, 2]); fb = sb("fb", [64, 2])
    def mkap(base, offset_elems, dims):
        return AP(base.tensor, base.offset + offset_elems, [list(base.ap[0])] + [list(d) for d in dims])

    def rwkv_layer(l, rkv, lor):
        P.dma("sp", h_spill[:], hT, reads=["hT"], writes=["h_spill"])
        barrier()
        carver.lim = MB0
        for (t, src) in ((pconv, rw_conv), (pw0, rw_w0), (pa0, rw_a0), (pvec, rw_vec), (pln, rw_ln)):
            P.dma("sp", t[:], src[l], writes=["rwp"])
        for (t, src) in ((w2t, rw_w2), (a2t, rw_a2), (g2t, rw_g2)):
            P.dma("pool", t[:], src[l], writes=["rwp"])
        P.op("dve", lambda e: e.tensor_scalar(out=omka[:], in0=pvec[:, 1, :], scalar1=-1.0, scalar2=1.0, op0=ALU.mult, op1=ALU.add),
             reads=["rwp"], writes=["omka"])
        tmpc = carver.get([128, LS])
        LH = LS // 2
        NPL = NPS * LP
        WS = carver.get([128, 2, 2, LH]); NBS = carver.get([128, 2, 2, LH], BF16); KDS = carver.get([128, 2, 2, LH], BF16)
        WP = carver.get([128, 2, 2, NPL]); NBP = carver.get([128, 2, 2, NPL], BF16); KDP = carver.get([128, 2, 2, NPL], BF16)
        snames = ["s"] + ["p%d" % i for i in range(NPS)]
        Ast = {sn: [carver.get([128, 4, 64]) for _ in range(2)] for sn in snames}
        Bst = {sn: carver.get([128, 4, 64]) for sn in snames}
        G1t = {sn: carver.get([128, 4, 64], BF16) for sn in snames}
        DVt = {sn: carver.get([128, 4, 64], BF16) for sn in snames}
        Abf = {sn: [carver.get([128, 4, 64], BF16) for _ in range(2)] for sn in snames}
        KVt = {sn: [carver.get([128, 4, 64]) for _ in range(2)] for sn in snames}
        gT = carver.get([128, 2, 512])
        bon = mixT[:, 0:2, :]
        Y = mixT[0:64, 2:6, :]
        kkn = mixT[:, 6:8, :]
        for c in range(6):
            for (s0, L, cnd) in cfg.seqs:
                u = rkv[:, c, s0:s0 + L]
                P.op("act", lambda e: e.activation(out=tmpc[:, 0:L], in_=u, func=AF.Identity, scale=pconv[:, c, 1:2]),
                     reads=["rkv", "rwp"], writes=["tmpc"])
                P.op("dve", lambda e: e.scalar_tensor_tensor(out=tmpc[:, 1:L], in0=rkv[:, c, s0:s0 + L - 1], scalar=pconv[:, c, 0:1],
                                                             in1=tmpc[:, 1:L], op0=ALU.mult, op1=ALU.add),
                     reads=["rkv", "rwp", "tmpc"], writes=["tmpc"])
                P.op("dve", lambda e: e.scalar_tensor_tensor(out=tmpc[:, 0:L - 1], in0=rkv[:, c, s0 + 1:s0 + L], scalar=pconv[:, c, 2:3],
                                                             in1=tmpc[:, 0:L - 1], op0=ALU.mult, op1=ALU.add),
                     reads=["rkv", "rwp", "tmpc"], writes=["tmpc"])
                P.op("act", lambda e: e.copy(out=rkv[:, c, s0:s0 + L], in_=tmpc[:, 0:L]), reads=["tmpc"], writes=["rkv"])

        def gen_range(t0, n, dsts, first):
            for p in range(2):
                if first:
                    kk_t, kk_k = tmp()
                    P.op("act", lambda e: e.activation(out=kk_t[:, 0:n], in_=rkv[:, 2 + p, t0:t0 + n], func=AF.Identity, scale=pvec[:, 0, p:p + 1]),
                         reads=["rkv", "rwp"], writes=[kk_k])
                    P.op("act", lambda e: e.activation(out=sqb[0][:, 0:n], in_=kk_t[:, 0:n], func=AF.Square), reads=[kk_k], writes=["sqb0"])
                    ps, pk = psum()
                    P.op("pe", lambda e: e.matmul(ps[:, 0:n], lhsT=C["onesblk_b"][:], rhs=sqb[0][:, 0:n], start=True, stop=True),
                         reads=["sqb0", "k_onesblk_b"], writes=[pk])
                    P.op("act", lambda e: e.activation(out=rstd[:, 0:n], in_=ps[:, 0:n], func=AF.Ln, bias=epsc(1e-12)), reads=[pk, "epsc"], writes=["rstd"])
                    P.op("act", lambda e: e.activation(out=rstd[:, 0:n], in_=rstd[:, 0:n], func=AF.Exp, scale=-0.5), reads=["rstd"], writes=["rstd"])
                    P.op("dve", lambda e: e.tensor_tensor(out=kkn[:, p, t0:t0 + n], in0=kk_t[:, 0:n], in1=rstd[:, 0:n], op=ALU.mult),
                         reads=[kk_k, "rstd"], writes=["kkn"])
                    psb, psbk = psfix(3)
                for d in range(2):
                    if dsts[d] is None and not first:
                        continue
                    if dsts[d] is not None:
                        ps, pk = psum()
                        P.op("pe", lambda e: e.matmul(ps[:, 0:n], lhsT=w2t[d * 64:(d + 1) * 64, p * 128:(p + 1) * 128],
                                                      rhs=lor[d * 64:(d + 1) * 64, 0, t0:t0 + n], start=True, stop=True),
                             reads=["rwp", "lor"], writes=[pk])
                        sg, sgk = tmp()
                        P.op("act", lambda e: e.activation(out=sg[:, 0:n], in_=ps[:, 0:n], func=AF.Sigmoid, bias=pw0[:, d, p:p + 1]),
                             reads=[pk, "rwp"], writes=[sgk])
                        P.op("act", lambda e: e.activation(out=dsts[d][0](p), in_=sg[:, 0:n], func=AF.Exp, scale=-math.exp(-0.5)),
                             reads=[sgk], writes=[dsts[d][3]])
                    ps, pk = psum()
                    P.op("pe", lambda e: e.matmul(ps[:, 0:n], lhsT=a2t[d * 64:(d + 1) * 64, p * 128:(p + 1) * 128],
                                                  rhs=lor[d * 64:(d + 1) * 64, 1, t0:t0 + n], start=True, stop=True),
                         reads=["rwp", "lor"], writes=[pk])
                    av, avk = tmp()
                    P.op("act", lambda e: e.activation(out=av[:, 0:n], in_=ps[:, 0:n], func=AF.Sigmoid, bias=pa0[:, d, p:p + 1]),
                         reads=[pk, "rwp"], writes=[avk])
                    if dsts[d] is not None:
                        P.op("dve", lambda e: e.scalar_tensor_tensor(out=dsts[d][1](p), in0=av[:, 0:n], scalar=-1.0, in1=kkn[:, p, t0:t0 + n],
                                                                     op0=ALU.mult, op1=ALU.mult), reads=[avk, "kkn"], writes=[dsts[d][3]])
                    P.op("dve", lambda e: e.tensor_scalar(out=av[:, 0:n], in0=av[:, 0:n], scalar1=pvec[:, 1, p:p + 1], scalar2=omka[:, p:p + 1],
                                                          op0=ALU.mult, op1=ALU.add), reads=[avk, "rwp", "omka"], writes=[avk])
                    P.op("dve", lambda e: e.tensor_tensor(out=av[:, 0:n], in0=av[:, 0:n], in1=rkv[:, 2 + p, t0:t0 + n], op=ALU.mult),
                         reads=[avk, "rkv"], writes=[avk])
                    if dsts[d] is not None:
                        P.op("act", lambda e: e.copy(out=dsts[d][2](p), in_=av[:, 0:n]), reads=[avk], writes=[dsts[d][3]])
                    if first:
                        P.op("dve", lambda e: e.scalar_tensor_tensor(out=sqb[1][:, 0:n], in0=av[:, 0:n], scalar=pvec[:, 2, p:p + 1],
                                                                     in1=rkv[:, p, t0:t0 + n], op0=ALU.mult, op1=ALU.mult),
                             reads=[avk, "rwp", "rkv"], writes=["sqb1"])
                        P.op("pe", lambda e: e.matmul(psb[:, 0:n], lhsT=C["onesblk_b"][:], rhs=sqb[1][:, 0:n], start=(d == 0), stop=(d == 1)),
                             reads=["sqb1", "k_onesblk_b"], writes=[psbk], pe_acc=(d == 1))
                if first:
                    P.op("dve", lambda e: e.tensor_tensor(out=bon[:, p, t0:t0 + n], in0=psb[:, 0:n], in1=rkv[:, 4 + p, t0:t0 + n], op=ALU.mult),
                         reads=[psbk, "rkv"], writes=["bon"])

        PIECE = min(512, LH)

        def s_dsts(t0, n, phase):
            half = 0 if t0 < LH else 1
            d = half if phase == 0 else 1 - half
            col = t0 - half * LH
            out = [None, None]
            out[d] = (lambda p: WS[:, p, d, col:col + n], lambda p: NBS[:, p, d, col:col + n], lambda p: KDS[:, p, d, col:col + n], "opS")
            return out
        for t0 in range(0, LS, PIECE):
            gen_range(t0, PIECE, s_dsts(t0, PIECE, 0), True)
        for q0 in range(0, NPL, PIECE if NPL >= PIECE else NPL):
            n = min(PIECE, NPL - q0)
            both = [(lambda p, d=d: WP[:, p, d, q0:q0 + n], lambda p, d=d: NBP[:, p, d, q0:q0 + n], lambda p, d=d: KDP[:, p, d, q0:q0 + n], "opP") for d in range(2)]
            gen_range(LS + q0, n, both, True)
        barrier()
        streams = [("s", 0, LS, True, 0)] + [("p%d" % i, LS + i * LP, LP, False, i * LP) for i in range(NPS)]
        ypsums = {}
        for (sn, s0, L, is_s, q0) in streams:
            if is_s:
                P.dma("sp", Ast[sn][0], st_in[l], writes=["A_" + sn])
            else:
                P.op("pool", lambda e: e.memset(Ast[sn][0], 0.0), writes=["A_" + sn])
            ypsums[sn] = psfix(len(ypsums))
        pend = {}

        def emit_y(sn, s0, L, i, nxt):
            ak = "A_" + sn
            slot = i % 64
            yp, ypk = ypsums[sn]
            for g in range(4):
                p, d = g // 2, g % 2
                rcol = slot if d == 0 else 63 - slot
                for j in range(2):
                    col = ((slot * 2 + d) * 2 + p) * 2 + j
                    tcol = s0 + (i if d == 0 else L - 1 - i)
                    P.op("pe", lambda e: e.matmul(yp[0:64, col:col + 1], lhsT=nxt[j * 64:(j + 1) * 64, g, :], rhs=rkv[j * 64:(j + 1) * 64, p, tcol:tcol + 1],
                                                  start=True, stop=True), reads=["Abf%s%d" % (sn, (i + 1) % 2), "rkv"], writes=[ypk], pe_acc=True)
            if slot == 63 or i == L - 1:
                ns = slot + 1
                i0 = i - slot
                first = i0 < L // 2
                ypv = yp[0:64, 0:ns * 8].rearrange("v (s d h) -> v s d h", d=2, h=4)
                for d in range(2):
                    if d == 0:
                        dst = mkap(Y[:, 0, 0:1], s0 + i0, [[1, ns], [Y.ap[1][0], 4]])
                    else:
                        dst = mkap(Y[:, 0, 0:1], s0 + L - 1 - i0, [[-1, ns], [Y.ap[1][0], 4]])
                    if first:
                        P.op("act", lambda e: e.copy(out=dst, in_=ypv[:, :, d, :]), reads=[ypk], writes=["Y"])
                    else:
                        P.op("dve", lambda e: e.tensor_tensor(out=dst, in0=ypv[:, :, d, :], in1=dst, op=ALU.add), reads=[ypk, "Y"], writes=["Y"])
        maxL = max(L for (_, _, L, _, _) in streams)
        def step_ctx(st_, i):
            (sn, s0, L, is_s, q0) = st_
            dstr = L - 1 - 2 * i
            if is_s:
                il = i % LH
                Wb, NBb, KDb, opk, Lb, off = WS, NBS, KDS, "opS", LH, il
                dl = LH - 1 - 2 * il
            else:
                Wb, NBb, KDb, opk, Lb, off = WP, NBP, KDP, "opP", NPL, q0 + i
                dl = L - 1 - 2 * i

            def gop(base3, pstride, dextra):
                return mkap(base3, s0 + i, [[pstride, 2], [dextra + dstr, 2], [0, 64]])

            def lop(buf):
                return mkap(buf[:, 0, 0, 0:1], off, [[2 * Lb, 2], [Lb + dl, 2], [0, 64]])
            return dict(sn=sn, s0=s0, L=L, i=i, cur=Ast[sn][i % 2], nxt=Ast[sn][(i + 1) % 2], ak="A_" + sn,
                        kk_op=gop(kkn[:, 0, 0:1], kkn.ap[1][0], 0), v_op=gop(rkv[:, 4, 0:1], NT, 0),
                        w_op=lop(Wb), kd_op=lop(KDb), NBb=NBb, off=off, dl=dl, opk=opk)

        def prep(st_, i):
            c = step_ctx(st_, i)
            sn = c["sn"]
            kvb, kvk = KVt[sn][i % 2], "KV%s%d" % (sn, i % 2)
            P.op("pool", lambda e: e.tensor_tensor(out=DVt[sn], in0=mkap(C["i2_f"][:, 0:1], 0, [[0, 4], [1, 64]]), in1=c["v_op"], op=ALU.mult),
                 reads=["rkv", "k_i2_f"], writes=["DV" + sn])
            ps2, ps2k = psum()
            P.op("pe", lambda e: e.matmul(ps2[:, 0:256], lhsT=C["onesblk_b"][:], rhs=DVt[sn].rearrange("p g v -> p (g v)"), start=True, stop=True),
                 reads=["DV" + sn, "k_onesblk_b"], writes=[ps2k])
            P.op("dve", lambda e: e.tensor_tensor(out=kvb, in0=ps2[:, 0:256].rearrange("p (g v) -> p g v", g=4), in1=c["kd_op"], op=ALU.mult),
                 reads=[ps2k, c["opk"]], writes=[kvk])
        prepped = {}
        for i in range(maxL):
            if i == LH:
                for t0 in range(0, LS, PIECE):
                    gen_range(t0, PIECE, s_dsts(t0, PIECE, 1), False)
            act_streams = [st_ for st_ in streams if i < st_[2]]
            p1 = {}
            for st_ in act_streams:
                (sn, s0, L, is_s, q0) = st_
                c = step_ctx(st_, i)
                cur, ak = c["cur"], c["ak"]
                if prepped.get(sn) != i:
                    prep(st_, i)
                    prepped[sn] = i
                slot = i % 64
                if slot == 0:
                    if pend.get(sn) is not None:
                        emit_y(*pend[sn])
                        pend[sn] = None
                P.op("dve", lambda e: e.tensor_tensor(out=G1t[sn], in0=cur, in1=c["kk_op"], op=ALU.mult), reads=[ak, "kkn"], writes=["G1" + sn])
                ps1, ps1k = psum()
                P.op("pe", lambda e: e.matmul(ps1[:, 0:256], lhsT=C["onesblk_b"][:], rhs=G1t[sn].rearrange("p g v -> p (g v)"), start=True, stop=True),
                     reads=["G1" + sn, "k_onesblk_b"], writes=[ps1k])
                P.op("dve", lambda e: e.tensor_tensor(out=Bst[sn], in0=cur, in1=c["w_op"], op=ALU.mult), reads=[ak, c["opk"]], writes=["B" + sn])
                P.op("dve", lambda e: e.tensor_tensor(out=Bst[sn], in0=Bst[sn], in1=KVt[sn][i % 2], op=ALU.add), reads=["B" + sn, "KV%s%d" % (sn, i % 2)], writes=["B" + sn])
                p1[sn] = (c, ps1, ps1k)
            for st_ in act_streams:
                (sn, s0, L, is_s, q0) = st_
                c, ps1, ps1k = p1[sn]
                cur, nxt, ak, opk, off, dl, NBb = c["cur"], c["nxt"], c["ak"], c["opk"], c["off"], c["dl"], c["NBb"]
                if pend.get(sn) is not None:
                    emit_y(*pend[sn])
                    pend[sn] = None
                for g in range(4):
                    p, d = g // 2, g % 2
                    lcol = off if d == 0 else off + dl
                    P.op("dve", lambda e: e.scalar_tensor_tensor(out=nxt[:, g, :], in0=ps1[:, g * 64:(g + 1) * 64], scalar=NBb[:, p, d, lcol:lcol + 1],
                                                                 in1=Bst[sn][:, g, :], op0=ALU.mult, op1=ALU.add),
                         reads=[ps1k, opk, "B" + sn], writes=[ak])
                abf = Abf[sn][(i + 1) % 2]
                P.op("act", lambda e: e.copy(out=abf, in_=nxt), reads=[ak], writes=["Abf%s%d" % (sn, (i + 1) % 2)])
                pend[sn] = (sn, s0, L, i, abf)
                if i + 1 < L and not (is_s and (i + 1) == LH):
                    prep(st_, i + 1)
                    prepped[sn] = i + 1
        for sn in list(pend.keys()):
            if pend[sn] is not None:
                emit_y(*pend[sn])
                pend[sn] = None
        for si, (sn, s0, L, is_s, q0) in enumerate(streams):
            if is_s:
                continue
            fin = Ast[sn][L % 2]
            for g in range(4):
                p, d = g // 2, g % 2
                ps, pk = psum()
                P.op("pe", lambda e: e.transpose(ps[0:64, 0:128], fin[:, g, :], C["ident_f"][:]), reads=["A_" + sn, "k_ident_f"], writes=[pk])
                tt, tk = tmp()
                P.op("act", lambda e: e.copy(out=tt[0:64, 0:128], in_=ps[0:64, 0:128]), reads=[pk], writes=[tk])
                for j in range(2):
                    out_toks.append(P.dma("sp", st_out[l, si - 1, d, 2 * p + j], tt[0:64, j * 64:(j + 1) * 64], reads=[tk], writes=["st_out"]))
        for (t0, n, cnd) in cfg.tiles:
            for p in range(2):
                ps, pk = psum()
                P.op("pe", lambda e: e.matmul(ps[:, 0:n], lhsT=g2t[:, p * 128:(p + 1) * 128], rhs=lor[:, 2, t0:t0 + n], start=True, stop=True),
                     reads=["rwp", "lor"], writes=[pk])
                P.op("act", lambda e: e.copy(out=gT[:, p, 0:n], in_=ps[:, 0:n]), reads=[pk], writes=["gT"])
            for p in range(2):
                pso, psok = psfix(3)
                for j in range(2):
                    h = 2 * p + j
                    yv = Y[:, h, t0:t0 + n]
                    psm, psmk = psum()
                    yc, yck = tmp()
                    P.op("act", lambda e: e.copy(out=yc[0:64, 0:n], in_=yv), reads=["Y"], writes=[yck])
                    P.op("pe", lambda e: e.matmul(psm[0:64, 0:n], lhsT=C["ones_f"][0:64, 0:64], rhs=yc[0:64, 0:n], start=True, stop=True),
                         reads=[yck, "k_ones_f"], writes=[psmk])
                    P.op("dve", lambda e: e.scalar_tensor_tensor(out=yc[0:64, 0:n], in0=psm[0:64, 0:n], scalar=-1.0 / 64, in1=yc[0:64, 0:n],
                                                                 op0=ALU.mult, op1=ALU.add), reads=[psmk, yck], writes=[yck])
                    sq, sqk = tmp()
                    P.op("act", lambda e: e.activation(out=sq[0:64, 0:n], in_=yc[0:64, 0:n], func=AF.Square), reads=[yck], writes=[sqk])
                    psv, psvk = psum()
                    P.op("pe", lambda e: e.matmul(psv[0:64, 0:n], lhsT=C["ones_f"][0:64, 0:64], rhs=sq[0:64, 0:n], start=True, stop=True),
                         reads=[sqk, "k_ones_f"], writes=[psvk])
                    P.op("act", lambda e: e.activation(out=sq[0:64, 0:n], in_=psv[0:64, 0:n], func=AF.Ln, scale=1.0 / 64, bias=eps_tiles[64e-5][0:64, 0:1]),
                         reads=[psvk, "epsc"], writes=[sqk])
                    P.op("act", lambda e: e.activation(out=sq[0:64, 0:n], in_=sq[0:64, 0:n], func=AF.Exp, scale=-0.5), reads=[sqk], writes=[sqk])
                    P.op("dve", lambda e: e.tensor_tensor(out=yc[0:64, 0:n], in0=yc[0:64, 0:n], in1=sq[0:64, 0:n], op=ALU.mult), reads=[yck, sqk], writes=[yck])
                    P.op("act", lambda e: e.activation(out=yc[0:64, 0:n], in_=yc[0:64, 0:n], func=AF.Identity, scale=pln[:, 0, h:h + 1], bias=pln[:, 1, h:h + 1]),
                         reads=[yck, "rwp"], writes=[yck])
                    P.op("pe", lambda e: e.matmul(pso[:, 0:n], lhsT=C["sel_f"][:, j, :], rhs=yc[0:64, 0:n], start=(j == 0), stop=(j == 1)),
                         reads=[yck, "k_sel_f"], writes=[psok], pe_acc=(j == 1))
                tt, tk = tmp()
                P.op("dve", lambda e: e.tensor_tensor(out=tt[:, 0:n], in0=pso[:, 0:n], in1=bon[:, p, t0:t0 + n], op=ALU.add), reads=[psok, "bon"], writes=[tk])
                P.op("dve", lambda e: e.tensor_tensor(out=mixT[:, p, t0:t0 + n], in0=tt[:, 0:n], in1=gT[:, p, 0:n], op=ALU.mult), reads=[tk, "gT"], writes=["mixT"])
        barrier()
        P.dma("sp", hT, h_spill[:], reads=["h_spill"], writes=["hT"])
        dbg_out("ya%d" % l, mixT[:, 0:2, :], ["mixT"])


    def attn_layer(l, lam0):
        carver.reset(0, HB0)
        NKC = PAST // 128
        qT = carver.get([128, 4, NT], BF16)
        kT = carver.get([128, 4, NT], BF16)
        kcT = carver.get([128, 4, PAST], BF16)
        vS = carver.get([128, NKC + LS // 128, 4, 128], BF16)
        vP = carver.get([128, NPS * LP // 128, 4, 128], BF16)
        stg = [carver.get([128, 512]) for _ in range(2)]
        wv = carver.get([128, KC, 512], BF16)
        pTs = [carver.get([128, 512], BF16) for _ in range(3)]
        xb = [carver.get([128, 512], BF16) for _ in range(2)]
        of = carver.get([128, 512])
        P.dma("sp", plam[:], dlam[l], writes=["plam"])
        P.dma("sp", psub[:], dsub[l], writes=["psub"])
        P.dma("pool", kcT, ckT[l].rearrange("h p t -> p h t"), writes=["kcT"])
        P.dma("pool", vS[:, 0:NKC], cv[l], writes=["vS"])
        for i in range(2):
            P.op("dve", lambda e: e.tensor_tensor(out=lamt[:], in0=plam[:, 2 * i, :], in1=plam[:, 2 * i + 1, :], op=ALU.mult), reads=["plam"], writes=["lamt"])
            P.op("dve", lambda e: e.reduce_sum(out=lamv[:, i:i + 1], in_=lamt[:], axis=AX.X), reads=["lamt"], writes=["lamv"])
        P.op("act", lambda e: e.activation(out=lamv[:, 0:2], in_=lamv[:, 0:2], func=AF.Exp), reads=["lamv"], writes=["lamv"])
        P.op("dve", lambda e: e.tensor_tensor(out=lamv[:, 2:3], in0=lamv[:, 0:1], in1=lamv[:, 1:2], op=ALU.subtract), reads=["lamv"], writes=["lamv"])
        P.op("dve", lambda e: e.tensor_scalar(out=lamv[:, 3:4], in0=lamv[:, 2:3], scalar1=-1.0, scalar2=-lam0, op0=ALU.mult, op1=ALU.add), reads=["lamv"], writes=["lamv"])
        P.op("dve", lambda e: e.tensor_scalar(out=psub[:], in0=psub[:], scalar1=(1.0 - lam0), scalar2=None, op0=ALU.mult), reads=["psub"], writes=["psub"])

        def cbqk(ps, pk, ci, t0, n, cnd):
            dst = (qT if ci < 4 else kT)[:, ci % 4, t0:t0 + n]
            dk = "qT" if ci < 4 else "kT"
            if cnd == 1 or os.environ.get("NOROPE"):
                P.op("act", lambda e: e.copy(out=dst, in_=ps[:, 0:n]), reads=[pk], writes=[dk])
                return
            xbt, xbk = xb[ci % 2], "xb%d" % (ci % 2)
            P.op("act", lambda e: e.copy(out=xbt[:, 0:n], in_=ps[:, 0:n]), reads=[pk], writes=[xbk])
            pr, prk = psum()
            P.op("pe", lambda e: e.matmul(pr[:, 0:n], lhsT=C["ropeR"][:], rhs=xbt[:, 0:n], start=True, stop=True), reads=[xbk, "k_ropeR"], writes=[prk])
            t1, t1k = tmp()
            P.op("dve", lambda e: e.tensor_tensor(out=t1[:, 0:n], in0=xbt[:, 0:n], in1=C["cos"][:, t0:t0 + n], op=ALU.mult), reads=[xbk, "k_cos"], writes=[t1k])
            t2, t2k = tmp()
            P.op("dve", lambda e: e.tensor_tensor(out=t2[:, 0:n], in0=pr[:, 0:n], in1=C["sin"][:, t0:t0 + n], op=ALU.mult), reads=[prk, "k_sin"], writes=[t2k])
            P.op("dve", lambda e: e.tensor_tensor(out=dst, in0=t1[:, 0:n], in1=t2[:, 0:n], op=ALU.add), reads=[t1k, t2k], writes=[dk])
        STG = int(os.environ.get("ATT_STAGE", "9"))
        if STG < 1:
            return
        proj_fm(w_in[l], 1152, 8, hT, "hT", cbqk)
        if STG < 2:
            return

        def proj_tm(col0, tok_ranges, cb):
            P.dma("pool", wv, w_in[l][:, col0:col0 + 512].rearrange("(kc p) c -> p kc c", p=128), writes=["wv"])
            for (t0, info) in tok_ranges:
                ps, pk = psum()
                for c in range(KC):
                    P.op("pe", lambda e: e.matmul(ps[:, 0:512], lhsT=hT[:, c, t0:t0 + 128], rhs=wv[:, c, :], start=(c == 0), stop=(c == KC - 1)),
                         reads=["hT", "wv"], writes=[pk], pe_acc=(c > 0))
                cb(ps, pk, t0, info)
        prm_chunks = [(LS + i * 128, i) for i in range(NPS * LP // 128)]
        sam_chunks = [(i * 128, i) for i in range(LS // 128)]
        scnt = [0]

        def cbk(ps, pk, t0, i):
            sg, sgk = stg[scnt[0] % 2], "stg%d" % (scnt[0] % 2)
            scnt[0] += 1
            P.op("act", lambda e: e.copy(out=sg[:], in_=ps[:, 0:512]), reads=[pk], writes=[sgk])
            out_toks.append(P.dma("sp", kc_out[l, i * 128:(i + 1) * 128, :], sg[:], reads=[sgk], writes=["kc_out"]))
        proj_tm(1664, prm_chunks, cbk)

        def cbv_p(ps, pk, t0, i):
            sg, sgk = stg[scnt[0] % 2], "stg%d" % (scnt[0] % 2)
            scnt[0] += 1
            P.op("act", lambda e: e.copy(out=sg[:], in_=ps[:, 0:512]), reads=[pk], writes=[sgk])
            out_toks.append(P.dma("sp", vc_out[l, i * 128:(i + 1) * 128, :], sg[:], reads=[sgk], writes=["vc_out"]))
            P.op("dve", lambda e: e.tensor_copy(out=vP[:, i].rearrange("p h e -> p (h e)"), in_=sg[:]), reads=[sgk], writes=["vP"])

        def cbv_s(ps, pk, t0, i):
            P.op("act", lambda e: e.copy(out=vS[:, NKC + i].rearrange("p h e -> p (h e)"), in_=ps[:, 0:512]), reads=[pk], writes=["vS"])
        proj_tm(2176, prm_chunks, cbv_p)
        proj_tm_v = None
        for (t0, i) in sam_chunks:
            ps, pk = psum()
            for c in range(KC):
                P.op("pe", lambda e: e.matmul(ps[:, 0:512], lhsT=hT[:, c, t0:t0 + 128], rhs=wv[:, c, :], start=(c == 0), stop=(c == KC - 1)),
                     reads=["hT", "wv"], writes=[pk], pe_acc=(c > 0))
            cbv_s(ps, pk, t0, i)

        if STG < 3:
            return
        jobs = []
        for q0 in range(0, LS, 512):
            nq = min(512, LS - q0)
            ks = [("c", c) for c in range(NKC)] + [("s", c) for c in range(LS // 128)]
            jobs.append((q0, nq, ks))
        for i in range(NPS):
            s0 = LS + i * LP
            jobs.append((s0, LP, [("p", i * (LP // 128) + c) for c in range(LP // 128)]))
        pcnt = [0]
        for (q0, nq, ks) in jobs:
            for h in range(4):
                O = [psfix(0), psfix(1)]
                Z = [psfix(2), psfix(3)]
                for ki, (kind, c) in enumerate(ks):
                    if kind == "c":
                        kap = lambda m: kcT[m * 64:(m + 1) * 64, h, c * 128:(c + 1) * 128]
                        vap = vS[:, c, h, :]
                        kkey, vkey = "kcT", "vS"
                    elif kind == "s":
                        kap = lambda m: kT[m * 64:(m + 1) * 64, h, c * 128:(c + 1) * 128]
                        vap = vS[:, NKC + c, h, :]
                        kkey, vkey = "kT", "vS"
                    else:
                        kap = lambda m: kT[m * 64:(m + 1) * 64, h, LS + c * 128:LS + (c + 1) * 128]
                        vap = vP[:, c, h, :]
                        kkey, vkey = "kT", "vP"
                    for m in range(2):
                        ps, pk = psum()
                        P.op("pe", lambda e: e.matmul(ps[:, 0:nq], lhsT=kap(m), rhs=qT[m * 64:(m + 1) * 64, h, q0:q0 + nq], start=True, stop=True),
                             reads=[kkey, "qT"], writes=[pk])
                        pT, pTk = pTs[pcnt[0] % 3], "pT%d" % (pcnt[0] % 3)
                        pcnt[0] += 1
                        P.op("act", lambda e: e.activation(out=pT[:, 0:nq], in_=ps[:, 0:nq], func=AF.Exp, scale=0.125), reads=[pk], writes=[pTk])
                        P.op("pe", lambda e: e.matmul(O[m][0][:, 0:nq], lhsT=vap, rhs=pT[:, 0:nq], start=(ki == 0), stop=(ki == len(ks) - 1)),
                             reads=[vkey, pTk], writes=[O[m][1]], pe_acc=(ki > 0))
                        P.op("pe", lambda e: e.matmul(Z[m][0][:, 0:nq], lhsT=C["ones_b"][:], rhs=pT[:, 0:nq], start=(ki == 0), stop=(ki == len(ks) - 1)),
                             reads=["k_ones_b", pTk], writes=[Z[m][1]], pe_acc=(ki > 0))
                o_m = []
                for m in range(2):
                    rz, rzk = tmp()
                    P.op("dve", lambda e: e.reciprocal(out=rz[:, 0:nq], in_=Z[m][0][:, 0:nq]), reads=[Z[m][1]], writes=[rzk])
                    P.op("dve", lambda e: e.tensor_tensor(out=rz[:, 0:nq], in0=O[m][0][:, 0:nq], in1=rz[:, 0:nq], op=ALU.mult), reads=[O[m][1], rzk], writes=[rzk])
                    o_m.append((rz, rzk))
                P.op("dve", lambda e: e.scalar_tensor_tensor(out=of[:, 0:nq], in0=o_m[1][0][:, 0:nq], scalar=lamv[:, 3:4], in1=o_m[0][0][:, 0:nq],
                                                             op0=ALU.mult, op1=ALU.add), reads=[o_m[0][1], o_m[1][1], "lamv"], writes=["of"])
                rms_stats(lambda c: of[:, 0:nq], ["of"], q0, nq, rstd, "rstd", nchunks=1, scale=1.0 / 128)
                tt, tk = tmp()
                P.op("dve", lambda e: e.tensor_tensor(out=tt[:, 0:nq], in0=of[:, 0:nq], in1=rstd[:, 0:nq], op=ALU.mult), reads=["of", "rstd"], writes=[tk])
                P.op("act", lambda e: e.activation(out=mixT[:, 2 + h, q0:q0 + nq], in_=tt[:, 0:nq], func=AF.Identity, scale=psub[:, 0:1]),
                     reads=[tk, "psub"], writes=["mixT"])
        dbg_out("yb%d" % l, mixT[:, 2:6, :], ["mixT"])

    def hyena_layer(l):
        carver.reset(0, HB0)
        uh = carver.get([128, 6, NT], BF16)

        def cbC(ps, pk, ci, t0, n, cnd):
            P.op("act", lambda e: e.copy(out=uh[:, ci, t0:t0 + n], in_=ps[:, 0:n]), reads=[pk], writes=["uh"])
        proj_fm(w_in[l], 2688, 6, hT, "hT", cbC)
        barrier()
        carver.lim = MB0
        hw3b = carver.get([64, 1024], BF16)
        ndecb = carver.get([128, 256])
        for (t, src) in ((hconv, hy_conv), (hw1, hy_w1), (hvec, hy_vec), (hw2, hy_w2), (hbias, hy_bias)):
            P.dma("sp", t[:], src[l], writes=["hyp"])
        P.dma("sp", ndecb, hy_decb[l], writes=["ndecb"])
        P.dma("pool", hw3b, hy_w3[l], writes=["hyp"])
        EV = carver.get([128, 2, 1024])
        P.op("dve", lambda e: e.tensor_scalar(out=EV[:, 0, 0:256], in0=ndecb, scalar1=-1.0, scalar2=None, op0=ALU.mult), reads=["ndecb"], writes=["EV"])
        P.op("dve", lambda e: e.tensor_tensor(out=ndecb, in0=ndecb, in1=EV[:, 0, 0:256], op=ALU.min), reads=["ndecb", "EV"], writes=["ndecb"])
        for (col, bi, fac) in ((0, None, 0.5), (1, None, 0.25), (2, 0, 0.5), (3, 0, 0.25), (4, 2, 0.5), (5, 2, 0.25)):
            if bi is None:
                P.op("dve", lambda e: e.tensor_scalar(out=hsc[:, col:col + 1], in0=hvec[:, 1:2], scalar1=fac, scalar2=None, op0=ALU.mult), reads=["hyp"], writes=["hsc"])
            else:
                P.op("dve", lambda e: e.scalar_tensor_tensor(out=hsc[:, col:col + 1], in0=hvec[:, 1:2], scalar=fac, in1=hvec[:, bi:bi + 1], op0=ALU.mult, op1=ALU.mult),
                     reads=["hyp"], writes=["hsc"])
        XS = carver.get([128, 3 * LS])
        tmpc = XS[:, 0:LS]
        embT = XS[0:33, LS:2 * LS]
        h1 = XS[0:64, 2 * LS:3 * LS]
        for c in range(6):
            for (s0, L, cnd) in cfg.seqs:
                P.op("act", lambda e: e.activation(out=tmpc[:, 0:L], in_=uh[:, c, s0:s0 + L], func=AF.Identity, scale=hconv[:, c, 1:2], bias=hconv[:, c, 3:4]),
                     reads=["uh", "hyp"], writes=["XS"])
                P.op("dve", lambda e: e.scalar_tensor_tensor(out=tmpc[:, 1:L], in0=uh[:, c, s0:s0 + L - 1], scalar=hconv[:, c, 0:1], in1=tmpc[:, 1:L], op0=ALU.mult, op1=ALU.add),
                     reads=["uh", "hyp", "XS"], writes=["XS"])
                P.op("dve", lambda e: e.scalar_tensor_tensor(out=tmpc[:, 0:L - 1], in0=uh[:, c, s0 + 1:s0 + L], scalar=hconv[:, c, 2:3], in1=tmpc[:, 0:L - 1], op0=ALU.mult, op1=ALU.add),
                     reads=["uh", "hyp", "XS"], writes=["XS"])
                P.op("act", lambda e: e.copy(out=uh[:, c, s0:s0 + L], in_=tmpc[:, 0:L]), reads=["XS"], writes=["uh"])
        z1 = carver.get([128, 2, NT], BF16)
        h2 = carver.get([64, LS], BF16)
        SCM = LS // 128
        FCM = (LS + 1 + 127) // 128
        Zfull = carver.get([128, max(SCM * 768, (LP // 128) * (512 + NPS * 256))], BF16)
        FYM = max(FCM, NPS * ((LP + 1 + 127) // 128))
        YRf = carver.get([128, FYM * 256], BF16)
        YIf = carver.get([128, FYM * 256], BF16)
        Hc = carver.get([128, 2, 256])
        Eexp = carver.get([128, 256])
        rsb = carver.get([128, 512])
        sqt = carver.get([128, 512], BF16)
        tlT = carver.get([128, SCM])
        tabs = XS.bitcast(BF16)
        TBN = (3 * LS * 2) // 4
        tabv = [tabs[:, i * TBN:(i + 1) * TBN] for i in range(4)]
        for (nm, L, s0, ns) in (("s", LS, 0, 1), ("p", LP, LS, NPS)):
            SC = L // 128
            FC = (L + 1 + 127) // 128
            ncols = 512 + ns * 256
            Z = Zfull[:, 0:SC * ncols].rearrange("p (s c) -> p s c", s=SC)
            YR = [YRf[:, j * FC * 256:(j + 1) * FC * 256].rearrange("p (f c) -> p f c", f=FC) for j in range(ns)]
            YI = [YIf[:, j * FC * 256:(j + 1) * FC * 256].rearrange("p (f c) -> p f c", f=FC) for j in range(ns)]
            barrier()
            P.dma("sp", embT[:, 0:L], din["c_embT_" + nm][:], writes=["XS"])
            P.dma("sp", tlT[:, 0:SC], din["c_tlinT_" + nm][:], writes=["tlT"])
            tls = [(a, min(512, L - a)) for a in range(0, L, 512)]

            def sin_layer(w_t, kdim, src, srck, dst, dstk, c0):
                for (a, n) in tls:
                    ps, pk = psum()
                    P.op("pe", lambda e: e.matmul(ps[0:64, 0:n], lhsT=w_t, rhs=src[0:kdim, a:a + n], start=True, stop=True), reads=["hyp", srck], writes=[pk])
                    s2, s2k = tmp()
                    s4, s4k = tmp()
                    P.op("act", lambda e: e.activation(out=s2[0:64, 0:n], in_=ps[0:64, 0:n], func=AF.Sin, scale=hsc[:, 0:1], bias=hsc[:, c0:c0 + 1]), reads=[pk, "hsc"], writes=[s2k])
                    P.op("act", lambda e: e.activation(out=s4[0:64, 0:n], in_=ps[0:64, 0:n], func=AF.Sin, scale=hsc[:, 1:2], bias=hsc[:, c0 + 1:c0 + 2]), reads=[pk, "hsc"], writes=[s4k])
                    P.op("dve", lambda e: e.tensor_tensor(out=s4[0:64, 0:n], in0=s4[0:64, 0:n], in1=s4[0:64, 0:n], op=ALU.mult), reads=[s4k], writes=[s4k])
                    P.op("dve", lambda e: e.tensor_scalar(out=s4[0:64, 0:n], in0=s4[0:64, 0:n], scalar1=-2.0, scalar2=1.0, op0=ALU.mult, op1=ALU.add), reads=[s4k], writes=[s4k])
                    P.op("dve", lambda e: e.scalar_tensor_tensor(out=dst[:, a:a + n], in0=s2[0:64, 0:n], scalar=2.0, in1=s4[0:64, 0:n], op0=ALU.mult, op1=ALU.mult),
                         reads=[s2k, s4k], writes=[dstk])
            sin_layer(hw1[:], 33, embT, "XS", h1, "XS", 2)
            sin_layer(hw2[:], 64, h1, "XS", h2, "h2", 4)
            barrier()
            fwdC, fwdS, invC, invS = (din["c_fwdC_" + nm], din["c_fwdS_" + nm], din["c_invC_" + nm], din["c_invS_" + nm])
            for o in range(2):
                psS, psSk = psfix(0)
                for sc in range(SC):
                    ps, pk = psum()
                    P.op("pe", lambda e: e.matmul(ps[:, 0:512], lhsT=h2[:, sc * 128:(sc + 1) * 128], rhs=hw3b[:, o * 512:(o + 1) * 512], start=True, stop=True),
                         reads=["h2", "hyp"], writes=[pk])
                    P.op("act", lambda e: e.activation(out=Eexp, in_=ndecb, func=AF.Exp, scale=tlT[:, sc:sc + 1]), reads=["ndecb", "tlT"], writes=["Eexp"])
                    P.op("dve", lambda e: e.tensor_tensor(out=Z[:, sc, 0:512].rearrange("p (d c) -> p d c", d=2), in0=ps[:, 0:512].rearrange("p (d c) -> p d c", d=2),
                                                          in1=mkap(Eexp[:, 0:1], 0, [[0, 2], [1, 256]]), op=ALU.mult), reads=[pk, "Eexp"], writes=["Zf"])
                    if sc == 0:
                        P.op("dve", lambda e: e.memset(Z[0:1, 0, 256:512], 0.0), reads=["Zf"], writes=["Zf"])
                    P.op("act", lambda e: e.activation(out=sqt, in_=Z[:, sc, 0:512], func=AF.Square), reads=["Zf"], writes=["sqt"])
                    P.op("pe", lambda e: e.matmul(psS[:, 0:512], lhsT=C["ones_b"][:], rhs=sqt, start=(sc == 0), stop=(sc == SC - 1)),
                         reads=["sqt", "k_ones_b"], writes=[psSk], pe_acc=(sc > 0))
                P.op("act", lambda e: e.copy(out=rsb, in_=psS[:, 0:512]), reads=[psSk], writes=["rsb"])
                P.op("dve", lambda e: e.tensor_tensor(out=rsb[:, 0:256], in0=rsb[:, 0:256], in1=rsb[:, 256:512], op=ALU.add), reads=["rsb"], writes=["rsb"])
                P.op("act", lambda e: e.activation(out=rsb[:, 0:256], in_=rsb[:, 0:256], func=AF.Ln, bias=epsc(1e-6)), reads=["rsb", "epsc"], writes=["rsb"])
                P.op("act", lambda e: e.activation(out=rsb[:, 0:256], in_=rsb[:, 0:256], func=AF.Exp, scale=-0.5), reads=["rsb"], writes=["rsb"])
                for sc in range(SC):
                    P.op("dve", lambda e: e.tensor_tensor(out=Z[:, sc, 0:512].rearrange("p (d c) -> p d c", d=2), in0=Z[:, sc, 0:512].rearrange("p (d c) -> p d c", d=2),
                                                          in1=mkap(rsb[:, 0:1], 0, [[0, 2], [1, 256]]), op=ALU.mult), reads=["Zf", "rsb"], writes=["Zf"])
                zsrc_all, zk = (uh[:, 4:6, :], "uh") if o == 0 else (z1, "z1")
                for j in range(ns):
                    for sc in range(SC):
                        for c in range(2):
                            t0 = s0 + j * L + sc * 128
                            ps, pk = psum()
                            pst = ps[:, 0:64].bitcast(BF16)
                            P.op("pe", lambda e: e.transpose(pst, zsrc_all[:, c, t0:t0 + 128], C["ident_b"][:]), reads=[zk, "k_ident_b"], writes=[pk])
                            P.op("act", lambda e: e.copy(out=Z[:, sc, 512 + j * 256 + c * 128:512 + j * 256 + (c + 1) * 128], in_=pst), reads=[pk], writes=["Zd"])
                blocks = [(cb, min(512, ncols - cb)) for cb in range(0, ncols, 512)]
                for fc in range(FC):
                    tC, tS = tabv[(fc % 2) * 2], tabv[(fc % 2) * 2 + 1]
                    tCk, tSk = "tab%d" % ((fc % 2) * 2), "tab%d" % ((fc % 2) * 2 + 1)
                    tCv = tC[:, 0:SC * 128].rearrange("p (s f) -> p s f", s=SC)
                    tSv = tS[:, 0:SC * 128].rearrange("p (s f) -> p s f", s=SC)
                    P.dma("sp", tCv, fwdC[fc], writes=[tCk])
                    P.dma("act", tSv, fwdS[fc], writes=[tSk])
                    for sc in range(SC):
                        for ri, (tv, tk_) in enumerate(((tCv, tCk), (tSv, tSk))):
                            for bi_, (cb, w) in enumerate(blocks):
                                bank, bk = psfix(ri * 2 + bi_)
                                P.op("pe", lambda e: e.matmul(bank[:, 0:w], lhsT=tv[:, sc, :], rhs=Z[:, sc, cb:cb + w], start=(sc == 0), stop=(sc == SC - 1)),
                                     reads=[tk_, "Zf", "Zd"], writes=[bk], pe_acc=(sc > 0))
                    for ri in range(2):
                        for bi_, (cb, w) in enumerate(blocks):
                            bank, bk = psfix(ri * 2 + bi_)
                            P.op("act", lambda e: e.copy(out=EV[:, ri, cb:cb + w], in_=bank[:, 0:w]), reads=[bk], writes=["EV"])
                    P.op("dve", lambda e: e.tensor_tensor(out=Hc[:, 0, :], in0=EV[:, 0, 0:256], in1=EV[:, 0, 256:512], op=ALU.add), reads=["EV"], writes=["Hc"])
                    P.op("dve", lambda e: e.tensor_tensor(out=Hc[:, 1, :], in0=EV[:, 1, 0:256], in1=EV[:, 1, 256:512], op=ALU.subtract), reads=["EV"], writes=["Hc"])
                    for j in range(ns):
                        vre = EV[:, 0, 512 + j * 256:512 + (j + 1) * 256]
                        vim = EV[:, 1, 512 + j * 256:512 + (j + 1) * 256]
                        t1, t1k = tmp()
                        t2, t2k = tmp()
                        P.op("dve", lambda e: e.tensor_tensor(out=t1[:, 0:256], in0=vre, in1=Hc[:, 0, :], op=ALU.mult), reads=["EV", "Hc"], writes=[t1k])
                        P.op("dve", lambda e: e.tensor_tensor(out=t2[:, 0:256], in0=vim, in1=Hc[:, 1, :], op=ALU.mult), reads=["EV", "Hc"], writes=[t2k])
                        P.op("dve", lambda e: e.tensor_tensor(out=YR[j][:, fc, :], in0=t1[:, 0:256], in1=t2[:, 0:256], op=ALU.subtract), reads=[t1k, t2k], writes=["YR"])
                        P.op("dve", lambda e: e.tensor_tensor(out=t1[:, 256:512], in0=vre, in1=Hc[:, 1, :], op=ALU.mult), reads=["EV", "Hc"], writes=[t1k])
                        P.op("dve", lambda e: e.tensor_tensor(out=t2[:, 256:512], in0=vim, in1=Hc[:, 0, :], op=ALU.mult), reads=["EV", "Hc"], writes=[t2k])
                        P.op("dve", lambda e: e.tensor_tensor(out=YI[j][:, fc, :], in0=t1[:, 256:512], in1=t2[:, 256:512], op=ALU.add), reads=[t1k, t2k], writes=["YI"])
                ttl = [(a, min(512, L - a)) for a in range(0, L, 512)]
                accs = {}
                bi2 = 0
                for j in range(ns):
                    for c in range(2):
                        for ti in range(len(ttl)):
                            accs[(j, c, ti)] = (PS[bi2], "ps%d" % bi2)
                            bi2 += 1
                assert bi2 <= 8
                for fc in range(FC):
                    tC, tS = tabv[(fc % 2) * 2], tabv[(fc % 2) * 2 + 1]
                    tCk, tSk = "tab%d" % ((fc % 2) * 2), "tab%d" % ((fc % 2) * 2 + 1)
                    P.dma("sp", tC[:, 0:L], invC[fc], writes=[tCk])
                    P.dma("act", tS[:, 0:L], invS[fc], writes=[tSk])
                    for j in range(ns):
                        for c in range(2):
                            for ti, (a, w) in enumerate(ttl):
                                acc, acck = accs[(j, c, ti)]
                                P.op("pe", lambda e: e.matmul(acc[:, 0:w], lhsT=YR[j][:, fc, c * 128:(c + 1) * 128], rhs=tC[:, a:a + w], start=(fc == 0), stop=False),
                                     reads=["YR", tCk], writes=[acck], pe_acc=(fc > 0))
                                P.op("pe", lambda e: e.matmul(acc[:, 0:w], lhsT=YI[j][:, fc, c * 128:(c + 1) * 128], rhs=tS[:, a:a + w], start=False, stop=(fc == FC - 1)),
                                     reads=["YI", tSk], writes=[acck], pe_acc=True)
                for j in range(ns):
                    for c in range(2):
                        for ti, (a, w) in enumerate(ttl):
                            acc, acck = accs[(j, c, ti)]
                            g0 = s0 + j * L + a
                            zs = zsrc_all[:, c, g0:g0 + w]
                            xg = uh[:, (0 if o == 0 else 2) + c, g0:g0 + w]
                            tt, tk = tmp()
                            P.op("dve", lambda e: e.scalar_tensor_tensor(out=tt[:, 0:w], in0=zs, scalar=hbias[:, o, c:c + 1], in1=acc[:, 0:w], op0=ALU.mult, op1=ALU.add),
                                 reads=[zk, "hyp", acck], writes=[tk])
                            dstz, dk = (z1[:, c, g0:g0 + w], "z1") if o == 0 else (mixT[:, 6 + c, g0:g0 + w], "mixT")
                            P.op("dve", lambda e: e.tensor_tensor(out=dstz, in0=tt[:, 0:w], in1=xg, op=ALU.mult), reads=[tk, "uh"], writes=[dk])
                barrier()
        dbg_out("yc%d" % l, mixT[:, 6:8, :], ["mixT"])

    for l in range(DEPTH):
        lam0 = lam_init_of(l)
        P.dma("sp", bmod[:], b_modT[l], writes=["bmod"])
        P.dma("sp", gn[:], gains[l], writes=["gains"])
        for j in range(48):
            wm, wk = load_w(w_mod[l][:, j * 128:(j + 1) * 128], KC)
            ps, pk = psum()
            for c in range(KC):
                P.op("pe", lambda e, wm=wm, c=c, ps=ps: e.matmul(ps[:, 0:2], lhsT=wm[:, c, :], rhs=scv[:, c, :],
                                                                start=(c == 0), stop=(c == KC - 1)),
                     reads=[wk, "scvec"], writes=[pk], pe_acc=(c > 0))
            P.op("dve", lambda e, ps=ps, j=j: e.tensor_scalar(out=modT[:, j, :], in0=ps[:, 0:2], scalar1=bmod[:, j:j + 1],
                                                              scalar2=None, op0=ALU.add),
                 reads=[pk, "bmod"], writes=["modT"])
        for (o, jsc, gi) in ((0, 8, 0), (2, 32, 2)):
            for cnd in range(2):
                P.op("dve", lambda e, o=o, jsc=jsc, gi=gi, cnd=cnd: e.scalar_tensor_tensor(
                    out=sc_all[:, o, :, cnd], in0=modT[:, jsc:jsc + 8, cnd], scalar=1.0, in1=gn[:, gi, :],
                    op0=ALU.add, op1=ALU.mult), reads=["modT", "gains"], writes=["sc_all"])
        for (o, jgt, gi) in ((1, 16, 1), (3, 40, 3)):
            for cnd in range(2):
                P.op("dve", lambda e, o=o, jgt=jgt, gi=gi, cnd=cnd: e.tensor_tensor(
                    out=sc_all[:, o, :, cnd], in0=modT[:, jgt:jgt + 8, cnd], in1=gn[:, gi, :], op=ALU.mult),
                    reads=["modT", "gains"], writes=["sc_all"])

        def norm_mod(src_tile_fn, src_keys, dst, dst_key_fn, sci, shj):
            for (t0, n, cnd) in cfg.tiles:
                rms_stats(lambda c: src_tile_fn(c, t0, n), src_keys, t0, n, rstd, "rstd")
                for c in range(KC):
                    tt, tk = tmp()
                    P.op("dve", lambda e, c=c, tt=tt: e.tensor_tensor(out=tt[:, 0:n], in0=src_tile_fn(c, t0, n), in1=rstd[:, 0:n],
                                                                     op=ALU.mult), reads=src_keys + ["rstd"], writes=[tk])
                    P.op("act", lambda e, c=c, tt=tt: e.activation(out=dst[:, c, t0:t0 + n], in_=tt[:, 0:n], func=AF.Identity,
                                                                  scale=sc_all[:, sci, c, cnd:cnd + 1],
                                                                  bias=modT[:, shj + c, cnd:cnd + 1]),
                         reads=[tk, "sc_all", "modT"], writes=[dst_key_fn(c, t0)])

        norm_mod(lambda c, t0, n: xT[:, c, t0:t0 + n], ["xT"], hT, lambda c, t0: "hT", 0, 0)
        if l == 0:
            dbg_out("h0", hT, ["hT"])
        P.dma("sp", x_spill[:], xT, reads=["xT"], writes=["x_spill"])
        barrier()

        def proj_fm(wsrc, col0, ncols_chunks, rhs, rhs_key, cb, kchunks=KC):
            for ci in range(ncols_chunks):
                wt, wk = load_w(wsrc[:, col0 + ci * 128: col0 + (ci + 1) * 128], kchunks)
                for (t0, n, cnd) in cfg.tiles:
                    ps, pk = psum()
                    for c in range(kchunks):
                        P.op("pe", lambda e, wt=wt, c=c, ps=ps, t0=t0, n=n: e.matmul(
                            ps[:, 0:n], lhsT=wt[:, c, :], rhs=rhs[:, c, t0:t0 + n], start=(c == 0), stop=(c == kchunks - 1)),
                            reads=[wk, rhs_key], writes=[pk], pe_acc=(c > 0))
                    cb(ps, pk, ci, t0, n, cnd)

        carver.reset(0, HB0)
        rkv = lor = None
        if "rwkv" in mixers:
            rkv = carver.get([128, 6, NT], BF16)
            lor = carver.get([128, 3, NT], BF16)

        def cbA(ps, pk, ci, t0, n, cnd):
            if ci < 6:
                P.op("act", lambda e: e.copy(out=rkv[:, ci, t0:t0 + n], in_=ps[:, 0:n]), reads=[pk], writes=["rkv"])
            else:
                fn = {6: AF.Tanh, 7: AF.Identity, 8: AF.Sigmoid}[ci]
                P.op("act", lambda e: e.activation(out=lor[:, ci - 6, t0:t0 + n], in_=ps[:, 0:n], func=fn),
                     reads=[pk], writes=["lor"])
        if "rwkv" in mixers:
            proj_fm(w_in[l], 0, 9, hT, "hT", cbA)
            rwkv_layer(l, rkv, lor)
        else:
            P.op("pool", lambda e: e.memset(mixT[:, 0:2, :], 0.0), writes=["mixT"])

        if "attn" in mixers:
            attn_layer(l, lam0)
        else:
            P.op("pool", lambda e: e.memset(mixT[:, 2:6, :], 0.0), writes=["mixT"])

        if "hyena" in mixers:
            hyena_layer(l)
        else:
            P.op("pool", lambda e: e.memset(mixT[:, 6:8, :], 0.0), writes=["mixT"])

        barrier()
        P.dma("sp", xT, x_spill[:], reads=["x_spill"], writes=["xT"])
        carver.reset(XB, MB0)
        mo = carver.get([128, KC, 512])
        for (t0, n, cnd) in cfg.tiles:
            for ci in range(KC):
                wt, wk = load_w(w_out[l][:, ci * 128:(ci + 1) * 128], KC)
                ps, pk = psum()
                for c in range(KC):
                    P.op("pe", lambda e, wt=wt, c=c, ps=ps: e.matmul(ps[:, 0:n], lhsT=wt[:, c, :], rhs=mixT[:, c, t0:t0 + n],
                                                                    start=(c == 0), stop=(c == KC - 1)),
                         reads=[wk, "mixT"], writes=[pk], pe_acc=(c > 0))
                P.op("act", lambda e, ci=ci, ps=ps: e.copy(out=mo[:, ci, 0:n], in_=ps[:, 0:n]), reads=[pk], writes=["mo"])
            rms_stats(lambda c: mo[:, c, 0:n], ["mo"], t0, n, rstd, "rstd")
            for c in range(KC):
                tt, tk = tmp()
                P.op("dve", lambda e, c=c, tt=tt: e.tensor_tensor(out=tt[:, 0:n], in0=mo[:, c, 0:n], in1=rstd[:, 0:n], op=ALU.mult),
                     reads=["mo", "rstd"], writes=[tk])
                P.op("dve", lambda e, c=c, tt=tt: e.scalar_tensor_tensor(
                    out=xT[:, c, t0:t0 + n], in0=tt[:, 0:n], scalar=sc_all[:, 1, c, cnd:cnd + 1], in1=xT[:, c, t0:t0 + n],
                    op0=ALU.mult, op1=ALU.add), reads=[tk, "sc_all", "xT"], writes=["xT"])
        if l == 0:
            dbg_out("x1", xT, ["xT"])

        barrier()
        carver.reset(XB, BIGB)
        h2 = carver.get([128, KC, 512], BF16)
        f1 = carver.get([128, 32, 512], BF16)
        fo = carver.get([128, KC, 512])
        wff2 = [carver.get([128, 32, 128], BF16) for _ in range(2)]
        for (t0, n, cnd) in cfg.tiles:
            rms_stats(lambda c: xT[:, c, t0:t0 + n], ["xT"], t0, n, rstd, "rstd")
            for c in range(KC):
                tt, tk = tmp()
                P.op("dve", lambda e, c=c, tt=tt: e.tensor_tensor(out=tt[:, 0:n], in0=xT[:, c, t0:t0 + n], in1=rstd[:, 0:n],
                                                                 op=ALU.mult), reads=["xT", "rstd"], writes=[tk])
                P.op("act", lambda e, c=c, tt=tt: e.activation(out=h2[:, c, 0:n], in_=tt[:, 0:n], func=AF.Identity,
                                                              scale=sc_all[:, 2, c, cnd:cnd + 1], bias=modT[:, 24 + c, cnd:cnd + 1]),
                     reads=[tk, "sc_all", "modT"], writes=["h2"])
            for ci in range(32):
                wt, wk = load_w(w_ff1[l][:, ci * 128:(ci + 1) * 128], KC)
                ps, pk = psum()
                for c in range(KC):
                    P.op("pe", lambda e, wt=wt, c=c, ps=ps: e.matmul(ps[:, 0:n], lhsT=wt[:, c, :], rhs=h2[:, c, 0:n],
                                                                    start=(c == 0), stop=(c == KC - 1)),
                         reads=[wk, "h2"], writes=[pk], pe_acc=(c > 0))
                tt, tk = tmp()
                P.op("act", lambda e, ps=ps, tt=tt: e.activation(out=tt[:, 0:n], in_=ps[:, 0:n], func=AF.Relu), reads=[pk], writes=[tk])
                P.op("dve", lambda e, ci=ci, tt=tt: e.tensor_tensor(out=f1[:, ci, 0:n], in0=tt[:, 0:n], in1=tt[:, 0:n], op=ALU.mult),
                     reads=[tk], writes=["f1"])
            for ci in range(KC):
                wt, wk = wff2[ci % 2], "wff2_%d" % (ci % 2)
                load_w(w_ff2[l][:, ci * 128:(ci + 1) * 128], 32, dst=wt, key=wk)
                ps, pk = psum()
                for c in range(32):
                    P.op("pe", lambda e, wt=wt, c=c, ps=ps: e.matmul(ps[:, 0:n], lhsT=wt[:, c, :], rhs=f1[:, c, 0:n],
                                                                    start=(c == 0), stop=(c == 31)),
                         reads=[wk, "f1"], writes=[pk], pe_acc=(c > 0))
                P.op("act", lambda e, ci=ci, ps=ps: e.copy(out=fo[:, ci, 0:n], in_=ps[:, 0:n]), reads=[pk], writes=["fo"])
            rms_stats(lambda c: fo[:, c, 0:n], ["fo"], t0, n, rstd, "rstd")
            for c in range(KC):
                tt, tk = tmp()
                P.op("dve", lambda e, c=c, tt=tt: e.tensor_tensor(out=tt[:, 0:n], in0=fo[:, c, 0:n], in1=rstd[:, 0:n], op=ALU.mult),
                     reads=["fo", "rstd"], writes=[tk])
                P.op("dve", lambda e, c=c, tt=tt: e.scalar_tensor_tensor(
                    out=xT[:, c, t0:t0 + n], in0=tt[:, 0:n], scalar=sc_all[:, 3, c, cnd:cnd + 1], in1=xT[:, c, t0:t0 + n],
                    op0=ALU.mult, op1=ALU.add), reads=[tk, "sc_all", "xT"], writes=["xT"])

    out_toks.append(P.dma("sp", yT_out[:], xT, reads=["xT"], writes=["yT_out"]))
    P.finish_wait("sp", out_toks)
    P.emit(st)
    st.close()
    return nc


def _lay(a):
    return np.ascontiguousarray(a)


def prep_inputs(cfg, inp, consts):
    D = cfg.depth
    f = np.float32
    shared = {}
    for k, v in consts.items():
        shared["c_" + k] = v
    shared["w_mod"] = _lay(inp["w_mod"][:D])
    shared["b_modT"] = _lay(inp["b_mod"][:D].reshape(D, 48, 128).transpose(0, 2, 1))
    g4 = np.stack([inp["g_mix_pre"][:D], inp["g_mix_post"][:D], inp["g_ffn_pre"][:D], inp["g_ffn_post"][:D]], axis=1)
    shared["gains"] = _lay(g4.reshape(D, 4, 8, 128).transpose(0, 3, 1, 2))
    for k in ("w_in", "w_out", "w_ff1", "w_ff2"):
        shared[k] = _lay(inp[k][:D])
    shared["rw_conv"] = _lay(inp["rwkv_conv"][:D].reshape(D, 3, 6, 128).transpose(0, 3, 2, 1))
    shared["rw_w0"] = _lay(inp["rwkv_w0"][:D].reshape(D, 2, 2, 128).transpose(0, 3, 1, 2))
    shared["rw_a0"] = _lay(inp["rwkv_a0"][:D].reshape(D, 2, 2, 128).transpose(0, 3, 1, 2))
    shared["rw_w2"] = _lay(inp["rwkv_w2"][:D].reshape(D, 128, 256))
    shared["rw_a2"] = _lay(inp["rwkv_a2"][:D].reshape(D, 128, 256))
    shared["rw_g2"] = _lay(inp["rwkv_g2"][:D])
    v3 = np.stack([inp["rwkv_kk"][:D], inp["rwkv_ka"][:D], inp["rwkv_rk"][:D].reshape(D, 256)], axis=1)
    shared["rw_vec"] = _lay(v3.reshape(D, 3, 2, 128).transpose(0, 3, 1, 2))
    ln = np.stack([inp["rwkv_ln_w"][:D], inp["rwkv_ln_b"][:D]], axis=1)
    shared["rw_ln"] = _lay(ln.reshape(D, 2, 4, 64).transpose(0, 3, 1, 2))
    lam = np.stack([inp["diff_lq1"][:D], inp["diff_lk1"][:D], inp["diff_lq2"][:D], inp["diff_lk2"][:D]], axis=1)
    shared["dlam"] = _lay(np.broadcast_to(lam[:, None], (D, 128, 4, 64)))
    shared["dsub"] = _lay(inp["diff_subln"][:D].reshape(D, 128, 1))
    hc = np.concatenate([inp["hy_conv_w"][:D], inp["hy_conv_b"][:D][:, None]], axis=1)
    shared["hy_conv"] = _lay(hc.reshape(D, 4, 6, 128).transpose(0, 3, 2, 1))
    shared["hy_w1"] = _lay(inp["hy_w1"][:D])
    shared["hy_vec"] = _lay(np.stack([inp["hy_b1"][:D], inp["hy_freq"][:D], inp["hy_b2"][:D]], axis=2))
    shared["hy_w2"] = _lay(inp["hy_w2"][:D])
    shared["hy_w3"] = _lay(inp["hy_w3"][:D])
    shared["hy_decb"] = _lay(np.broadcast_to(inp["hy_decay"][:D][:, None, :], (D, 128, 256)))
    shared["hy_bias"] = _lay(inp["hy_bias"][:D].reshape(D, 2, 2, 128).transpose(0, 3, 1, 2))
    maps = []
    nb_s = inp["x_sample"].shape[0]
    for core in range(8):
        b = core % nb_s
        m = dict(shared)
        xs = inp["x_sample"][b]
        xp = inp["x_prompt"][cfg.nps * core:cfg.nps * (core + 1)].reshape(-1, 1024)
        x = np.concatenate([xs, xp], axis=0)
        m["xT"] = _lay(x.T.reshape(8, 128, cfg.nt).transpose(1, 0, 2))
        cv2 = np.stack([inp["c"][b], inp["c_ctx"]], axis=1)
        m["cvecT"] = _lay(cv2.reshape(8, 128, 2).transpose(1, 0, 2))
        st0 = inp["state_rwkv"][b][:D]
        m["st_in"] = _lay(st0.reshape(D, 2, 2, 2, 64, 64).transpose(0, 3, 5, 2, 1, 4).reshape(D, 128, 4, 64))
        ck = inp["cache_k"][b][:D]
        m["ckT"] = _lay(ck.reshape(D, cfg.past, 4, 128).transpose(0, 2, 3, 1))
        cvv = inp["cache_v"][b][:D]
        m["cv"] = _lay(cvv.reshape(D, cfg.past // 128, 128, 4, 128).transpose(0, 2, 1, 3, 4))
        maps.append({k: np.ascontiguousarray(v) for k, v in m.items()})
    return maps


_CACHE = {}


def kernel(**inputs):
    inp = {k: np.asarray(v) for k, v in inputs.items()}
    cfg = Cfg()
    consts = host_consts(cfg)
    if "nc" not in _CACHE:
        _CACHE["nc"] = build_program(cfg, consts, mixers=("rwkv", "attn", "hyena"))
    nc = _CACHE["nc"]
    maps = prep_inputs(cfg, inp, consts)
    res = run_bass_kernel_spmd(nc, maps, core_ids=list(range(8)))
    R = res.results
    D = cfg.depth
    B = inp["x_prompt"].shape[0]
    y_prompt = np.zeros((B, cfg.lp, 1024), np.float32)
    y_sample = np.zeros((inp["x_sample"].shape[0], cfg.ls, 1024), np.float32)
    new_state = np.zeros((B, D, 2, 4, 64, 64), np.float32)
    new_k = np.zeros((B, D, cfg.lp, 4, 2, 64), np.float32)
    new_v = np.zeros((B, D, cfg.lp, 4, 128), np.float32)
    for core in range(8):
        yT = np.asarray(R[core]["yT"])
        y = yT.transpose(1, 0, 2).reshape(1024, cfg.nt).T
        if core < y_sample.shape[0]:
            y_sample[core] = y[:cfg.ls]
        for i in range(cfg.nps):
            bp = cfg.nps * core + i
            y_prompt[bp] = y[cfg.ls + i * cfg.lp: cfg.ls + (i + 1) * cfg.lp]
            new_state[bp] = np.asarray(R[core]["st_out"])[:, i]
            new_k[bp] = np.asarray(R[core]["kc_out"])[:, i * cfg.lp:(i + 1) * cfg.lp].reshape(D, cfg.lp, 4, 2, 64)
            new_v[bp] = np.asarray(R[core]["vc_out"])[:, i * cfg.lp:(i + 1) * cfg.lp].reshape(D, cfg.lp, 4, 128)
    return (y_prompt, y_sample, new_state, new_k, new_v)
```
